# Optimizing a Trainium2 kernel written in Bass

```python
import math
import jax, jax.numpy as jnp
from jax import lax
import numpy as np

D_MODEL = 2048
BATCH = 4
SEQ = 4096
DEPTH = 1

CHUNK = 64
EPS = 1e-6
A_HEAD_DIM = 64
A_WIDTH = D_MODEL // 2
A_HEADS = A_WIDTH // A_HEAD_DIM
A_LEFT_CHUNKS = 8
A_BAND = (A_LEFT_CHUNKS + 1) * CHUNK
MAX_REL = 128
N_REL = 2 * MAX_REL + 1
B_WIDTH = D_MODEL - A_WIDTH
B_GROUPS = 8
B_GROUP_DIM = B_WIDTH // B_GROUPS
B_BLOCK = 128
IN_WIDTH = 3 * A_WIDTH + 2 * B_WIDTH
D_FF = 5504
N_MOD = 9

kernel_name = "hybrid_chunk_attn_gmlp_macaron_block"


def rms_norm(x, g):
    xf = x.astype(jnp.float32)
    y = xf * lax.rsqrt(jnp.mean(xf * xf, axis=-1, keepdims=True) + EPS)
    return (y * g.astype(jnp.float32)).astype(x.dtype)


def layer_norm(x, g, b):
    xf = x.astype(jnp.float32)
    mu = jnp.mean(xf, axis=-1, keepdims=True)
    xc = xf - mu
    y = xc * lax.rsqrt(jnp.mean(xc * xc, axis=-1, keepdims=True) + EPS)
    return (y * g.astype(jnp.float32) + b.astype(jnp.float32)).astype(x.dtype)


def modulate(h, shift, scale):
    return h * (1.0 + scale) + shift


def swiglu(h, w_gu, w_down):
    gu = h @ w_gu
    g, u = jnp.split(gu, 2, axis=-1)
    return (jax.nn.silu(g) * u) @ w_down


def chunk_band_attention(q, k, v, rel_bias):
    bsz, seq, nh, dh = q.shape
    nc = seq // CHUNK
    qc = q.reshape(bsz, nc, CHUNK, nh, dh)
    pad = ((0, 0), (A_LEFT_CHUNKS, 0), (0, 0), (0, 0), (0, 0))
    kp = jnp.pad(k.reshape(bsz, nc, CHUNK, nh, dh), pad)
    vp = jnp.pad(v.reshape(bsz, nc, CHUNK, nh, dh), pad)
    kb = jnp.concatenate([kp[:, w:w + nc] for w in range(A_LEFT_CHUNKS + 1)], axis=2)
    vb = jnp.concatenate([vp[:, w:w + nc] for w in range(A_LEFT_CHUNKS + 1)], axis=2)
    scores = jnp.einsum('bcqhd,bckhd->bhcqk', qc, kb).astype(jnp.float32) / math.sqrt(dh)
    qi = np.arange(CHUNK)[:, None]
    kj = np.arange(A_BAND)[None, :]
    dist = qi - (kj - A_LEFT_CHUNKS * CHUNK)
    rel_idx = np.clip(dist, -MAX_REL, MAX_REL) + MAX_REL
    bias = rel_bias.astype(jnp.float32)[:, rel_idx]
    key_chunk = np.arange(nc)[:, None] - A_LEFT_CHUNKS + (np.arange(A_BAND) // CHUNK)[None, :]
    valid = key_chunk >= 0
    scores = scores + bias[:, None, :, :]
    scores = jnp.where(valid[None, None, :, None, :], scores, jnp.float32(-1e30))
    probs = jax.nn.softmax(scores, axis=-1).astype(v.dtype)
    out = jnp.einsum('bhcqk,bckhd->bcqhd', probs, vb)
    return out.reshape(bsz, seq, nh * dh)


def chunk_spatial_gating(u, v, ln_g, ln_b, w_s, b_s):
    bsz, seq, _ = u.shape
    nb = seq // B_BLOCK
    u = jax.nn.gelu(u, approximate=False)
    v = layer_norm(jax.nn.gelu(v, approximate=False), ln_g, ln_b)
    vb = v.reshape(bsz, nb, B_BLOCK, B_GROUPS, B_GROUP_DIM)
    t = np.arange(B_BLOCK)
    mask = (t[:, None] // CHUNK) >= (t[None, :] // CHUNK)
    w = jnp.where(mask[None], w_s, jnp.zeros_like(w_s))
    z = jnp.einsum('gts,bnsgc->bntgc', w, vb) + jnp.transpose(b_s)[None, None, :, :, None]
    return u * z.reshape(bsz, seq, B_WIDTH)


def setup_inputs(seed: int = 0) -> dict:
    key = jax.random.key(seed)
    ks = jax.random.split(key, 24)
    f32 = jnp.float32

    def nrm(k, shape, scale):
        return jax.random.normal(k, shape, f32) * scale

    def gain(k, shape):
        return 1.0 + 0.1 * jax.random.normal(k, shape, f32)

    L, D = DEPTH, D_MODEL
    return {
        "x": nrm(ks[0], (BATCH, SEQ, D), 1.0),
        "c": nrm(ks[1], (BATCH, D), 1.0),
        "w_ada": nrm(ks[2], (L, D, N_MOD * D), 0.5 * D ** -0.5),
        "b_ada": nrm(ks[3], (L, N_MOD * D), 0.02),
        "ffn1_pre_g": gain(ks[4], (L, D)),
        "ffn1_post_g": gain(ks[5], (L, D)),
        "ffn1_w_gu": nrm(ks[6], (L, D, 2 * D_FF), D ** -0.5),
        "ffn1_w_down": nrm(ks[7], (L, D_FF, D), D_FF ** -0.5),
        "mix_pre_g": gain(ks[8], (L, D)),
        "mix_post_g": gain(ks[9], (L, D)),
        "w_in": nrm(ks[10], (L, D, IN_WIDTH), D ** -0.5),
        "rel_bias": nrm(ks[11], (L, A_HEADS, N_REL), 0.5),
        "ln_v_g": gain(ks[12], (L, B_WIDTH)),
        "ln_v_b": nrm(ks[13], (L, B_WIDTH), 0.02),
        "w_s": nrm(ks[14], (L, B_GROUPS, B_BLOCK, B_BLOCK), B_BLOCK ** -0.5),
        "b_s": gain(ks[15], (L, B_GROUPS, B_BLOCK)),
        "g_out_a": gain(ks[16], (L, A_WIDTH)),
        "g_out_b": gain(ks[17], (L, B_WIDTH)),
        "w_out": nrm(ks[18], (L, D, D), D ** -0.5),
        "ffn2_pre_g": gain(ks[19], (L, D)),
        "ffn2_post_g": gain(ks[20], (L, D)),
        "ffn2_w_gu": nrm(ks[21], (L, D, 2 * D_FF), D ** -0.5),
        "ffn2_w_down": nrm(ks[22], (L, D_FF, D), D_FF ** -0.5),
    }


def reference(x, c, w_ada, b_ada, ffn1_pre_g, ffn1_post_g, ffn1_w_gu, ffn1_w_down,
              mix_pre_g, mix_post_g, w_in, rel_bias, ln_v_g, ln_v_b, w_s, b_s,
              g_out_a, g_out_b, w_out, ffn2_pre_g, ffn2_post_g, ffn2_w_gu, ffn2_w_down):
    bsz, seq, _ = x.shape
    c_act = jax.nn.silu(c)
    for l in range(DEPTH):
        mod = (c_act @ w_ada[l] + b_ada[l]).reshape(bsz, N_MOD, D_MODEL)[:, :, None, :]
        sh1, sc1, gt1 = mod[:, 0], mod[:, 1], mod[:, 2]
        sh2, sc2, gt2 = mod[:, 3], mod[:, 4], mod[:, 5]
        sh3, sc3, gt3 = mod[:, 6], mod[:, 7], mod[:, 8]

        h = modulate(rms_norm(x, ffn1_pre_g[l]), sh1, sc1)
        y = swiglu(h, ffn1_w_gu[l], ffn1_w_down[l])
        x = x + 0.5 * gt1 * rms_norm(y, ffn1_post_g[l])

        h = modulate(rms_norm(x, mix_pre_g[l]), sh2, sc2)
        proj = h @ w_in[l]
        q, k, v, u_b, v_b = jnp.split(proj, [A_WIDTH, 2 * A_WIDTH, 3 * A_WIDTH,
                                             3 * A_WIDTH + B_WIDTH], axis=-1)
        hd = (bsz, seq, A_HEADS, A_HEAD_DIM)
        out_a = chunk_band_attention(q.reshape(hd), k.reshape(hd), v.reshape(hd), rel_bias[l])
        out_b = chunk_spatial_gating(u_b, v_b, ln_v_g[l], ln_v_b[l], w_s[l], b_s[l])
        merged = jnp.concatenate([rms_norm(out_a, g_out_a[l]), rms_norm(out_b, g_out_b[l])], axis=-1)
        y = merged @ w_out[l]
        x = x + gt2 * rms_norm(y, mix_post_g[l])

        h = modulate(rms_norm(x, ffn2_pre_g[l]), sh3, sc3)
        y = swiglu(h, ffn2_w_gu[l], ffn2_w_down[l])
        x = x + 0.5 * gt3 * rms_norm(y, ffn2_post_g[l])
    return x
```

```python
import numpy as np
import concourse.bass as bass
import concourse.mybir as mybir
from concourse.bass_utils import run_bass_kernel_spmd
from contextlib import ExitStack

F32 = mybir.dt.float32
BF16 = mybir.dt.bfloat16
AF = mybir.ActivationFunctionType
ALU = mybir.AluOpType
AX = mybir.AxisListType

D = 2048
DFF = 5504
NJ = 43
HALO = 512
NMAIN = 2048
NEXT = NMAIN + HALO
EPS = 1e-6
NH = 16
VW = 65


import os
SKIP_SELF = os.environ.get("SKIP_SELF", "act,pe").split(",")


class Eng:
    def __init__(self, e, sem, name):
        self.e, self.sem, self.name = e, sem, name
        self.n = 0
        self.seen = {}
        self.last = None

    def wait(self, toks):
        best = {}
        for t in toks:
            if t is None:
                continue
            s, v = t
            k = id(s)
            if s is self.sem and self.name in SKIP_SELF:
                continue
            if k not in best or best[k][1] < v:
                best[k] = (s, v)
        for k, (s, v) in best.items():
            if self.seen.get(k, 0) < v:
                self.e.wait_ge(s, v)
                self.seen[k] = v

    def mark(self, ins):
        ins.then_inc(self.sem, 1)
        self.n += 1
        self.last = (self.sem, self.n)
        return self.last


class Buf:
    def __init__(self, name=""):
        self.name = name
        self.w = {}
        self.r = {}
        self.pr = {}

    @staticmethod
    def _add(d, tok):
        if tok is None:
            return
        k = id(tok[0])
        if k not in d or d[k][1] < tok[1]:
            d[k] = tok

    def rdeps(self):
        return list(self.w.values())

    def wdeps(self):
        return list(self.w.values()) + list(self.r.values()) + list(self.pr.values())

    def pwdeps(self):
        if self.r:
            npr = {}
            for d in (self.w, self.r, self.pr):
                for t in d.values():
                    self._add(npr, t)
            self.pr, self.w, self.r = npr, {}, {}
        return list(self.pr.values())

    def add_r(self, tok):
        self._add(self.r, tok)

    def set_w(self, tok):
        self.w = {}
        self._add(self.w, tok)
        self.r = {}
        self.pr = {}
        self._add(self.pr, tok)

    def add_w(self, tok):
        self._add(self.w, tok)


class _Stop(Exception):
    pass


class Prog:
    def __init__(self):
        self.nc = bass.Bass("TRN2", target_bir_lowering=False)
        self.st = ExitStack()
        self.nsem = 0
        self.pending_stores = []

    def sem(self, name):
        self.nsem += 1
        return self.st.enter_context(self.nc.semaphore(f"{name}_{self.nsem}"))

    def setup_engines(self):
        nc = self.nc
        self.PE = Eng(nc.tensor, self.sem("pe"), "pe")
        self.ACT = Eng(nc.scalar, self.sem("act"), "act")
        self.DVE = Eng(nc.vector, self.sem("dve"), "dve")
        self.POOL = Eng(nc.gpsimd, self.sem("pool"), "pool")
        self.SP = Eng(nc.sync, self.sem("sp"), "sp")

    def _deps(self, reads, writes, pwrites):
        deps = []
        for b in reads:
            deps += b.rdeps()
        for b in writes:
            deps += b.wdeps()
        for b in pwrites:
            deps += b.pwdeps()
        return deps

    def _upd(self, tok, reads, writes, pwrites):
        for b in reads:
            b.add_r(tok)
        for b in writes:
            b.set_w(tok)
        for b in pwrites:
            b.add_w(tok)

    def op(self, E, f, reads=(), writes=(), pwrites=()):
        E.wait(self._deps(reads, writes, pwrites))
        tok = E.mark(f())
        self._upd(tok, reads, writes, pwrites)
        return tok

    def pe_group(self, fs, reads=(), writes=(), pwrites=()):
        self.PE.wait(self._deps(reads, writes, pwrites))
        ins = None
        for f in fs:
            ins = f()
        tok = self.PE.mark(ins)
        self._upd(tok, reads, writes, pwrites)
        return tok

    def dma(self, Q, dsem, out, in_, reads=(), writes=(), pwrites=()):
        Q.wait(self._deps(reads, writes, pwrites))
        Q.e.dma_start(out=out, in_=in_).then_inc(dsem[0], 16)
        dsem[1] += 1
        tok = (dsem[0], 16 * dsem[1])
        self._upd(tok, reads, writes, pwrites)
        return tok

    def dsem(self, name):
        return [self.sem(name), 0]

    def barrier(self, bufs=()):
        toks = [E.last for E in (self.PE, self.ACT, self.DVE)]
        for b in bufs:
            toks += b.wdeps()
        toks += self.pending_stores
        for E in (self.PE, self.ACT, self.DVE, self.SP):
            E.wait(toks)


def build_program(upto="all", dbg=False, stop=0, skip_pro=False):
    P = Prog()
    lvl = {"pro": 0, "ffn1": 1, "mix": 2, "all": 3}[upto]
    nc = P.nc
    st = P.st
    en = st.enter_context

    def inp(name, shape):
        return nc.dram_tensor(name, shape, F32, kind="ExternalInput").ap()

    x_d = inp("x", [NEXT, D])
    cT_d = inp("cT", [128, 16])
    hv_d = inp("hv", [128, 1])
    wada_d = inp("w_ada", [D, 9 * D]) if not skip_pro else None
    bada_d = inp("b_ada", [1, 9 * D])
    wgu_d = [inp("w_gu1", [D, 2 * DFF]) if (lvl >= 1 and not os.environ.get("SKIP_F1")) else None, inp("w_gu2", [D, 2 * DFF]) if lvl >= 3 else None]
    wdn_d = [inp("w_dn1", [DFF, D]) if (lvl >= 1 and not os.environ.get("SKIP_F1")) else None, inp("w_dn2", [DFF, D]) if lvl >= 3 else None]
    win_d = inp("w_in", [D, 5120]) if lvl >= 2 else None
    wout_d = inp("w_out", [D, D]) if lvl >= 2 else None
    gpre_d = inp("gpre", [128, 3 * 16])
    gpost_d = inp("gpost", [3, D])
    lnv_d = inp("lnv", [2, 1024])
    gout_d = inp("gout", [128, 16])
    bsT_d = inp("bsT", [128, 8])
    wsT_d = inp("wsT", [128, 8 * 128])
    biasT_d = inp("biasT", [NH, 128, 640])
    ident_d = inp("ident", [128, 128])
    out_d = nc.dram_tensor("out", [NMAIN, D], F32, kind="ExternalOutput").ap()
    kd = dict(kind="ExternalOutput") if dbg else {}
    X1 = nc.dram_tensor("X1", [NEXT, D], F32, **kd).ap()
    X2 = nc.dram_tensor("X2", [NMAIN, D], F32, **kd).ap()
    Yd = nc.dram_tensor("Yd", [1024, D], F32, **kd).ap()
    Cs = nc.dram_tensor("Cs", [3 * 128, D], F32, **kd).ap()
    if dbg:
        dbg_cols = nc.dram_tensor("dbg_cols", [128, 160], F32, kind="ExternalOutput").ap()
        dbg_hT = nc.dram_tensor("dbg_hT", [128, 16 * 1024], BF16, kind="ExternalOutput").ap()
        dbg_aT = nc.dram_tensor("dbg_aT", [128, NJ * 1024], BF16, kind="ExternalOutput").ap()

    P.setup_engines()
    PE, ACT, DVE, POOL, SP = P.PE, P.ACT, P.DVE, P.POOL, P.SP
    V, A, T = nc.vector, nc.scalar, nc.tensor

    ident = en(nc.sbuf_tensor("ident_sb", [128, 128], F32))
    cols = en(nc.sbuf_tensor("cols", [128, 16 * 10], F32))
    hv = en(nc.sbuf_tensor("hv_sb", [128, 1], F32))
    epst = en(nc.sbuf_tensor("epst", [128, 1], F32))
    bsT = en(nc.sbuf_tensor("bsT_sb", [128, 8], F32))
    wring = [en(nc.sbuf_tensor(f"wring{i}", [128, 8192], BF16)) for i in range(3)]
    stt = [en(nc.sbuf_tensor(f"stt{i}", [128, 8], F32)) for i in range(4)]
    junk = en(nc.sbuf_tensor("junk", [128, D], BF16))
    ystage = [en(nc.sbuf_tensor(f"ystage{i}", [128, 512], F32)) for i in range(2)]
    banks = [en(nc.psum_tensor(f"bank{i}", [128, 512], F32)) for i in range(8)]

    b_ident, b_cols, b_hv, b_eps, b_bsT = Buf(), Buf(), Buf(), Buf(), Buf()
    b_wring = [Buf(f"wr{i}") for i in range(3)]
    b_stt = [Buf() for _ in range(4)]
    b_ystage = [Buf(), Buf()]
    b_bank = [Buf(f"bank{i}") for i in range(8)]
    bank_ctr = [0]

    def next_bank():
        i = bank_ctr[0] % 8
        bank_ctr[0] += 1
        return banks[i], b_bank[i]

    def Acol(s):
        return cols[:, (2 * s) * 16:(2 * s + 1) * 16]

    def Bcol(s):
        return cols[:, (2 * s + 1) * 16:(2 * s + 2) * 16]

    def gprecol(s):
        return cols[:, (6 + s) * 16:(7 + s) * 16]

    goutcol = cols[:, 9 * 16:10 * 16]

    d_ct = P.dsem("ct")
    d_w = [P.dsem(f"w{i}") for i in range(3)]
    d_ld = [P.dsem(f"ld{i}") for i in range(4)]
    d_st = [P.dsem("st0"), P.dsem("st1")]
    d_cs = P.dsem("cs")
    d_ys = [P.dsem("ys0"), P.dsem("ys1")]

    plan = []

    def plan_ada():
        for v in range(9):
            for nb in range(4):
                c0 = v * D + nb * 512
                plan.append((f"ada{v}_{nb}",
                             lambda s, c0=c0: (s[:, :].rearrange("p (k c) -> p k c", k=16),
                                               wada_d[:, c0:c0 + 512].rearrange("(k p) c -> p k c", p=128))))

    def plan_ffn(f):
        wg, wd = wgu_d[f], wdn_d[f]
        ngroups = 3 if f == 0 else 2
        for g in range(ngroups):
            for blk in range(22):
                j0 = 2 * blk
                ncol = 256 if blk < 21 else 128
                plan.append((f"gu{f}_{g}_{blk}",
                             lambda s, j0=j0, ncol=ncol, wg=wg: (
                                 s[:, :].rearrange("p (two k c) -> p two k c", two=2, k=16)[:, :, :, 0:ncol],
                                 wg[:, :].rearrange("(k p) (two c) -> p two k c", p=128, two=2)[:, :, :, j0 * 128:j0 * 128 + ncol])))
            for cb in range(4):
                for jb in range(3):
                    j0 = 16 * jb
                    nj = min(16, NJ - j0)
                    plan.append((f"dn{f}_{g}_{cb}_{jb}",
                                 lambda s, j0=j0, nj=nj, cb=cb, wd=wd: (
                                     s[:, :].rearrange("p (j c) -> p j c", j=16)[:, 0:nj, :],
                                     wd[j0 * 128:(j0 + nj) * 128, cb * 512:(cb + 1) * 512].rearrange("(j p) c -> p j c", p=128))))

    def plan_mixer():
        for gi in range(5):
            blks = [2, 3, 4, 5] if gi == 0 else list(range(10))
            for blk in blks:
                plan.append((f"in{gi}_{blk}",
                             lambda s, blk=blk: (s[:, :].rearrange("p (k c) -> p k c", k=16),
                                                 win_d[:, blk * 512:(blk + 1) * 512].rearrange("(k p) c -> p k c", p=128))))
            if gi > 0:
                for cb in range(4):
                    plan.append((f"wo{gi}_{cb}",
                                 lambda s, cb=cb: (s[:, :].rearrange("p (k c) -> p k c", k=16),
                                                   wout_d[:, cb * 512:(cb + 1) * 512].rearrange("(k p) c -> p k c", p=128))))

    if not skip_pro:
        plan_ada()
    if lvl >= 1 and stop != 1 and not os.environ.get("SKIP_F1"):
        plan_ffn(0)
    if lvl >= 2:
        plan_mixer()
    if lvl >= 3:
        plan_ffn(1)
    ws_state = {"issued": 0, "got": 0}

    def ws_issue_upto(n):
        while ws_state["issued"] < min(n, len(plan)):
            i = ws_state["issued"]
            s = i % 3
            o, src = plan[i][1](wring[s])
            P.dma(POOL, d_w[s], o, src, writes=[b_wring[s]])
            ws_state["issued"] += 1

    def ws_get(tag):
        i = ws_state["got"]
        assert plan[i][0] == tag, (plan[i][0], tag)
        ws_issue_upto(i + 1)
        ws_state["got"] += 1
        return wring[i % 3], b_wring[i % 3]

    def ws_done():
        ws_issue_upto(ws_state["got"] + 2)

    P.dma(SP, P.dsem("m"), ident[:], ident_d[:, :], writes=[b_ident])
    P.dma(SP, P.dsem("m"), hv[:], hv_d[:, :], writes=[b_hv])
    P.dma(SP, P.dsem("m"), bsT[:], bsT_d[:, :], writes=[b_bsT])
    P.dma(SP, P.dsem("m"), cols[:, 96:144], gpre_d[:, :], pwrites=[b_cols])
    P.dma(SP, P.dsem("m"), cols[:, 144:160], gout_d[:, :], pwrites=[b_cols])
    P.op(DVE, lambda: V.memset(epst[:], EPS), writes=[b_eps])
    ws_issue_upto(2)

    def rstd_from_sumsq(si, n):
        P.op(ACT, lambda: A.activation(out=stt[si][:, 1:2], in_=stt[si][:, 0:1], func=AF.Sqrt,
                                       scale=1.0 / n, bias=epst[:, 0:1]), reads=[b_eps], writes=[b_stt[si]])
        P.op(DVE, lambda: V.reciprocal(out=stt[si][:, 2:3], in_=stt[si][:, 1:2]), writes=[b_stt[si]])

    def transpose_tile(src, b_src, dst_fn, b_dst, scol, bcol):
        bk = [next_bank() for _ in range(4)]
        for q in range(4):
            P.pe_group([lambda kc=kc, q=q: T.transpose(out=bk[q][0][:, (kc % 4) * 128:(kc % 4 + 1) * 128],
                                                       in_=src[:, kc * 128:(kc + 1) * 128], identity=ident[:])
                        for kc in range(4 * q, 4 * q + 4)], reads=[b_src, b_ident], writes=[bk[q][1]])
        DT = int(os.environ.get("DBG_T", "2"))
        for kc in range(16):
            q = kc // 4
            pin = bk[q][0][:, (kc % 4) * 128:(kc % 4 + 1) * 128]
            if DT == 0 or (DT == 1 and kc % 2 == 1) or (DT == 3 and kc % 2 == 0):
                continue
            if True:
                if bcol is None:
                    P.op(DVE, lambda: V.tensor_scalar(out=dst_fn(kc), in0=pin, scalar1=scol[:, kc:kc + 1], scalar2=None,
                                                      op0=ALU.mult), reads=[bk[q][1], b_cols], pwrites=[b_dst])
                else:
                    P.op(DVE, lambda: V.tensor_scalar(out=dst_fn(kc), in0=pin, scalar1=scol[:, kc:kc + 1],
                                                      scalar2=bcol[:, kc:kc + 1], op0=ALU.mult, op1=ALU.add),
                         reads=[bk[q][1], b_cols], pwrites=[b_dst])
            else:
                if bcol is None:
                    P.op(ACT, lambda: A.activation(out=dst_fn(kc), in_=pin, func=AF.Copy, scale=scol[:, kc:kc + 1]),
                         reads=[bk[q][1], b_cols], pwrites=[b_dst])
                else:
                    P.op(ACT, lambda: A.activation(out=dst_fn(kc), in_=pin, func=AF.Identity, scale=scol[:, kc:kc + 1],
                                                   bias=bcol[:, kc:kc + 1]), reads=[bk[q][1], b_cols], pwrites=[b_dst])

    def stage_a(xs, r0, ntiles, Tt, b_Tt, hT, b_hT, s):
        for i in range(ntiles):
            sl = i % 2
            xt, b_xt = Tt[sl], b_Tt[sl]
            xn, b_xn = Tt[2 + sl], b_Tt[2 + sl]
            import os
            DA = int(os.environ.get("DBG_A", "9"))
            if i >= int(os.environ.get("DBG_NT", "99")):
                continue
            P.dma(SP, d_ld[sl], xt, xs[r0 + i * 128:r0 + (i + 1) * 128, :], writes=[b_xt])
            P.op(DVE, lambda: V.scalar_tensor_tensor(out=junk[:], in0=xt, scalar=1.0, in1=xt, op0=ALU.mult, op1=ALU.mult,
                                                     accum_out=stt[sl][:, 0:1]), reads=[b_xt], writes=[b_stt[sl]])
            if DA < 2:
                continue
            rstd_from_sumsq(sl, D)
            if DA < 3:
                continue
            P.op(ACT, lambda: A.activation(out=xn, in_=xt, func=AF.Copy, scale=stt[sl][:, 2:3]),
                 reads=[b_xt, b_stt[sl]], writes=[b_xn])
            if DA < 4:
                continue
            transpose_tile(xn, b_xn, lambda kc, i=i: hT[:, kc, i * 128:(i + 1) * 128], b_hT[i], Acol(s), Bcol(s))

    def stage_d(xs, rs0, xd, rd0, ntiles, Tt, b_Tt, ct, b_ct):
        for i in range(ntiles):
            sl = i % 2
            yt, b_yt = Tt[sl], b_Tt[sl]
            xt, b_xt = Tt[2 + sl], b_Tt[2 + sl]
            P.dma(SP, d_ld[sl], yt, Yd[i * 128:(i + 1) * 128, :], writes=[b_yt])
            P.dma(SP, d_ld[2 + sl], xt, xs[rs0 + i * 128:rs0 + (i + 1) * 128, :], writes=[b_xt])
            P.op(DVE, lambda: V.scalar_tensor_tensor(out=junk[:], in0=yt, scalar=1.0, in1=yt, op0=ALU.mult, op1=ALU.mult,
                                                     accum_out=stt[sl][:, 0:1]), reads=[b_yt], writes=[b_stt[sl]])
            rstd_from_sumsq(sl, D)
            P.op(DVE, lambda: V.scalar_tensor_tensor(out=yt, in0=yt, scalar=stt[sl][:, 2:3], in1=ct, op0=ALU.mult,
                                                     op1=ALU.mult), reads=[b_stt[sl], b_ct], writes=[b_yt])
            P.op(DVE, lambda: V.tensor_tensor(out=xt, in0=yt, in1=xt, op=ALU.add), reads=[b_yt], writes=[b_xt])
            tok = P.dma(SP, d_st[sl], xd[rd0 + i * 128:rd0 + (i + 1) * 128, :], xt, reads=[b_xt])
            P.pending_stores.append(tok)

    evac_ctr = [0]

    def store_y_block(bank, b_bk, i, cb):
        k = evac_ctr[0] % 2
        evac_ctr[0] += 1
        if k == 0:
            P.op(ACT, lambda: A.copy(out=ystage[k][:], in_=bank[:, :]), reads=[b_bk], writes=[b_ystage[k]])
        else:
            P.op(DVE, lambda: V.tensor_copy(out=ystage[k][:], in_=bank[:, :]), reads=[b_bk], writes=[b_ystage[k]])
        tok = P.dma(SP, d_ys[k], Yd[i * 128:(i + 1) * 128, cb * 512:(cb + 1) * 512], ystage[k][:], reads=[b_ystage[k]])
        P.pending_stores.append(tok)

    def load_ct(ct, b_ct, s):
        P.dma(SP, d_ct, ct, Cs[s * 128:(s + 1) * 128, :], writes=[b_ct])

    if skip_pro:
        P.op(DVE, lambda: V.memset(cols[:, 0:96], 1.0), pwrites=[b_cols])
    with ExitStack() as sc:
        e2 = sc.enter_context
        cT = e2(nc.sbuf_tensor("cT_sb", [128, 16], F32))
        cact = e2(nc.sbuf_tensor("cact", [128, 16], F32))
        crep = e2(nc.sbuf_tensor("crep", [128, 16, 128], BF16))
        modv = e2(nc.sbuf_tensor("modv", [128, D], F32))
        badab = e2(nc.sbuf_tensor("badab", [128, D], F32))
        gpostb = e2(nc.sbuf_tensor("gpostb", [128, D], F32))
        tmpd = e2(nc.sbuf_tensor("tmpd", [128, D], F32))
        cres = e2(nc.sbuf_tensor("cres", [128, D], F32))
        dcol = e2(nc.sbuf_tensor("dcol", [128, 16], F32))
        b_cT, b_cact, b_crep, b_modv, b_badab, b_gpostb, b_tmpd, b_cres, b_dcol = [Buf() for _ in range(9)]
        P.dma(SP, P.dsem("m"), cT[:], cT_d[:, :], writes=[b_cT])
        P.op(ACT, lambda: A.activation(out=cact[:], in_=cT[:], func=AF.Silu), reads=[b_cT], writes=[b_cact])
        P.op(DVE, lambda: V.tensor_copy(out=crep[:], in_=cact[:].unsqueeze(2).to_broadcast([128, 16, 128])),
             reads=[b_cact], writes=[b_crep])
        for v in range(0 if skip_pro else 9):
            s, role = v // 3, v % 3
            P.dma(SP, d_ld[0], badab[:], bada_d[0:1, v * D:(v + 1) * D].partition_broadcast(128), writes=[b_badab])
            if role == 2:
                P.dma(SP, d_ld[1], gpostb[:], gpost_d[s:s + 1, :].partition_broadcast(128), writes=[b_gpostb])
            for nb in range(4):
                slot, b_slot = ws_get(f"ada{v}_{nb}")
                sv = slot[:, :].rearrange("p (k c) -> p k c", k=16)
                bk, b_bk = next_bank()
                P.pe_group([lambda kc=kc: T.matmul(out=bk[:, :], lhsT=crep[:, kc, :], rhs=sv[:, kc, :],
                                                   start=(kc == 0), stop=(kc == 15)) for kc in range(16)],
                           reads=[b_crep, b_slot], writes=[b_bk])
                ws_done()
                P.op(DVE, lambda: V.tensor_tensor(out=modv[:, nb * 512:(nb + 1) * 512], in0=bk[:, :],
                                                  in1=badab[:, nb * 512:(nb + 1) * 512], op=ALU.add),
                     reads=[b_bk, b_badab], pwrites=[b_modv])
            if role in (0, 1):
                P.op(DVE, lambda: V.tensor_tensor(out=tmpd[:].rearrange("p (a b) -> p a b", a=16),
                                                  in0=modv[:].rearrange("p (a b) -> p a b", a=16),
                                                  in1=ident[:].unsqueeze(1).to_broadcast([128, 16, 128]), op=ALU.mult),
                     reads=[b_modv, b_ident], writes=[b_tmpd])
                if role == 0:
                    P.op(DVE, lambda: V.tensor_reduce(out=Bcol(s), in_=tmpd[:].rearrange("p (a b) -> p a b", a=16),
                                                      axis=AX.X, op=ALU.add), reads=[b_tmpd], pwrites=[b_cols])
                else:
                    P.op(DVE, lambda: V.tensor_reduce(out=dcol[:], in_=tmpd[:].rearrange("p (a b) -> p a b", a=16),
                                                      axis=AX.X, op=ALU.add), reads=[b_tmpd], writes=[b_dcol])
                    P.op(DVE, lambda: V.scalar_tensor_tensor(out=Acol(s), in0=dcol[:], scalar=1.0, in1=gprecol(s),
                                                             op0=ALU.add, op1=ALU.mult), reads=[b_dcol, b_cols], pwrites=[b_cols])
            else:
                gsc = 1.0 if s == 1 else 0.5
                P.op(DVE, lambda: V.scalar_tensor_tensor(out=cres[:], in0=modv[:], scalar=gsc, in1=gpostb[:],
                                                         op0=ALU.mult, op1=ALU.mult),
                     reads=[b_modv, b_gpostb], writes=[b_cres])
                tok = P.dma(SP, d_cs, Cs[s * 128:(s + 1) * 128, :], cres[:], reads=[b_cres])
                P.pending_stores.append(tok)
        if dbg:
            P.pending_stores.append(P.dma(SP, P.dsem("dbg"), dbg_cols[:, :], cols[:], reads=[b_cols]))
        P.barrier()
        P.pending_stores = []

    def ffn_phase(f, xs, xd, groups, s):
        with ExitStack() as sc:
            e2 = sc.enter_context
            hTt = e2(nc.sbuf_tensor(f"hT{f}", [128, 16, 1024], BF16))
            RA = e2(nc.sbuf_tensor(f"RA{f}", [128, 22016], F32))
            sg = [e2(nc.sbuf_tensor(f"sg{f}_{i}", [128, 512], F32)) for i in range(2)]
            b_sg = [Buf(), Buf()]
            aT = RA[:, :].bitcast(BF16).rearrange("p (j t) -> p j t", j=NJ)
            Tt = [RA[:, i * D:(i + 1) * D] for i in range(4)]
            ct = RA[:, 4 * D:5 * D]
            b_Tt = [Buf() for _ in range(4)]
            b_ct = Buf()
            b_hT = [Buf() for _ in range(8)]
            b_aT = [Buf() for _ in range(NJ)]
            sgc = 0
            for g, (rs0, rd0, ntok) in enumerate(groups):
                nt = ntok // 128
                ntb = ntok // 512
                stage_a(xs, rs0, nt, Tt, b_Tt, hTt, b_hT, s)
                P.barrier()
                if stop == 1:
                    P.pending_stores.append(P.dma(SP, P.dsem("dbg"), dbg_hT[:, :], hTt[:].rearrange("p k t -> p (k t)"), reads=b_hT))
                    P.barrier()
                    P.stopped = True
                    return
                for blk in range(22):
                    slot, b_slot = ws_get(f"gu{f}_{g}_{blk}")
                    sv = slot[:, :].rearrange("p (two k c) -> p two k c", two=2, k=16)
                    for jj in range(2):
                        j = 2 * blk + jj
                        if j >= NJ:
                            continue
                        for tb in range(ntb):
                            bg, b_bg = next_bank()
                            bu, b_bu = next_bank()
                            hb = [b_hT[4 * tb + q] for q in range(4)]
                            P.pe_group([lambda kc=kc: T.matmul(out=bg[:, :], lhsT=sv[:, 0, kc, jj * 128:(jj + 1) * 128],
                                                               rhs=hTt[:, kc, tb * 512:(tb + 1) * 512],
                                                               start=(kc == 0), stop=(kc == 15)) for kc in range(16)],
                                       reads=[b_slot] + hb, writes=[b_bg])
                            P.pe_group([lambda kc=kc: T.matmul(out=bu[:, :], lhsT=sv[:, 1, kc, jj * 128:(jj + 1) * 128],
                                                               rhs=hTt[:, kc, tb * 512:(tb + 1) * 512],
                                                               start=(kc == 0), stop=(kc == 15)) for kc in range(16)],
                                       reads=[b_slot] + hb, writes=[b_bu])
                            k = sgc % 2
                            sgc += 1
                            P.op(ACT, lambda: A.activation(out=sg[k][:], in_=bg[:, :], func=AF.Silu),
                                 reads=[b_bg], writes=[b_sg[k]])
                            P.op(DVE, lambda: V.tensor_tensor(out=aT[:, j, tb * 512:(tb + 1) * 512], in0=sg[k][:],
                                                              in1=bu[:, :], op=ALU.mult),
                                 reads=[b_sg[k], b_bu], pwrites=[b_aT[j]])
                    ws_done()
                if stop == 2:
                    P.barrier()
                    P.pending_stores.append(P.dma(SP, P.dsem("dbg"), dbg_aT[:, :], RA[:, :].bitcast(BF16), reads=b_aT))
                    P.barrier()
                    P.stopped = True
                    return
                for cb in range(4):
                    for jb in range(3):
                        slot, b_slot = ws_get(f"dn{f}_{g}_{cb}_{jb}")
                        sv = slot[:, :].rearrange("p (j c) -> p j c", j=16)
                        j0 = 16 * jb
                        nj = min(16, NJ - j0)
                        for i in range(nt):
                            P.pe_group([lambda j=j: T.matmul(out=banks[i][:, :], lhsT=aT[:, j, i * 128:(i + 1) * 128],
                                                             rhs=sv[:, j - j0, :], start=(j == 0), stop=(j == NJ - 1))
                                        for j in range(j0, j0 + nj)],
                                       reads=[b_slot] + [b_aT[j] for j in range(j0, j0 + nj)], writes=[b_bank[i]])
                            if jb == 2:
                                store_y_block(banks[i], b_bank[i], i, cb)
                        ws_done()
                P.barrier()
                P.pending_stores = []
                if stop == 3:
                    P.stopped = True
                    return
                load_ct(ct, b_ct, s)
                stage_d(xs, rs0, xd, rd0, nt, Tt, b_Tt, ct, b_ct)
                P.barrier(b_Tt)
                P.pending_stores = []
                if stop == 4:
                    P.stopped = True
                    return

    P.stopped = False
    SKIP_F1 = bool(os.environ.get("SKIP_F1"))
    if lvl >= 1 and not SKIP_F1:
        ffn_phase(0, x_d, X1, [(0, 0, 1024), (1024, 1024, 1024), (2048, 2048, 512)], 0)

    def mixer_phase():
      with ExitStack() as sc:
        e2 = sc.enter_context
        hTm = e2(nc.sbuf_tensor("hTm", [128, 16, 512], BF16))
        qT = e2(nc.sbuf_tensor("qT", [128, 8, 512], BF16))
        kT = [e2(nc.sbuf_tensor(f"kT{i}", [128, 8, 512], BF16)) for i in range(2)]
        vaug = [e2(nc.sbuf_tensor(f"vaug{i}", [128, 4, NH, VW], BF16)) for i in range(2)]
        RM = e2(nc.sbuf_tensor("RM", [128, 5 * D], F32))
        out_a = e2(nc.sbuf_tensor("out_a", [128, 1024], F32))
        out_b = e2(nc.sbuf_tensor("out_b", [128, 1024], F32))
        vln = e2(nc.sbuf_tensor("vln", [128, 1024], BF16))
        sc_sb = [e2(nc.sbuf_tensor(f"sc_sb{i}", [128, 640], F32)) for i in range(2)]
        pT = [e2(nc.sbuf_tensor(f"pT{i}", [128, 640], BF16)) for i in range(2)]
        biasb = [e2(nc.sbuf_tensor(f"biasb{i}", [128, 640], F32)) for i in range(2)]
        lnb = e2(nc.sbuf_tensor("lnb", [128, 2048], F32))
        wsTf = e2(nc.sbuf_tensor("wsTf", [128, 1024], F32))
        wsT = e2(nc.sbuf_tensor("wsT_bf", [128, 8, 128], BF16))
        b_hTm = [Buf() for _ in range(4)]
        b_qT, b_kT, b_vaug = Buf(), [Buf(), Buf()], [Buf(), Buf()]
        b_out_a, b_out_b, b_vln = Buf(), Buf(), Buf()
        b_sc, b_pT, b_biasb = [Buf(), Buf()], [Buf(), Buf()], [Buf(), Buf()]
        b_lnb, b_wsTf, b_wsT = Buf(), Buf(), Buf()
        Tt = [RM[:, i * D:(i + 1) * D] for i in range(4)]
        ct = RM[:, 4 * D:5 * D]
        b_Tt = [Buf() for _ in range(4)]
        b_ct = Buf()
        u_g = RM[:, 0:2 * D].rearrange("p (t c) -> p t c", t=4)
        vg = RM[:, 2 * D:4 * D].rearrange("p (t c) -> p t c", t=4)
        merged = RM[:, 4 * D:5 * D]
        b_ug, b_vg = [Buf() for _ in range(4)], [Buf() for _ in range(4)]
        b_merged = Buf()
        d_bias = [P.dsem("bias0"), P.dsem("bias1")]

        P.dma(SP, P.dsem("m"), lnb[:, 0:1024], lnv_d[0:1, :].partition_broadcast(128), pwrites=[b_lnb])
        P.dma(SP, P.dsem("m"), lnb[:, 1024:2048], lnv_d[1:2, :].partition_broadcast(128), pwrites=[b_lnb])
        P.dma(SP, P.dsem("m"), wsTf[:], wsT_d[:, :], writes=[b_wsTf])
        P.op(DVE, lambda: V.tensor_copy(out=wsT[:].rearrange("p g t -> p (g t)"), in_=wsTf[:]), reads=[b_wsTf], writes=[b_wsT])
        P.op(DVE, lambda: V.memset(wsT[64:128, :, 0:64], 0.0), writes=[b_wsT])

        ev = [0]

        def copy_evac(out, in_, reads, writes):
            ev[0] += 1
            if ev[0] % 2 == 0:
                P.op(ACT, lambda: A.copy(out=out, in_=in_), reads=reads, pwrites=writes)
            else:
                P.op(DVE, lambda: V.tensor_copy(out=out, in_=in_), reads=reads, pwrites=writes)

        for gi in range(5):
            ks = gi % 2
            stage_a(X1, 512 * gi, 4, Tt, b_Tt, hTm, b_hTm, 1)
            P.barrier()
            P.op(DVE, lambda: V.memset(vaug[ks][:, :, :, 64:65], 1.0), writes=[b_vaug[ks]])
            if gi == 0:
                P.op(DVE, lambda: V.tensor_scalar(out=vaug[ks][:, :, :, 64:65], in0=vaug[ks][:, :, :, 64:65],
                                                  scalar1=hv[:, 0:1], scalar2=None, op0=ALU.mult),
                     reads=[b_hv], writes=[b_vaug[ks]])
            blks = [2, 3, 4, 5] if gi == 0 else list(range(10))
            for blk in blks:
                slot, b_slot = ws_get(f"in{gi}_{blk}")
                sv = slot[:, :].rearrange("p (k c) -> p k c", k=16)
                if blk < 4:
                    dst, b_dst = (qT, b_qT) if blk < 2 else (kT[ks], b_kT[ks])
                    for c in range(4):
                        bk, b_bk = next_bank()
                        P.pe_group([lambda kc=kc: T.matmul(out=bk[:, :], lhsT=sv[:, kc, c * 128:(c + 1) * 128],
                                                           rhs=hTm[:, kc, :], start=(kc == 0), stop=(kc == 15))
                                    for kc in range(16)], reads=[b_slot] + b_hTm, writes=[b_bk])
                        copy_evac(dst[:, (blk % 2) * 4 + c, :], bk[:, :], [b_bk], [b_dst])
                else:
                    for i in range(4):
                        bk, b_bk = next_bank()
                        P.pe_group([lambda kc=kc: T.matmul(out=bk[:, :], lhsT=hTm[:, kc, i * 128:(i + 1) * 128],
                                                           rhs=sv[:, kc, :], start=(kc == 0), stop=(kc == 15))
                                    for kc in range(16)], reads=[b_slot, b_hTm[i]], writes=[b_bk])
                        if blk < 6:
                            hb = blk - 4
                            o = vaug[ks][:, i, hb * 8:(hb + 1) * 8, 0:64]
                            src = bk[:, :].rearrange("p (h d) -> p h d", h=8)
                            if gi == 0:
                                P.op(DVE, lambda: V.tensor_scalar(out=o, in0=src, scalar1=hv[:, 0:1], scalar2=None,
                                                                  op0=ALU.mult), reads=[b_bk, b_hv], pwrites=[b_vaug[ks]])
                            else:
                                copy_evac(o, src, [b_bk], [b_vaug[ks]])
                        elif blk < 8:
                            P.op(ACT, lambda: A.activation(out=u_g[:, i, (blk - 6) * 512:(blk - 5) * 512], in_=bk[:, :],
                                                           func=AF.Gelu), reads=[b_bk], pwrites=[b_ug[i]])
                        else:
                            P.op(ACT, lambda: A.activation(out=vg[:, i, (blk - 8) * 512:(blk - 7) * 512], in_=bk[:, :],
                                                           func=AF.Gelu), reads=[b_bk], pwrites=[b_vg[i]])
                ws_done()
            DM = int(os.environ.get("DBG_M", "99"))
            if gi == 0:
                P.barrier()
                if DM == 1:
                    return
                continue
            if DM == 2:
                P.barrier()
                return
            for t in range(4):
                def scores(h):
                    k = h % 2
                    hp, hc = h % 2, h // 2
                    P.dma(SP, d_bias[k], biasb[k][:], biasT_d[h, :, :], writes=[b_biasb[k]])
                    bA, b_bA = next_bank()
                    bB, b_bB = next_bank()
                    for c in range(5):
                        kt = 4 * gi + t - 4 + c
                        ksl = (kt // 4) % 2
                        kc0 = (kt % 4) * 128
                        o = bA[:, c * 128:(c + 1) * 128] if c < 4 else bB[:, 0:128]
                        P.pe_group([lambda: T.matmul(out=o, lhsT=kT[ksl][hp * 64:(hp + 1) * 64, hc, kc0:kc0 + 128],
                                                     rhs=qT[hp * 64:(hp + 1) * 64, hc, t * 128:(t + 1) * 128],
                                                     start=True, stop=True)],
                                   reads=[b_kT[ksl], b_qT], pwrites=[b_bA if c < 4 else b_bB])
                    P.op(DVE, lambda: V.scalar_tensor_tensor(out=sc_sb[k][:, 0:512], in0=bA[:, :], scalar=0.125,
                                                             in1=biasb[k][:, 0:512], op0=ALU.mult, op1=ALU.add),
                         reads=[b_bA, b_biasb[k]], pwrites=[b_sc[k]])
                    P.op(DVE, lambda: V.scalar_tensor_tensor(out=sc_sb[k][:, 512:640], in0=bB[:, 0:128], scalar=0.125,
                                                             in1=biasb[k][:, 512:640], op0=ALU.mult, op1=ALU.add),
                         reads=[b_bB, b_biasb[k]], pwrites=[b_sc[k]])
                    P.op(ACT, lambda: A.activation(out=pT[k][:], in_=sc_sb[k][:], func=AF.Exp),
                         reads=[b_sc[k]], writes=[b_pT[k]])

                def pv(h):
                    k = h % 2
                    bO, b_bO = next_bank()
                    fs = []
                    rd = [b_pT[k]]
                    for c in range(5):
                        kt = 4 * gi + t - 4 + c
                        ksl = (kt // 4) % 2
                        rd.append(b_vaug[ksl])
                        fs.append(lambda c=c, kt=kt, ksl=ksl: T.matmul(out=bO[:, 0:VW], lhsT=pT[k][:, c * 128:(c + 1) * 128],
                                                                      rhs=vaug[ksl][:, kt % 4, h, :],
                                                                      start=(c == 0), stop=(c == 4)))
                    P.pe_group(fs, reads=rd, writes=[b_bO])
                    P.op(DVE, lambda: V.reciprocal(out=stt[2][:, 0:1], in_=bO[:, 64:65]), reads=[b_bO], writes=[b_stt[2]])
                    P.op(DVE, lambda: V.tensor_scalar(out=out_a[:, h * 64:(h + 1) * 64], in0=bO[:, 0:64],
                                                      scalar1=stt[2][:, 0:1], scalar2=None, op0=ALU.mult),
                         reads=[b_bO, b_stt[2]], pwrites=[b_out_a])

                for h in range(NH + 1):
                    if h < NH:
                        scores(h)
                    if h > 0:
                        pv(h - 1)
                if DM == 3:
                    P.barrier()
                    return
                vt = vg[:, t, :]
                P.op(DVE, lambda: V.tensor_reduce(out=stt[3][:, 0:1], in_=vt, axis=AX.X, op=ALU.add),
                     reads=[b_vg[t]], writes=[b_stt[3]])
                P.op(DVE, lambda: V.tensor_scalar(out=stt[3][:, 1:2], in0=stt[3][:, 0:1], scalar1=-1.0 / 1024, scalar2=None,
                                                  op0=ALU.mult), writes=[b_stt[3]])
                P.op(DVE, lambda: V.tensor_scalar(out=vt, in0=vt, scalar1=stt[3][:, 1:2], scalar2=None, op0=ALU.add),
                     reads=[b_stt[3]], writes=[b_vg[t]])
                P.op(DVE, lambda: V.scalar_tensor_tensor(out=junk[:, 0:1024], in0=vt, scalar=1.0, in1=vt, op0=ALU.mult,
                                                         op1=ALU.mult, accum_out=stt[0][:, 0:1]),
                     reads=[b_vg[t]], writes=[b_stt[0]])
                rstd_from_sumsq(0, 1024)
                P.op(DVE, lambda: V.scalar_tensor_tensor(out=vt, in0=vt, scalar=stt[0][:, 2:3], in1=lnb[:, 0:1024],
                                                         op0=ALU.mult, op1=ALU.mult),
                     reads=[b_stt[0], b_lnb], writes=[b_vg[t]])
                P.op(DVE, lambda: V.tensor_tensor(out=vln[:], in0=vt, in1=lnb[:, 1024:2048], op=ALU.add),
                     reads=[b_vg[t], b_lnb], writes=[b_vln])
                bz = [next_bank(), next_bank()]
                for g in range(8):
                    zb, b_zb = bz[g // 4]
                    P.pe_group([lambda: T.matmul(out=zb[:, (g % 4) * 128:(g % 4 + 1) * 128], lhsT=wsT[:, g, :],
                                                 rhs=vln[:, g * 128:(g + 1) * 128], start=True, stop=True)],
                               reads=[b_wsT, b_vln], pwrites=[b_zb])
                for g in range(8):
                    zb, b_zb = bz[g // 4]
                    P.op(DVE, lambda: V.scalar_tensor_tensor(out=out_b[:, g * 128:(g + 1) * 128],
                                                             in0=zb[:, (g % 4) * 128:(g % 4 + 1) * 128],
                                                             scalar=bsT[:, g:g + 1], in1=u_g[:, t, g * 128:(g + 1) * 128],
                                                             op0=ALU.add, op1=ALU.mult),
                         reads=[b_zb, b_bsT, b_ug[t]], pwrites=[b_out_b])
                if DM == 4:
                    P.barrier()
                    return
                P.op(DVE, lambda: V.scalar_tensor_tensor(out=junk[:, 0:1024], in0=out_a[:], scalar=1.0, in1=out_a[:],
                                                         op0=ALU.mult, op1=ALU.mult, accum_out=stt[0][:, 0:1]),
                     reads=[b_out_a], writes=[b_stt[0]])
                rstd_from_sumsq(0, 1024)
                P.op(DVE, lambda: V.scalar_tensor_tensor(out=junk[:, 1024:2048], in0=out_b[:], scalar=1.0, in1=out_b[:],
                                                         op0=ALU.mult, op1=ALU.mult, accum_out=stt[1][:, 0:1]),
                     reads=[b_out_b], writes=[b_stt[1]])
                rstd_from_sumsq(1, 1024)
                P.op(ACT, lambda: A.activation(out=merged[:, 0:1024], in_=out_a[:], func=AF.Copy, scale=stt[0][:, 2:3]),
                     reads=[b_out_a, b_stt[0]], pwrites=[b_merged])
                P.op(ACT, lambda: A.activation(out=merged[:, 1024:2048], in_=out_b[:], func=AF.Copy, scale=stt[1][:, 2:3]),
                     reads=[b_out_b, b_stt[1]], pwrites=[b_merged])
                transpose_tile(merged, b_merged, lambda kc, t=t: hTm[:, kc, t * 128:(t + 1) * 128], b_hTm[t], goutcol, None)
                if DM == 5:
                    P.barrier()
                    return
            for cb in range(4):
                slot, b_slot = ws_get(f"wo{gi}_{cb}")
                sv = slot[:, :].rearrange("p (k c) -> p k c", k=16)
                for i in range(4):
                    bk, b_bk = next_bank()
                    P.pe_group([lambda kc=kc: T.matmul(out=bk[:, :], lhsT=hTm[:, kc, i * 128:(i + 1) * 128],
                                                       rhs=sv[:, kc, :], start=(kc == 0), stop=(kc == 15))
                                for kc in range(16)], reads=[b_slot, b_hTm[i]], writes=[b_bk])
                    store_y_block(bk, b_bk, i, cb)
                ws_done()
            P.barrier()
            P.pending_stores = []
            load_ct(ct, b_ct, 1)
            stage_d(X1, 512 * gi, X2, 512 * (gi - 1), 4, Tt, b_Tt, ct, b_ct)
            P.barrier(b_Tt)
            P.pending_stores = []
            if DM == 6:
                return

    if lvl >= 2 and not P.stopped:
        mixer_phase()
    if lvl >= 3 and not P.stopped:
        ffn_phase(1, X2, out_d, [(0, 0, 1024), (1024, 1024, 1024)], 2)
    P.barrier(b_wring)
    P.st.close()
    return nc


_CACHE = {}


def _host_layout(inputs, core):
    b, half = core // 2, core % 2
    f32 = np.float32
    x = np.asarray(inputs["x"], dtype=f32)
    xs = np.zeros((NEXT, D), f32)
    xs[HALO:] = x[b, half * NMAIN:(half + 1) * NMAIN]
    if half == 1:
        xs[:HALO] = x[b, NMAIN - HALO:NMAIN]
    return {
        "x": xs,
        "cT": np.ascontiguousarray(np.asarray(inputs["c"], f32)[b].reshape(16, 128).T),
        "hv": np.full((128, 1), float(half), f32),
    }


def _shared_layout(inputs):
    f32 = np.float32
    g = lambda k: np.asarray(inputs[k], dtype=f32)[0]
    col = lambda v: np.ascontiguousarray(v.reshape(16, 128).T)
    gpre = np.concatenate([col(g("ffn1_pre_g")), col(g("mix_pre_g")), col(g("ffn2_pre_g"))], axis=1)
    gpost = np.stack([g("ffn1_post_g"), g("mix_post_g"), g("ffn2_post_g")])
    lnv = np.stack([g("ln_v_g"), g("ln_v_b")])
    gout = col(np.concatenate([g("g_out_a"), g("g_out_b")]))
    bsT = np.ascontiguousarray(g("b_s").T)
    wsT = np.ascontiguousarray(g("w_s").transpose(2, 0, 1)).reshape(128, 1024)
    rel_ext = np.concatenate([g("rel_bias"), np.full((NH, 1), -30000.0, f32)], axis=1)
    kj = np.arange(640)[:, None]
    qi = np.arange(128)[None, :]
    idx = np.clip(qi - kj + 512, -128, 128) + 128
    masked = ((kj < 64) & (qi >= 64)) | ((kj >= 576) & (qi < 64))
    idx = np.where(masked, 257, idx)
    bt = rel_ext[:, idx]
    biasT = np.ascontiguousarray(bt.reshape(NH, 5, 128, 128).transpose(0, 2, 1, 3)).reshape(NH, 128, 640)
    return {
        "w_ada": g("w_ada"), "b_ada": g("b_ada")[None, :],
        "w_gu1": g("ffn1_w_gu"), "w_dn1": g("ffn1_w_down"), "w_gu2": g("ffn2_w_gu"), "w_dn2": g("ffn2_w_down"),
        "w_in": g("w_in"), "w_out": g("w_out"),
        "gpre": np.ascontiguousarray(gpre), "gpost": gpost, "lnv": lnv, "gout": gout, "bsT": bsT, "wsT": wsT,
        "biasT": biasT, "ident": np.eye(128, dtype=f32),
    }


def kernel(**inputs):
    if "nc" not in _CACHE:
        _CACHE["nc"] = build_program()
    nc = _CACHE["nc"]
    shared = _shared_layout(inputs)
    in_maps = []
    for core in range(8):
        m = dict(shared)
        m.update(_host_layout(inputs, core))
        in_maps.append(m)
    res = run_bass_kernel_spmd(nc, in_maps, core_ids=list(range(8)))
    out = np.zeros((4, 4096, D), np.float32)
    for core in range(8):
        b, half = core // 2, core % 2
        out[b, half * NMAIN:(half + 1) * NMAIN] = res.results[core]["out"]
    return out
```

```python
import numpy as np
import concourse.bass as bass
import concourse.mybir as mybir
from concourse.bass_utils import run_bass_kernel_spmd
from contextlib import ExitStack

F32 = mybir.dt.float32
BF16 = mybir.dt.bfloat16
AF = mybir.ActivationFunctionType
ALU = mybir.AluOpType
AX = mybir.AxisListType

D = 2048
DFF = 5504
NJ = 43
HALO = 512
NMAIN = 2048
NEXT = NMAIN + HALO
EPS = 1e-6
NH = 16
VW = 65


import os
SKIP_SELF = os.environ.get("SKIP_SELF", "act,pe").split(",")


class Eng:
    def __init__(self, e, sem, name):
        self.e, self.sem, self.name = e, sem, name
        self.n = 0
        self.seen = {}
        self.last = None

    def wait(self, toks):
        best = {}
        for t in toks:
            if t is None:
                continue
            s, v = t
            k = id(s)
            if s is self.sem and self.name in SKIP_SELF:
                continue
            if k not in best or best[k][1] < v:
                best[k] = (s, v)
        for k, (s, v) in best.items():
            if self.seen.get(k, 0) < v:
                self.e.wait_ge(s, v)
                self.seen[k] = v

    def mark(self, ins):
        ins.then_inc(self.sem, 1)
        self.n += 1
        self.last = (self.sem, self.n)
        return self.last


class Buf:
    def __init__(self, name=""):
        self.name = name
        self.w = {}
        self.r = {}
        self.pr = {}

    @staticmethod
    def _add(d, tok):
        if tok is None:
            return
        k = id(tok[0])
        if k not in d or d[k][1] < tok[1]:
            d[k] = tok

    def rdeps(self):
        return list(self.w.values())

    def wdeps(self):
        return list(self.w.values()) + list(self.r.values()) + list(self.pr.values())

    def pwdeps(self):
        if self.r:
            npr = {}
            for d in (self.w, self.r, self.pr):
                for t in d.values():
                    self._add(npr, t)
            self.pr, self.w, self.r = npr, {}, {}
        return list(self.pr.values())

    def add_r(self, tok):
        self._add(self.r, tok)

    def set_w(self, tok):
        self.w = {}
        self._add(self.w, tok)
        self.r = {}
        self.pr = {}
        self._add(self.pr, tok)

    def add_w(self, tok):
        self._add(self.w, tok)


class _Stop(Exception):
    pass


class Prog:
    def __init__(self):
        self.nc = bass.Bass("TRN2", target_bir_lowering=False)
        self.st = ExitStack()
        self.nsem = 0
        self.pending_stores = []

    def sem(self, name):
        self.nsem += 1
        return self.st.enter_context(self.nc.semaphore(f"{name}_{self.nsem}"))

    def setup_engines(self):
        nc = self.nc
        self.PE = Eng(nc.tensor, self.sem("pe"), "pe")
        self.ACT = Eng(nc.scalar, self.sem("act"), "act")
        self.DVE = Eng(nc.vector, self.sem("dve"), "dve")
        self.POOL = Eng(nc.gpsimd, self.sem("pool"), "pool")
        self.SP = Eng(nc.sync, self.sem("sp"), "sp")

    def _deps(self, reads, writes, pwrites):
        deps = []
        for b in reads:
            deps += b.rdeps()
        for b in writes:
            deps += b.wdeps()
        for b in pwrites:
            deps += b.pwdeps()
        return deps

    def _upd(self, tok, reads, writes, pwrites):
        for b in reads:
            b.add_r(tok)
        for b in writes:
            b.set_w(tok)
        for b in pwrites:
            b.add_w(tok)

    def op(self, E, f, reads=(), writes=(), pwrites=()):
        E.wait(self._deps(reads, writes, pwrites))
        tok = E.mark(f())
        self._upd(tok, reads, writes, pwrites)
        return tok

    def pe_group(self, fs, reads=(), writes=(), pwrites=()):
        self.PE.wait(self._deps(reads, writes, pwrites))
        ins = None
        for f in fs:
            ins = f()
        tok = self.PE.mark(ins)
        self._upd(tok, reads, writes, pwrites)
        return tok

    def dma(self, Q, dsem, out, in_, reads=(), writes=(), pwrites=()):
        Q.wait(self._deps(reads, writes, pwrites))
        Q.e.dma_start(out=out, in_=in_).then_inc(dsem[0], 16)
        dsem[1] += 1
        tok = (dsem[0], 16 * dsem[1])
        self._upd(tok, reads, writes, pwrites)
        return tok

    def dsem(self, name):
        return [self.sem(name), 0]

    def sp_sync(self, extra=()):
        toks = [E.last for E in (self.PE, self.ACT, self.DVE)] + list(extra) + self.pending_stores
        self.SP.wait(toks)

    def wait_bufs(self, engines, bufs):
        toks = []
        for b in bufs:
            toks += b.wdeps()
        toks += self.pending_stores
        for E in engines:
            E.wait(toks)

    def barrier(self, bufs=()):
        toks = [E.last for E in (self.PE, self.ACT, self.DVE)]
        for b in bufs:
            toks += b.wdeps()
        toks += self.pending_stores
        for E in (self.PE, self.ACT, self.DVE, self.SP):
            E.wait(toks)


def build_program(upto="all", dbg=False, stop=0, skip_pro=False):
    P = Prog()
    lvl = {"pro": 0, "ffn1": 1, "mix": 2, "all": 3}[upto]
    nc = P.nc
    st = P.st
    en = st.enter_context

    def inp(name, shape):
        return nc.dram_tensor(name, shape, F32, kind="ExternalInput").ap()

    x_d = inp("x", [NEXT, D])
    cT_d = inp("cT", [128, 16])
    hv_d = inp("hv", [128, 1])
    wada_d = inp("w_ada", [D, 9 * D]) if not skip_pro else None
    bada_d = inp("b_ada", [1, 9 * D])
    wgu_d = [inp("w_gu1", [D, 2 * DFF]) if (lvl >= 1 and not os.environ.get("SKIP_F1")) else None, inp("w_gu2", [D, 2 * DFF]) if lvl >= 3 else None]
    wdn_d = [inp("w_dn1", [DFF, D]) if (lvl >= 1 and not os.environ.get("SKIP_F1")) else None, inp("w_dn2", [DFF, D]) if lvl >= 3 else None]
    win_d = inp("w_in", [D, 5120]) if lvl >= 2 else None
    wout_d = inp("w_out", [D, D]) if lvl >= 2 else None
    gpre_d = inp("gpre", [128, 3 * 16])
    gpost_d = inp("gpost", [3, D])
    lnv_d = inp("lnv", [2, 1024])
    gout_d = inp("gout", [128, 16])
    bsT_d = inp("bsT", [128, 8])
    wsT_d = inp("wsT", [128, 8 * 128])
    biasT_d = inp("biasT", [NH, 128, 640])
    ident_d = inp("ident", [128, 128])
    out_d = nc.dram_tensor("out", [NMAIN, D], F32, kind="ExternalOutput").ap()
    kd = dict(kind="ExternalOutput") if dbg else {}
    X1 = nc.dram_tensor("X1", [NEXT, D], F32, **kd).ap()
    X2 = nc.dram_tensor("X2", [NMAIN, D], F32, **kd).ap()
    Yd = nc.dram_tensor("Yd", [1024, D], F32, **kd).ap()
    Cs = nc.dram_tensor("Cs", [3 * 128, D], F32, **kd).ap()
    if dbg:
        dbg_cols = nc.dram_tensor("dbg_cols", [128, 160], F32, kind="ExternalOutput").ap()
        dbg_hT = nc.dram_tensor("dbg_hT", [128, 16 * 1024], BF16, kind="ExternalOutput").ap()
        dbg_aT = nc.dram_tensor("dbg_aT", [128, NJ * 1024], BF16, kind="ExternalOutput").ap()

    P.setup_engines()
    PE, ACT, DVE, POOL, SP = P.PE, P.ACT, P.DVE, P.POOL, P.SP
    V, A, T = nc.vector, nc.scalar, nc.tensor

    ident = en(nc.sbuf_tensor("ident_sb", [128, 128], F32))
    cols = en(nc.sbuf_tensor("cols", [128, 16 * 10], F32))
    hv = en(nc.sbuf_tensor("hv_sb", [128, 1], F32))
    epst = en(nc.sbuf_tensor("epst", [128, 1], F32))
    bsT = en(nc.sbuf_tensor("bsT_sb", [128, 8], F32))
    wring = [en(nc.sbuf_tensor(f"wring{i}", [128, 8192], BF16)) for i in range(3)]
    stt = [en(nc.sbuf_tensor(f"stt{i}", [128, 8], F32)) for i in range(4)]
    junk = en(nc.sbuf_tensor("junk", [128, D], BF16))
    ystage = [en(nc.sbuf_tensor(f"ystage{i}", [128, 512], F32)) for i in range(2)]
    banks = [en(nc.psum_tensor(f"bank{i}", [128, 512], F32)) for i in range(8)]

    b_ident, b_cols, b_hv, b_eps, b_bsT = Buf(), Buf(), Buf(), Buf(), Buf()
    b_wring = [Buf(f"wr{i}") for i in range(3)]
    b_stt = [Buf() for _ in range(4)]
    b_ystage = [Buf(), Buf()]
    b_bank = [Buf(f"bank{i}") for i in range(8)]
    bank_ctr = [0]

    def next_bank():
        i = bank_ctr[0] % 8
        bank_ctr[0] += 1
        return banks[i], b_bank[i]

    def Acol(s):
        return cols[:, (2 * s) * 16:(2 * s + 1) * 16]

    def Bcol(s):
        return cols[:, (2 * s + 1) * 16:(2 * s + 2) * 16]

    def gprecol(s):
        return cols[:, (6 + s) * 16:(7 + s) * 16]

    goutcol = cols[:, 9 * 16:10 * 16]

    d_ct = P.dsem("ct")
    d_w = [P.dsem(f"w{i}") for i in range(3)]
    d_ld = [P.dsem(f"ld{i}") for i in range(8)]
    d_st = [P.dsem(f"st{i}") for i in range(4)]
    d_cs = P.dsem("cs")
    d_ys = [P.dsem("ys0"), P.dsem("ys1")]

    plan = []

    def plan_ada():
        for v in range(9):
            for nb in range(4):
                c0 = v * D + nb * 512
                plan.append((f"ada{v}_{nb}",
                             lambda s, c0=c0: (s[:, :].rearrange("p (k c) -> p k c", k=16),
                                               wada_d[:, c0:c0 + 512].rearrange("(k p) c -> p k c", p=128))))

    def plan_ffn(f):
        wg, wd = wgu_d[f], wdn_d[f]
        ngroups = 3 if f == 0 else 2
        for g in range(ngroups):
            for blk in range(22):
                j0 = 2 * blk
                ncol = 256 if blk < 21 else 128
                plan.append((f"gu{f}_{g}_{blk}",
                             lambda s, j0=j0, ncol=ncol, wg=wg: (
                                 s[:, :].rearrange("p (two k c) -> p two k c", two=2, k=16)[:, :, :, 0:ncol],
                                 wg[:, :].rearrange("(k p) (two c) -> p two k c", p=128, two=2)[:, :, :, j0 * 128:j0 * 128 + ncol])))
            for cb in range(4):
                for jb in range(3):
                    j0 = 16 * jb
                    nj = min(16, NJ - j0)
                    plan.append((f"dn{f}_{g}_{cb}_{jb}",
                                 lambda s, j0=j0, nj=nj, cb=cb, wd=wd: (
                                     s[:, :].rearrange("p (j c) -> p j c", j=16)[:, 0:nj, :],
                                     wd[j0 * 128:(j0 + nj) * 128, cb * 512:(cb + 1) * 512].rearrange("(j p) c -> p j c", p=128))))

    def plan_mixer():
        for gi in range(5):
            blks = [2, 3, 4, 5] if gi == 0 else list(range(10))
            for blk in blks:
                plan.append((f"in{gi}_{blk}",
                             lambda s, blk=blk: (s[:, :].rearrange("p (k c) -> p k c", k=16),
                                                 win_d[:, blk * 512:(blk + 1) * 512].rearrange("(k p) c -> p k c", p=128))))
            if gi > 0:
                for cb in range(4):
                    plan.append((f"wo{gi}_{cb}",
                                 lambda s, cb=cb: (s[:, :].rearrange("p (k c) -> p k c", k=16),
                                                   wout_d[:, cb * 512:(cb + 1) * 512].rearrange("(k p) c -> p k c", p=128))))

    if not skip_pro:
        plan_ada()
    if lvl >= 1 and stop != 1 and not os.environ.get("SKIP_F1"):
        plan_ffn(0)
    if lvl >= 2:
        plan_mixer()
    if lvl >= 3:
        plan_ffn(1)
    ws_state = {"issued": 0, "got": 0}

    def ws_issue_upto(n):
        while ws_state["issued"] < min(n, len(plan)):
            i = ws_state["issued"]
            s = i % 3
            o, src = plan[i][1](wring[s])
            P.dma(POOL, d_w[s], o, src, writes=[b_wring[s]])
            ws_state["issued"] += 1

    def ws_get(tag):
        i = ws_state["got"]
        assert plan[i][0] == tag, (plan[i][0], tag)
        ws_issue_upto(i + 1)
        ws_state["got"] += 1
        return wring[i % 3], b_wring[i % 3]

    def ws_done():
        ws_issue_upto(ws_state["got"] + 2)

    P.dma(SP, P.dsem("m"), ident[:], ident_d[:, :], writes=[b_ident])
    P.dma(SP, P.dsem("m"), hv[:], hv_d[:, :], writes=[b_hv])
    P.dma(SP, P.dsem("m"), bsT[:], bsT_d[:, :], writes=[b_bsT])
    P.dma(SP, P.dsem("m"), cols[:, 96:144], gpre_d[:, :], pwrites=[b_cols])
    P.dma(SP, P.dsem("m"), cols[:, 144:160], gout_d[:, :], pwrites=[b_cols])
    P.op(DVE, lambda: V.memset(epst[:], EPS), writes=[b_eps])
    ws_issue_upto(2)

    def rstd_from_sumsq(si, n):
        P.op(ACT, lambda: A.activation(out=stt[si][:, 1:2], in_=stt[si][:, 0:1], func=AF.Sqrt,
                                       scale=1.0 / n, bias=epst[:, 0:1]), reads=[b_eps], writes=[b_stt[si]])
        P.op(DVE, lambda: V.reciprocal(out=stt[si][:, 2:3], in_=stt[si][:, 1:2]), writes=[b_stt[si]])

    def transpose_tile(src, b_src, dst_fn, b_dst, scol, bcol):
        bk = [next_bank() for _ in range(4)]
        for q in range(4):
            P.pe_group([lambda kc=kc, q=q: T.transpose(out=bk[q][0][:, (kc % 4) * 128:(kc % 4 + 1) * 128],
                                                       in_=src[:, kc * 128:(kc + 1) * 128], identity=ident[:])
                        for kc in range(4 * q, 4 * q + 4)], reads=[b_src, b_ident], writes=[bk[q][1]])
        DT = int(os.environ.get("DBG_T", "2"))
        for kc in range(16):
            q = kc // 4
            pin = bk[q][0][:, (kc % 4) * 128:(kc % 4 + 1) * 128]
            if DT == 0 or (DT == 1 and kc % 2 == 1) or (DT == 3 and kc % 2 == 0):
                continue
            if True:
                if bcol is None:
                    P.op(DVE, lambda: V.tensor_scalar(out=dst_fn(kc), in0=pin, scalar1=scol[:, kc:kc + 1], scalar2=None,
                                                      op0=ALU.mult), reads=[bk[q][1], b_cols], pwrites=[b_dst])
                else:
                    P.op(DVE, lambda: V.tensor_scalar(out=dst_fn(kc), in0=pin, scalar1=scol[:, kc:kc + 1],
                                                      scalar2=bcol[:, kc:kc + 1], op0=ALU.mult, op1=ALU.add),
                         reads=[bk[q][1], b_cols], pwrites=[b_dst])
            else:
                if bcol is None:
                    P.op(ACT, lambda: A.activation(out=dst_fn(kc), in_=pin, func=AF.Copy, scale=scol[:, kc:kc + 1]),
                         reads=[bk[q][1], b_cols], pwrites=[b_dst])
                else:
                    P.op(ACT, lambda: A.activation(out=dst_fn(kc), in_=pin, func=AF.Identity, scale=scol[:, kc:kc + 1],
                                                   bias=bcol[:, kc:kc + 1]), reads=[bk[q][1], b_cols], pwrites=[b_dst])

    def stage_a(xs, r0, ntiles, Tt, b_Tt, hT, b_hT, s):
        for i in range(ntiles):
            sl = i % 2
            xt, b_xt = Tt[sl], b_Tt[sl]
            xn, b_xn = Tt[2 + sl], b_Tt[2 + sl]
            import os
            DA = int(os.environ.get("DBG_A", "9"))
            if i >= int(os.environ.get("DBG_NT", "99")):
                continue
            P.dma(SP, d_ld[sl], xt, xs[r0 + i * 128:r0 + (i + 1) * 128, :], writes=[b_xt])
            P.op(DVE, lambda: V.scalar_tensor_tensor(out=junk[:], in0=xt, scalar=1.0, in1=xt, op0=ALU.mult, op1=ALU.mult,
                                                     accum_out=stt[sl][:, 0:1]), reads=[b_xt], writes=[b_stt[sl]])
            if DA < 2:
                continue
            rstd_from_sumsq(sl, D)
            if DA < 3:
                continue
            P.op(ACT, lambda: A.activation(out=xn, in_=xt, func=AF.Copy, scale=stt[sl][:, 2:3]),
                 reads=[b_xt, b_stt[sl]], writes=[b_xn])
            if DA < 4:
                continue
            transpose_tile(xn, b_xn, lambda kc, i=i: hT[:, kc, i * 128:(i + 1) * 128], b_hT[i], Acol(s), Bcol(s))

    def stage_d(xs, rs0, xd, rd0, ntiles, Tt, b_Tt, ct, b_ct, nsl=2):
        def loads(i):
            sl = i % nsl
            P.dma(SP, d_ld[sl], Tt[sl], Yd[i * 128:(i + 1) * 128, :], writes=[b_Tt[sl]])
            P.dma(SP, d_ld[nsl + sl], Tt[nsl + sl], xs[rs0 + i * 128:rs0 + (i + 1) * 128, :], writes=[b_Tt[nsl + sl]])

        for i in range(min(nsl - 1, ntiles)):
            loads(i)
        for i in range(ntiles):
            if i + nsl - 1 < ntiles:
                loads(i + nsl - 1)
            sl = i % nsl
            yt, b_yt = Tt[sl], b_Tt[sl]
            xt, b_xt = Tt[nsl + sl], b_Tt[nsl + sl]
            P.op(DVE, lambda: V.scalar_tensor_tensor(out=junk[:], in0=yt, scalar=1.0, in1=yt, op0=ALU.mult, op1=ALU.mult,
                                                     accum_out=stt[sl][:, 0:1]), reads=[b_yt], writes=[b_stt[sl]])
            rstd_from_sumsq(sl, D)
            P.op(DVE, lambda: V.scalar_tensor_tensor(out=yt, in0=yt, scalar=stt[sl][:, 2:3], in1=ct, op0=ALU.mult,
                                                     op1=ALU.mult), reads=[b_stt[sl], b_ct], writes=[b_yt])
            P.op(DVE, lambda: V.tensor_tensor(out=xt, in0=yt, in1=xt, op=ALU.add), reads=[b_yt], writes=[b_xt])
            tok = P.dma(SP, d_st[sl], xd[rd0 + i * 128:rd0 + (i + 1) * 128, :], xt, reads=[b_xt])
            P.pending_stores.append(tok)

    evac_ctr = [0]

    def store_y_block(bank, b_bk, i, cb):
        k = evac_ctr[0] % 2
        evac_ctr[0] += 1
        if k == 0:
            P.op(ACT, lambda: A.copy(out=ystage[k][:], in_=bank[:, :]), reads=[b_bk], writes=[b_ystage[k]])
        else:
            P.op(DVE, lambda: V.tensor_copy(out=ystage[k][:], in_=bank[:, :]), reads=[b_bk], writes=[b_ystage[k]])
        tok = P.dma(SP, d_ys[k], Yd[i * 128:(i + 1) * 128, cb * 512:(cb + 1) * 512], ystage[k][:], reads=[b_ystage[k]])
        P.pending_stores.append(tok)

    def load_ct(ct, b_ct, s):
        P.dma(SP, d_ct, ct, Cs[s * 128:(s + 1) * 128, :], writes=[b_ct])

    if skip_pro:
        P.op(DVE, lambda: V.memset(cols[:, 0:96], 1.0), pwrites=[b_cols])
    with ExitStack() as sc:
        e2 = sc.enter_context
        cT = e2(nc.sbuf_tensor("cT_sb", [128, 16], F32))
        cact = e2(nc.sbuf_tensor("cact", [128, 16], F32))
        crep = e2(nc.sbuf_tensor("crep", [128, 16, 128], BF16))
        modv = e2(nc.sbuf_tensor("modv", [128, D], F32))
        badab = e2(nc.sbuf_tensor("badab", [128, D], F32))
        gpostb = e2(nc.sbuf_tensor("gpostb", [128, D], F32))
        tmpd = e2(nc.sbuf_tensor("tmpd", [128, D], F32))
        cres = e2(nc.sbuf_tensor("cres", [128, D], F32))
        dcol = e2(nc.sbuf_tensor("dcol", [128, 16], F32))
        b_cT, b_cact, b_crep, b_modv, b_badab, b_gpostb, b_tmpd, b_cres, b_dcol = [Buf() for _ in range(9)]
        P.dma(SP, P.dsem("m"), cT[:], cT_d[:, :], writes=[b_cT])
        P.op(ACT, lambda: A.activation(out=cact[:], in_=cT[:], func=AF.Silu), reads=[b_cT], writes=[b_cact])
        P.op(DVE, lambda: V.tensor_copy(out=crep[:], in_=cact[:].unsqueeze(2).to_broadcast([128, 16, 128])),
             reads=[b_cact], writes=[b_crep])
        for v in range(0 if skip_pro else 9):
            s, role = v // 3, v % 3
            P.dma(SP, d_ld[0], badab[:], bada_d[0:1, v * D:(v + 1) * D].partition_broadcast(128), writes=[b_badab])
            if role == 2:
                P.dma(SP, d_ld[1], gpostb[:], gpost_d[s:s + 1, :].partition_broadcast(128), writes=[b_gpostb])
            for nb in range(4):
                slot, b_slot = ws_get(f"ada{v}_{nb}")
                sv = slot[:, :].rearrange("p (k c) -> p k c", k=16)
                bk, b_bk = next_bank()
                P.pe_group([lambda kc=kc: T.matmul(out=bk[:, :], lhsT=crep[:, kc, :], rhs=sv[:, kc, :],
                                                   start=(kc == 0), stop=(kc == 15)) for kc in range(16)],
                           reads=[b_crep, b_slot], writes=[b_bk])
                ws_done()
                P.op(DVE, lambda: V.tensor_tensor(out=modv[:, nb * 512:(nb + 1) * 512], in0=bk[:, :],
                                                  in1=badab[:, nb * 512:(nb + 1) * 512], op=ALU.add),
                     reads=[b_bk, b_badab], pwrites=[b_modv])
            if role in (0, 1):
                P.op(DVE, lambda: V.tensor_tensor(out=tmpd[:].rearrange("p (a b) -> p a b", a=16),
                                                  in0=modv[:].rearrange("p (a b) -> p a b", a=16),
                                                  in1=ident[:].unsqueeze(1).to_broadcast([128, 16, 128]), op=ALU.mult),
                     reads=[b_modv, b_ident], writes=[b_tmpd])
                if role == 0:
                    P.op(DVE, lambda: V.tensor_reduce(out=Bcol(s), in_=tmpd[:].rearrange("p (a b) -> p a b", a=16),
                                                      axis=AX.X, op=ALU.add), reads=[b_tmpd], pwrites=[b_cols])
                else:
                    P.op(DVE, lambda: V.tensor_reduce(out=dcol[:], in_=tmpd[:].rearrange("p (a b) -> p a b", a=16),
                                                      axis=AX.X, op=ALU.add), reads=[b_tmpd], writes=[b_dcol])
                    P.op(DVE, lambda: V.scalar_tensor_tensor(out=Acol(s), in0=dcol[:], scalar=1.0, in1=gprecol(s),
                                                             op0=ALU.add, op1=ALU.mult), reads=[b_dcol, b_cols], pwrites=[b_cols])
            else:
                gsc = 1.0 if s == 1 else 0.5
                P.op(DVE, lambda: V.scalar_tensor_tensor(out=cres[:], in0=modv[:], scalar=gsc, in1=gpostb[:],
                                                         op0=ALU.mult, op1=ALU.mult),
                     reads=[b_modv, b_gpostb], writes=[b_cres])
                tok = P.dma(SP, d_cs, Cs[s * 128:(s + 1) * 128, :], cres[:], reads=[b_cres])
                P.pending_stores.append(tok)
        if dbg:
            P.pending_stores.append(P.dma(SP, P.dsem("dbg"), dbg_cols[:, :], cols[:], reads=[b_cols]))
        P.barrier()
        P.pending_stores = []

    def ffn_phase(f, xs, xd, groups, s):
        with ExitStack() as sc:
            e2 = sc.enter_context
            hTt = e2(nc.sbuf_tensor(f"hT{f}", [128, 16, 1024], BF16))
            RA = e2(nc.sbuf_tensor(f"RA{f}", [128, 22016], F32))
            sg = [e2(nc.sbuf_tensor(f"sg{f}_{i}", [128, 512], F32)) for i in range(2)]
            b_sg = [Buf(), Buf()]
            aT = RA[:, :].bitcast(BF16).rearrange("p (j t) -> p j t", j=NJ)
            Tt = [RA[:, i * D:(i + 1) * D] for i in range(8)]
            ct = RA[:, 8 * D:9 * D]
            b_Tt = [Buf() for _ in range(8)]
            b_ct = Buf()
            b_hT = [Buf() for _ in range(8)]
            b_aT = [Buf() for _ in range(NJ)]
            sgc = 0
            for g, (rs0, rd0, ntok) in enumerate(groups):
                nt = ntok // 128
                ntb = ntok // 512
                stage_a(xs, rs0, nt, Tt, b_Tt, hTt, b_hT, s)
                P.wait_bufs((DVE,), b_Tt + [b_ct])
                if stop == 1:
                    P.pending_stores.append(P.dma(SP, P.dsem("dbg"), dbg_hT[:, :], hTt[:].rearrange("p k t -> p (k t)"), reads=b_hT))
                    P.barrier()
                    P.stopped = True
                    return
                for blk in range(22):
                    slot, b_slot = ws_get(f"gu{f}_{g}_{blk}")
                    sv = slot[:, :].rearrange("p (two k c) -> p two k c", two=2, k=16)
                    for jj in range(2):
                        j = 2 * blk + jj
                        if j >= NJ:
                            continue
                        for tb in range(ntb):
                            bg, b_bg = next_bank()
                            bu, b_bu = next_bank()
                            hb = [b_hT[4 * tb + q] for q in range(4)]
                            P.pe_group([lambda kc=kc: T.matmul(out=bg[:, :], lhsT=sv[:, 0, kc, jj * 128:(jj + 1) * 128],
                                                               rhs=hTt[:, kc, tb * 512:(tb + 1) * 512],
                                                               start=(kc == 0), stop=(kc == 15)) for kc in range(16)],
                                       reads=[b_slot] + hb, writes=[b_bg])
                            P.pe_group([lambda kc=kc: T.matmul(out=bu[:, :], lhsT=sv[:, 1, kc, jj * 128:(jj + 1) * 128],
                                                               rhs=hTt[:, kc, tb * 512:(tb + 1) * 512],
                                                               start=(kc == 0), stop=(kc == 15)) for kc in range(16)],
                                       reads=[b_slot] + hb, writes=[b_bu])
                            k = sgc % 2
                            sgc += 1
                            P.op(ACT, lambda: A.activation(out=sg[k][:], in_=bg[:, :], func=AF.Silu),
                                 reads=[b_bg], writes=[b_sg[k]])
                            P.op(DVE, lambda: V.tensor_tensor(out=aT[:, j, tb * 512:(tb + 1) * 512], in0=sg[k][:],
                                                              in1=bu[:, :], op=ALU.mult),
                                 reads=[b_sg[k], b_bu], pwrites=[b_aT[j]])
                    ws_done()
                if stop == 2:
                    P.barrier()
                    P.pending_stores.append(P.dma(SP, P.dsem("dbg"), dbg_aT[:, :], RA[:, :].bitcast(BF16), reads=b_aT))
                    P.barrier()
                    P.stopped = True
                    return
                for cb in range(4):
                    for jb in range(3):
                        slot, b_slot = ws_get(f"dn{f}_{g}_{cb}_{jb}")
                        sv = slot[:, :].rearrange("p (j c) -> p j c", j=16)
                        j0 = 16 * jb
                        nj = min(16, NJ - j0)
                        for i in range(nt):
                            P.pe_group([lambda j=j: T.matmul(out=banks[i][:, :], lhsT=aT[:, j, i * 128:(i + 1) * 128],
                                                             rhs=sv[:, j - j0, :], start=(j == 0), stop=(j == NJ - 1))
                                        for j in range(j0, j0 + nj)],
                                       reads=[b_slot] + [b_aT[j] for j in range(j0, j0 + nj)], writes=[b_bank[i]])
                            if jb == 2:
                                store_y_block(banks[i], b_bank[i], i, cb)
                        ws_done()
                P.sp_sync()
                if stop == 3:
                    P.stopped = True
                    return
                load_ct(ct, b_ct, s)
                stage_d(xs, rs0, xd, rd0, nt, Tt, b_Tt, ct, b_ct, nsl=4)
                if stop == 4:
                    P.barrier(b_Tt)
                    P.stopped = True
                    return
            P.barrier(b_Tt)
            P.pending_stores = []

    P.stopped = False
    SKIP_F1 = bool(os.environ.get("SKIP_F1"))
    if lvl >= 1 and not SKIP_F1:
        ffn_phase(0, x_d, X1, [(0, 0, 1024), (1024, 1024, 1024), (2048, 2048, 512)], 0)

    def mixer_phase():
      with ExitStack() as sc:
        e2 = sc.enter_context
        hTm = e2(nc.sbuf_tensor("hTm", [128, 16, 512], BF16))
        qT = e2(nc.sbuf_tensor("qT", [128, 8, 512], BF16))
        kT = [e2(nc.sbuf_tensor(f"kT{i}", [128, 8, 512], BF16)) for i in range(2)]
        vaug = [e2(nc.sbuf_tensor(f"vaug{i}", [128, 4, NH, VW], BF16)) for i in range(2)]
        RM = e2(nc.sbuf_tensor("RM", [128, 5 * D], F32))
        out_a = e2(nc.sbuf_tensor("out_a", [128, 1024], F32))
        out_b = e2(nc.sbuf_tensor("out_b", [128, 1024], F32))
        vln = e2(nc.sbuf_tensor("vln", [128, 1024], BF16))
        sc_sb = [e2(nc.sbuf_tensor(f"sc_sb{i}", [128, 640], F32)) for i in range(2)]
        pT = [e2(nc.sbuf_tensor(f"pT{i}", [128, 640], BF16)) for i in range(2)]
        biasb = [e2(nc.sbuf_tensor(f"biasb{i}", [128, 640], F32)) for i in range(2)]
        lnb = e2(nc.sbuf_tensor("lnb", [128, 2048], F32))
        wsTf = e2(nc.sbuf_tensor("wsTf", [128, 1024], F32))
        wsT = e2(nc.sbuf_tensor("wsT_bf", [128, 8, 128], BF16))
        b_hTm = [Buf() for _ in range(4)]
        b_qT, b_kT, b_vaug = Buf(), [Buf(), Buf()], [Buf(), Buf()]
        b_out_a, b_out_b, b_vln = Buf(), Buf(), Buf()
        b_sc, b_pT, b_biasb = [Buf(), Buf()], [Buf(), Buf()], [Buf(), Buf()]
        b_lnb, b_wsTf, b_wsT = Buf(), Buf(), Buf()
        Tt = [RM[:, i * D:(i + 1) * D] for i in range(4)]
        ct = RM[:, 4 * D:5 * D]
        b_Tt = [Buf() for _ in range(4)]
        b_ct = Buf()
        u_g = RM[:, 0:2 * D].rearrange("p (t c) -> p t c", t=4)
        vg = RM[:, 2 * D:4 * D].rearrange("p (t c) -> p t c", t=4)
        merged = RM[:, 4 * D:5 * D]
        b_ug, b_vg = [Buf() for _ in range(4)], [Buf() for _ in range(4)]
        b_merged = Buf()
        d_bias = [P.dsem("bias0"), P.dsem("bias1")]

        P.dma(SP, P.dsem("m"), lnb[:, 0:1024], lnv_d[0:1, :].partition_broadcast(128), pwrites=[b_lnb])
        P.dma(SP, P.dsem("m"), lnb[:, 1024:2048], lnv_d[1:2, :].partition_broadcast(128), pwrites=[b_lnb])
        P.dma(SP, P.dsem("m"), wsTf[:], wsT_d[:, :], writes=[b_wsTf])
        P.op(DVE, lambda: V.tensor_copy(out=wsT[:].rearrange("p g t -> p (g t)"), in_=wsTf[:]), reads=[b_wsTf], writes=[b_wsT])
        P.op(DVE, lambda: V.memset(wsT[64:128, :, 0:64], 0.0), writes=[b_wsT])

        ev = [0]

        def copy_evac(out, in_, reads, writes):
            ev[0] += 1
            if ev[0] % 2 == 0:
                P.op(ACT, lambda: A.copy(out=out, in_=in_), reads=reads, pwrites=writes)
            else:
                P.op(DVE, lambda: V.tensor_copy(out=out, in_=in_), reads=reads, pwrites=writes)

        for gi in range(5):
            ks = gi % 2
            stage_a(X1, 512 * gi, 4, Tt, b_Tt, hTm, b_hTm, 1)
            P.wait_bufs((ACT, DVE), b_Tt + [b_ct])
            P.op(DVE, lambda: V.memset(vaug[ks][:, :, :, 64:65], 1.0), writes=[b_vaug[ks]])
            if gi == 0:
                P.op(DVE, lambda: V.tensor_scalar(out=vaug[ks][:, :, :, 64:65], in0=vaug[ks][:, :, :, 64:65],
                                                  scalar1=hv[:, 0:1], scalar2=None, op0=ALU.mult),
                     reads=[b_hv], writes=[b_vaug[ks]])
            blks = [2, 3, 4, 5] if gi == 0 else list(range(10))
            for blk in blks:
                slot, b_slot = ws_get(f"in{gi}_{blk}")
                sv = slot[:, :].rearrange("p (k c) -> p k c", k=16)
                if blk < 4:
                    dst, b_dst = (qT, b_qT) if blk < 2 else (kT[ks], b_kT[ks])
                    for c in range(4):
                        bk, b_bk = next_bank()
                        P.pe_group([lambda kc=kc: T.matmul(out=bk[:, :], lhsT=sv[:, kc, c * 128:(c + 1) * 128],
                                                           rhs=hTm[:, kc, :], start=(kc == 0), stop=(kc == 15))
                                    for kc in range(16)], reads=[b_slot] + b_hTm, writes=[b_bk])
                        copy_evac(dst[:, (blk % 2) * 4 + c, :], bk[:, :], [b_bk], [b_dst])
                else:
                    for i in range(4):
                        bk, b_bk = next_bank()
                        P.pe_group([lambda kc=kc: T.matmul(out=bk[:, :], lhsT=hTm[:, kc, i * 128:(i + 1) * 128],
                                                           rhs=sv[:, kc, :], start=(kc == 0), stop=(kc == 15))
                                    for kc in range(16)], reads=[b_slot, b_hTm[i]], writes=[b_bk])
                        if blk < 6:
                            hb = blk - 4
                            o = vaug[ks][:, i, hb * 8:(hb + 1) * 8, 0:64]
                            src = bk[:, :].rearrange("p (h d) -> p h d", h=8)
                            if gi == 0:
                                P.op(DVE, lambda: V.tensor_scalar(out=o, in0=src, scalar1=hv[:, 0:1], scalar2=None,
                                                                  op0=ALU.mult), reads=[b_bk, b_hv], pwrites=[b_vaug[ks]])
                            else:
                                copy_evac(o, src, [b_bk], [b_vaug[ks]])
                        elif blk < 8:
                            P.op(ACT, lambda: A.activation(out=u_g[:, i, (blk - 6) * 512:(blk - 5) * 512], in_=bk[:, :],
                                                           func=AF.Gelu), reads=[b_bk], pwrites=[b_ug[i]])
                        else:
                            P.op(ACT, lambda: A.activation(out=vg[:, i, (blk - 8) * 512:(blk - 7) * 512], in_=bk[:, :],
                                                           func=AF.Gelu), reads=[b_bk], pwrites=[b_vg[i]])
                ws_done()
            DM = int(os.environ.get("DBG_M", "99"))
            if gi == 0:
                if DM == 1:
                    P.barrier()
                    return
                continue
            if DM == 2:
                P.barrier()
                return
            for t in range(4):
                def scores(h):
                    k = h % 2
                    hp, hc = h % 2, h // 2
                    P.dma(SP, d_bias[k], biasb[k][:], biasT_d[h, :, :], writes=[b_biasb[k]])
                    bA, b_bA = next_bank()
                    bB, b_bB = next_bank()
                    for c in range(5):
                        kt = 4 * gi + t - 4 + c
                        ksl = (kt // 4) % 2
                        kc0 = (kt % 4) * 128
                        o = bA[:, c * 128:(c + 1) * 128] if c < 4 else bB[:, 0:128]
                        P.pe_group([lambda: T.matmul(out=o, lhsT=kT[ksl][hp * 64:(hp + 1) * 64, hc, kc0:kc0 + 128],
                                                     rhs=qT[hp * 64:(hp + 1) * 64, hc, t * 128:(t + 1) * 128],
                                                     start=True, stop=True)],
                                   reads=[b_kT[ksl], b_qT], pwrites=[b_bA if c < 4 else b_bB])
                    P.op(DVE, lambda: V.scalar_tensor_tensor(out=sc_sb[k][:, 0:512], in0=bA[:, :], scalar=0.125,
                                                             in1=biasb[k][:, 0:512], op0=ALU.mult, op1=ALU.add),
                         reads=[b_bA, b_biasb[k]], pwrites=[b_sc[k]])
                    P.op(DVE, lambda: V.scalar_tensor_tensor(out=sc_sb[k][:, 512:640], in0=bB[:, 0:128], scalar=0.125,
                                                             in1=biasb[k][:, 512:640], op0=ALU.mult, op1=ALU.add),
                         reads=[b_bB, b_biasb[k]], pwrites=[b_sc[k]])
                    P.op(ACT, lambda: A.activation(out=pT[k][:], in_=sc_sb[k][:], func=AF.Exp),
                         reads=[b_sc[k]], writes=[b_pT[k]])

                def pv(h):
                    k = h % 2
                    bO, b_bO = next_bank()
                    fs = []
                    rd = [b_pT[k]]
                    for c in range(5):
                        kt = 4 * gi + t - 4 + c
                        ksl = (kt // 4) % 2
                        rd.append(b_vaug[ksl])
                        fs.append(lambda c=c, kt=kt, ksl=ksl: T.matmul(out=bO[:, 0:VW], lhsT=pT[k][:, c * 128:(c + 1) * 128],
                                                                      rhs=vaug[ksl][:, kt % 4, h, :],
                                                                      start=(c == 0), stop=(c == 4)))
                    P.pe_group(fs, reads=rd, writes=[b_bO])
                    P.op(DVE, lambda: V.reciprocal(out=stt[2][:, 0:1], in_=bO[:, 64:65]), reads=[b_bO], writes=[b_stt[2]])
                    P.op(DVE, lambda: V.tensor_scalar(out=out_a[:, h * 64:(h + 1) * 64], in0=bO[:, 0:64],
                                                      scalar1=stt[2][:, 0:1], scalar2=None, op0=ALU.mult),
                         reads=[b_bO, b_stt[2]], pwrites=[b_out_a])

                for h in range(NH + 1):
                    if h < NH:
                        scores(h)
                    if h > 0:
                        pv(h - 1)
                if DM == 3:
                    P.barrier()
                    return
                vt = vg[:, t, :]
                P.op(DVE, lambda: V.tensor_reduce(out=stt[3][:, 0:1], in_=vt, axis=AX.X, op=ALU.add),
                     reads=[b_vg[t]], writes=[b_stt[3]])
                P.op(DVE, lambda: V.tensor_scalar(out=stt[3][:, 1:2], in0=stt[3][:, 0:1], scalar1=-1.0 / 1024, scalar2=None,
                                                  op0=ALU.mult), writes=[b_stt[3]])
                P.op(DVE, lambda: V.tensor_scalar(out=vt, in0=vt, scalar1=stt[3][:, 1:2], scalar2=None, op0=ALU.add),
                     reads=[b_stt[3]], writes=[b_vg[t]])
                P.op(DVE, lambda: V.scalar_tensor_tensor(out=junk[:, 0:1024], in0=vt, scalar=1.0, in1=vt, op0=ALU.mult,
                                                         op1=ALU.mult, accum_out=stt[0][:, 0:1]),
                     reads=[b_vg[t]], writes=[b_stt[0]])
                rstd_from_sumsq(0, 1024)
                P.op(DVE, lambda: V.scalar_tensor_tensor(out=vt, in0=vt, scalar=stt[0][:, 2:3], in1=lnb[:, 0:1024],
                                                         op0=ALU.mult, op1=ALU.mult),
                     reads=[b_stt[0], b_lnb], writes=[b_vg[t]])
                P.op(DVE, lambda: V.tensor_tensor(out=vln[:], in0=vt, in1=lnb[:, 1024:2048], op=ALU.add),
                     reads=[b_vg[t], b_lnb], writes=[b_vln])
                bz = [next_bank(), next_bank()]
                for g in range(8):
                    zb, b_zb = bz[g // 4]
                    P.pe_group([lambda: T.matmul(out=zb[:, (g % 4) * 128:(g % 4 + 1) * 128], lhsT=wsT[:, g, :],
                                                 rhs=vln[:, g * 128:(g + 1) * 128], start=True, stop=True)],
                               reads=[b_wsT, b_vln], pwrites=[b_zb])
                for g in range(8):
                    zb, b_zb = bz[g // 4]
                    P.op(DVE, lambda: V.scalar_tensor_tensor(out=out_b[:, g * 128:(g + 1) * 128],
                                                             in0=zb[:, (g % 4) * 128:(g % 4 + 1) * 128],
                                                             scalar=bsT[:, g:g + 1], in1=u_g[:, t, g * 128:(g + 1) * 128],
                                                             op0=ALU.add, op1=ALU.mult),
                         reads=[b_zb, b_bsT, b_ug[t]], pwrites=[b_out_b])
                if DM == 4:
                    P.barrier()
                    return
                P.op(DVE, lambda: V.scalar_tensor_tensor(out=junk[:, 0:1024], in0=out_a[:], scalar=1.0, in1=out_a[:],
                                                         op0=ALU.mult, op1=ALU.mult, accum_out=stt[0][:, 0:1]),
                     reads=[b_out_a], writes=[b_stt[0]])
                rstd_from_sumsq(0, 1024)
                P.op(DVE, lambda: V.scalar_tensor_tensor(out=junk[:, 1024:2048], in0=out_b[:], scalar=1.0, in1=out_b[:],
                                                         op0=ALU.mult, op1=ALU.mult, accum_out=stt[1][:, 0:1]),
                     reads=[b_out_b], writes=[b_stt[1]])
                rstd_from_sumsq(1, 1024)
                P.op(ACT, lambda: A.activation(out=merged[:, 0:1024], in_=out_a[:], func=AF.Copy, scale=stt[0][:, 2:3]),
                     reads=[b_out_a, b_stt[0]], pwrites=[b_merged])
                P.op(ACT, lambda: A.activation(out=merged[:, 1024:2048], in_=out_b[:], func=AF.Copy, scale=stt[1][:, 2:3]),
                     reads=[b_out_b, b_stt[1]], pwrites=[b_merged])
                transpose_tile(merged, b_merged, lambda kc, t=t: hTm[:, kc, t * 128:(t + 1) * 128], b_hTm[t], goutcol, None)
                if DM == 5:
                    P.barrier()
                    return
            for cb in range(4):
                slot, b_slot = ws_get(f"wo{gi}_{cb}")
                sv = slot[:, :].rearrange("p (k c) -> p k c", k=16)
                for i in range(4):
                    bk, b_bk = next_bank()
                    P.pe_group([lambda kc=kc: T.matmul(out=bk[:, :], lhsT=hTm[:, kc, i * 128:(i + 1) * 128],
                                                       rhs=sv[:, kc, :], start=(kc == 0), stop=(kc == 15))
                                for kc in range(16)], reads=[b_slot, b_hTm[i]], writes=[b_bk])
                    store_y_block(bk, b_bk, i, cb)
                ws_done()
            P.sp_sync()
            load_ct(ct, b_ct, 1)
            stage_d(X1, 512 * gi, X2, 512 * (gi - 1), 4, Tt, b_Tt, ct, b_ct)
            if DM == 6:
                P.barrier(b_Tt)
                return
        P.barrier(b_Tt)
        P.pending_stores = []

    if lvl >= 2 and not P.stopped:
        mixer_phase()
    if lvl >= 3 and not P.stopped:
        ffn_phase(1, X2, out_d, [(0, 0, 1024), (1024, 1024, 1024)], 2)
    P.barrier(b_wring)
    P.st.close()
    return nc


_CACHE = {}


def _host_layout(inputs, core):
    b, half = core // 2, core % 2
    f32 = np.float32
    x = np.asarray(inputs["x"], dtype=f32)
    xs = np.zeros((NEXT, D), f32)
    xs[HALO:] = x[b, half * NMAIN:(half + 1) * NMAIN]
    if half == 1:
        xs[:HALO] = x[b, NMAIN - HALO:NMAIN]
    return {
        "x": xs,
        "cT": np.ascontiguousarray(np.asarray(inputs["c"], f32)[b].reshape(16, 128).T),
        "hv": np.full((128, 1), float(half), f32),
    }


def _shared_layout(inputs):
    f32 = np.float32
    g = lambda k: np.asarray(inputs[k], dtype=f32)[0]
    col = lambda v: np.ascontiguousarray(v.reshape(16, 128).T)
    gpre = np.concatenate([col(g("ffn1_pre_g")), col(g("mix_pre_g")), col(g("ffn2_pre_g"))], axis=1)
    gpost = np.stack([g("ffn1_post_g"), g("mix_post_g"), g("ffn2_post_g")])
    lnv = np.stack([g("ln_v_g"), g("ln_v_b")])
    gout = col(np.concatenate([g("g_out_a"), g("g_out_b")]))
    bsT = np.ascontiguousarray(g("b_s").T)
    wsT = np.ascontiguousarray(g("w_s").transpose(2, 0, 1)).reshape(128, 1024)
    rel_ext = np.concatenate([g("rel_bias"), np.full((NH, 1), -30000.0, f32)], axis=1)
    kj = np.arange(640)[:, None]
    qi = np.arange(128)[None, :]
    idx = np.clip(qi - kj + 512, -128, 128) + 128
    masked = ((kj < 64) & (qi >= 64)) | ((kj >= 576) & (qi < 64))
    idx = np.where(masked, 257, idx)
    bt = rel_ext[:, idx]
    biasT = np.ascontiguousarray(bt.reshape(NH, 5, 128, 128).transpose(0, 2, 1, 3)).reshape(NH, 128, 640)
    return {
        "w_ada": g("w_ada"), "b_ada": g("b_ada")[None, :],
        "w_gu1": g("ffn1_w_gu"), "w_dn1": g("ffn1_w_down"), "w_gu2": g("ffn2_w_gu"), "w_dn2": g("ffn2_w_down"),
        "w_in": g("w_in"), "w_out": g("w_out"),
        "gpre": np.ascontiguousarray(gpre), "gpost": gpost, "lnv": lnv, "gout": gout, "bsT": bsT, "wsT": wsT,
        "biasT": biasT, "ident": np.eye(128, dtype=f32),
    }


def kernel(**inputs):
    if "nc" not in _CACHE:
        _CACHE["nc"] = build_program()
    nc = _CACHE["nc"]
    shared = _shared_layout(inputs)
    in_maps = []
    for core in range(8):
        m = dict(shared)
        m.update(_host_layout(inputs, core))
        in_maps.append(m)
    res = run_bass_kernel_spmd(nc, in_maps, core_ids=list(range(8)))
    out = np.zeros((4, 4096, D), np.float32)
    for core in range(8):
        b, half = core // 2, core % 2
        out[b, half * NMAIN:(half + 1) * NMAIN] = res.results[core]["out"]
    return out
```

```python
import numpy as np
import concourse.bass as bass
import concourse.mybir as mybir
from concourse.bass_utils import run_bass_kernel_spmd
from contextlib import ExitStack

F32 = mybir.dt.float32
BF16 = mybir.dt.bfloat16
AF = mybir.ActivationFunctionType
ALU = mybir.AluOpType
AX = mybir.AxisListType

D = 2048
DFF = 5504
NJ = 43
HALO = 512
NMAIN = 2048
NEXT = NMAIN + HALO
EPS = 1e-6
NH = 16
VW = 65


import os
SKIP_SELF = os.environ.get("SKIP_SELF", "act,pe").split(",")


class Eng:
    def __init__(self, e, sem, name):
        self.e, self.sem, self.name = e, sem, name
        self.n = 0
        self.seen = {}
        self.last = None

    def wait(self, toks):
        best = {}
        for t in toks:
            if t is None:
                continue
            s, v = t
            k = id(s)
            if s is self.sem and self.name in SKIP_SELF:
                continue
            if k not in best or best[k][1] < v:
                best[k] = (s, v)
        for k, (s, v) in best.items():
            if self.seen.get(k, 0) < v:
                self.e.wait_ge(s, v)
                self.seen[k] = v

    def mark(self, ins):
        ins.then_inc(self.sem, 1)
        self.n += 1
        self.last = (self.sem, self.n)
        return self.last


class Buf:
    def __init__(self, name=""):
        self.name = name
        self.w = {}
        self.r = {}
        self.pr = {}

    @staticmethod
    def _add(d, tok):
        if tok is None:
            return
        k = id(tok[0])
        if k not in d or d[k][1] < tok[1]:
            d[k] = tok

    def rdeps(self):
        return list(self.w.values())

    def wdeps(self):
        return list(self.w.values()) + list(self.r.values()) + list(self.pr.values())

    def pwdeps(self):
        if self.r:
            npr = {}
            for d in (self.w, self.r, self.pr):
                for t in d.values():
                    self._add(npr, t)
            self.pr, self.w, self.r = npr, {}, {}
        return list(self.pr.values())

    def add_r(self, tok):
        self._add(self.r, tok)

    def set_w(self, tok):
        self.w = {}
        self._add(self.w, tok)
        self.r = {}
        self.pr = {}
        self._add(self.pr, tok)

    def add_w(self, tok):
        self._add(self.w, tok)


class _Stop(Exception):
    pass


class Prog:
    def __init__(self):
        self.nc = bass.Bass("TRN2", target_bir_lowering=False)
        self.st = ExitStack()
        self.nsem = 0
        self.pending_stores = []

    def sem(self, name):
        self.nsem += 1
        return self.st.enter_context(self.nc.semaphore(f"{name}_{self.nsem}"))

    def setup_engines(self):
        nc = self.nc
        self.PE = Eng(nc.tensor, self.sem("pe"), "pe")
        self.ACT = Eng(nc.scalar, self.sem("act"), "act")
        self.DVE = Eng(nc.vector, self.sem("dve"), "dve")
        self.POOL = Eng(nc.gpsimd, self.sem("pool"), "pool")
        self.SP = Eng(nc.sync, self.sem("sp"), "sp")

    def _deps(self, reads, writes, pwrites):
        deps = []
        for b in reads:
            deps += b.rdeps()
        for b in writes:
            deps += b.wdeps()
        for b in pwrites:
            deps += b.pwdeps()
        return deps

    def _upd(self, tok, reads, writes, pwrites):
        for b in reads:
            b.add_r(tok)
        for b in writes:
            b.set_w(tok)
        for b in pwrites:
            b.add_w(tok)

    def op(self, E, f, reads=(), writes=(), pwrites=()):
        E.wait(self._deps(reads, writes, pwrites))
        tok = E.mark(f())
        self._upd(tok, reads, writes, pwrites)
        return tok

    def pe_group(self, fs, reads=(), writes=(), pwrites=()):
        self.PE.wait(self._deps(reads, writes, pwrites))
        ins = None
        for f in fs:
            ins = f()
        tok = self.PE.mark(ins)
        self._upd(tok, reads, writes, pwrites)
        return tok

    def dma(self, Q, dsem, out, in_, reads=(), writes=(), pwrites=()):
        Q.wait(self._deps(reads, writes, pwrites))
        Q.e.dma_start(out=out, in_=in_).then_inc(dsem[0], 16)
        dsem[1] += 1
        tok = (dsem[0], 16 * dsem[1])
        self._upd(tok, reads, writes, pwrites)
        return tok

    def dsem(self, name):
        return [self.sem(name), 0]

    def sp_sync(self, extra=()):
        toks = [E.last for E in (self.PE, self.ACT, self.DVE)] + list(extra) + self.pending_stores
        self.SP.wait(toks)

    def wait_bufs(self, engines, bufs):
        toks = []
        for b in bufs:
            toks += b.wdeps()
        toks += self.pending_stores
        for E in engines:
            E.wait(toks)

    def barrier(self, bufs=()):
        toks = [E.last for E in (self.PE, self.ACT, self.DVE)]
        for b in bufs:
            toks += b.wdeps()
        toks += self.pending_stores
        for E in (self.PE, self.ACT, self.DVE, self.SP):
            E.wait(toks)


def build_program(upto="all", dbg=False, stop=0, skip_pro=False):
    P = Prog()
    lvl = {"pro": 0, "ffn1": 1, "mix": 2, "all": 3}[upto]
    nc = P.nc
    st = P.st
    en = st.enter_context

    def inp(name, shape):
        return nc.dram_tensor(name, shape, F32, kind="ExternalInput").ap()

    x_d = inp("x", [NEXT, D])
    cT_d = inp("cT", [128, 16])
    hv_d = inp("hv", [128, 1])
    wada_d = inp("w_ada", [D, 9 * D]) if not skip_pro else None
    bada_d = inp("b_ada", [1, 9 * D])
    wgu_d = [inp("w_gu1", [D, 2 * DFF]) if (lvl >= 1 and not os.environ.get("SKIP_F1")) else None, inp("w_gu2", [D, 2 * DFF]) if lvl >= 3 else None]
    wdn_d = [inp("w_dn1", [DFF, D]) if (lvl >= 1 and not os.environ.get("SKIP_F1")) else None, inp("w_dn2", [DFF, D]) if lvl >= 3 else None]
    win_d = inp("w_in", [D, 5120]) if lvl >= 2 else None
    wout_d = inp("w_out", [D, D]) if lvl >= 2 else None
    gpre_d = inp("gpre", [128, 3 * 16])
    gpost_d = inp("gpost", [3, D])
    lnv_d = inp("lnv", [2, 1024])
    gout_d = inp("gout", [128, 16])
    bsT_d = inp("bsT", [128, 8])
    wsT_d = inp("wsT", [128, 8 * 128])
    biasT_d = inp("biasT", [NH, 128, 640])
    ident_d = inp("ident", [128, 128])
    out_d = nc.dram_tensor("out", [NMAIN, D], F32, kind="ExternalOutput").ap()
    kd = dict(kind="ExternalOutput") if dbg else {}
    X1 = nc.dram_tensor("X1", [NEXT, D], F32, **kd).ap()
    X2 = nc.dram_tensor("X2", [NMAIN, D], F32, **kd).ap()
    Yd = nc.dram_tensor("Yd", [1024, D], F32, **kd).ap()
    Cs = nc.dram_tensor("Cs", [3 * 128, D], F32, **kd).ap()
    if dbg:
        dbg_cols = nc.dram_tensor("dbg_cols", [128, 160], F32, kind="ExternalOutput").ap()
        dbg_hT = nc.dram_tensor("dbg_hT", [128, 16 * 1024], BF16, kind="ExternalOutput").ap()
        dbg_aT = nc.dram_tensor("dbg_aT", [128, NJ * 1024], BF16, kind="ExternalOutput").ap()

    P.setup_engines()
    PE, ACT, DVE, POOL, SP = P.PE, P.ACT, P.DVE, P.POOL, P.SP
    V, A, T = nc.vector, nc.scalar, nc.tensor

    ident = en(nc.sbuf_tensor("ident_sb", [128, 128], F32))
    cols = en(nc.sbuf_tensor("cols", [128, 16 * 10], F32))
    hv = en(nc.sbuf_tensor("hv_sb", [128, 1], F32))
    epst = en(nc.sbuf_tensor("epst", [128, 1], F32))
    bsT = en(nc.sbuf_tensor("bsT_sb", [128, 8], F32))
    wring = [en(nc.sbuf_tensor(f"wring{i}", [128, 8192], BF16)) for i in range(3)]
    stt = [en(nc.sbuf_tensor(f"stt{i}", [128, 8], F32)) for i in range(4)]
    junk = en(nc.sbuf_tensor("junk", [128, D], BF16))
    ystage = [en(nc.sbuf_tensor(f"ystage{i}", [128, 512], F32)) for i in range(2)]
    banks = [en(nc.psum_tensor(f"bank{i}", [128, 512], F32)) for i in range(8)]

    b_ident, b_cols, b_hv, b_eps, b_bsT = Buf(), Buf(), Buf(), Buf(), Buf()
    b_cs = [Buf(), Buf(), Buf()]
    b_Cs = [Buf(), Buf(), Buf()]
    b_wring = [Buf(f"wr{i}") for i in range(3)]
    b_stt = [Buf() for _ in range(4)]
    b_ystage = [Buf(), Buf()]
    b_bank = [Buf(f"bank{i}") for i in range(8)]
    bank_ctr = [0]

    def next_bank():
        i = bank_ctr[0] % 8
        bank_ctr[0] += 1
        return banks[i], b_bank[i]

    def Acol(s):
        return cols[:, (2 * s) * 16:(2 * s + 1) * 16]

    def Bcol(s):
        return cols[:, (2 * s + 1) * 16:(2 * s + 2) * 16]

    def gprecol(s):
        return cols[:, (6 + s) * 16:(7 + s) * 16]

    goutcol = cols[:, 9 * 16:10 * 16]

    d_ct = P.dsem("ct")
    d_w = [P.dsem(f"w{i}") for i in range(3)]
    d_ld = [P.dsem(f"ld{i}") for i in range(8)]
    d_st = [P.dsem(f"st{i}") for i in range(4)]
    d_cs = P.dsem("cs")
    d_ys = [P.dsem("ys0"), P.dsem("ys1")]

    plan = []

    def plan_ada(vs):
        for v in vs:
            for nb in range(4):
                c0 = v * D + nb * 512
                plan.append((f"ada{v}_{nb}",
                             lambda s, c0=c0: (s[:, :].rearrange("p (k c) -> p k c", k=16),
                                               wada_d[:, c0:c0 + 512].rearrange("(k p) c -> p k c", p=128))))

    def plan_ada_one(v, nb):
        c0 = v * D + nb * 512
        plan.append((f"ada{v}_{nb}",
                     lambda s, c0=c0: (s[:, :].rearrange("p (k c) -> p k c", k=16),
                                       wada_d[:, c0:c0 + 512].rearrange("(k p) c -> p k c", p=128))))

    def plan_ffn(f):
        wg, wd = wgu_d[f], wdn_d[f]
        ngroups = 3 if f == 0 else 2
        for g in range(ngroups):
            for blk in range(22):
                j0 = 2 * blk
                ncol = 256 if blk < 21 else 128
                plan.append((f"gu{f}_{g}_{blk}",
                             lambda s, j0=j0, ncol=ncol, wg=wg: (
                                 s[:, :].rearrange("p (two k c) -> p two k c", two=2, k=16)[:, :, :, 0:ncol],
                                 wg[:, :].rearrange("(k p) (two c) -> p two k c", p=128, two=2)[:, :, :, j0 * 128:j0 * 128 + ncol])))
                if f == 0 and plan_q:
                    v = plan_q.pop(0)
                    n_left = sum(1 for q in plan_q if q == v)
                    plan_ada_one(v, 3 - n_left)
            for cb in range(4):
                for jb in range(3):
                    j0 = 16 * jb
                    nj = min(16, NJ - j0)
                    plan.append((f"dn{f}_{g}_{cb}_{jb}",
                                 lambda s, j0=j0, nj=nj, cb=cb, wd=wd: (
                                     s[:, :].rearrange("p (j c) -> p j c", j=16)[:, 0:nj, :],
                                     wd[j0 * 128:(j0 + nj) * 128, cb * 512:(cb + 1) * 512].rearrange("(j p) c -> p j c", p=128))))

    def plan_mixer():
        for gi in range(5):
            blks = [2, 3, 4, 5] if gi == 0 else list(range(10))
            for blk in blks:
                plan.append((f"in{gi}_{blk}",
                             lambda s, blk=blk: (s[:, :].rearrange("p (k c) -> p k c", k=16),
                                                 win_d[:, blk * 512:(blk + 1) * 512].rearrange("(k p) c -> p k c", p=128))))
            if gi > 0:
                for cb in range(4):
                    plan.append((f"wo{gi}_{cb}",
                                 lambda s, cb=cb: (s[:, :].rearrange("p (k c) -> p k c", k=16),
                                                   wout_d[:, cb * 512:(cb + 1) * 512].rearrange("(k p) c -> p k c", p=128))))

    plan_q = []
    if not skip_pro:
        plan_ada(range(3) if lvl >= 1 else range(9))
        if lvl >= 1:
            plan_q = [v for v in range(3, 9) for _ in range(4)]
    if lvl >= 1 and stop != 1 and not os.environ.get("SKIP_F1"):
        plan_ffn(0)
    if lvl >= 2:
        plan_mixer()
    if lvl >= 3:
        plan_ffn(1)
    ws_state = {"issued": 0, "got": 0}

    def ws_issue_upto(n):
        while ws_state["issued"] < min(n, len(plan)):
            i = ws_state["issued"]
            s = i % 3
            o, src = plan[i][1](wring[s])
            P.dma(POOL, d_w[s], o, src, writes=[b_wring[s]])
            ws_state["issued"] += 1

    def ws_get(tag):
        i = ws_state["got"]
        assert plan[i][0] == tag, (plan[i][0], tag)
        ws_issue_upto(i + 1)
        ws_state["got"] += 1
        return wring[i % 3], b_wring[i % 3]

    def ws_done():
        ws_issue_upto(ws_state["got"] + 2)

    P.dma(SP, P.dsem("m"), ident[:], ident_d[:, :], writes=[b_ident])
    P.dma(SP, P.dsem("m"), hv[:], hv_d[:, :], writes=[b_hv])
    P.dma(SP, P.dsem("m"), bsT[:], bsT_d[:, :], writes=[b_bsT])
    P.dma(SP, P.dsem("m"), cols[:, 96:144], gpre_d[:, :], pwrites=[b_cols])
    P.dma(SP, P.dsem("m"), cols[:, 144:160], gout_d[:, :], pwrites=[b_cols])
    P.op(DVE, lambda: V.memset(epst[:], EPS), writes=[b_eps])
    ws_issue_upto(2)

    def rstd_from_sumsq(si, n):
        P.op(ACT, lambda: A.activation(out=stt[si][:, 1:2], in_=stt[si][:, 0:1], func=AF.Sqrt,
                                       scale=1.0 / n, bias=epst[:, 0:1]), reads=[b_eps], writes=[b_stt[si]])
        P.op(DVE, lambda: V.reciprocal(out=stt[si][:, 2:3], in_=stt[si][:, 1:2]), writes=[b_stt[si]])

    def transpose_tile(src, b_src, dst_fn, b_dst, scol, bcol, b_sc):
        bk = [next_bank() for _ in range(4)]
        for q in range(4):
            P.pe_group([lambda kc=kc, q=q: T.transpose(out=bk[q][0][:, (kc % 4) * 128:(kc % 4 + 1) * 128],
                                                       in_=src[:, kc * 128:(kc + 1) * 128], identity=ident[:])
                        for kc in range(4 * q, 4 * q + 4)], reads=[b_src, b_ident], writes=[bk[q][1]])
        DT = int(os.environ.get("DBG_T", "2"))
        for kc in range(16):
            q = kc // 4
            pin = bk[q][0][:, (kc % 4) * 128:(kc % 4 + 1) * 128]
            if DT == 0 or (DT == 1 and kc % 2 == 1) or (DT == 3 and kc % 2 == 0):
                continue
            if True:
                if bcol is None:
                    P.op(DVE, lambda: V.tensor_scalar(out=dst_fn(kc), in0=pin, scalar1=scol[:, kc:kc + 1], scalar2=None,
                                                      op0=ALU.mult), reads=[bk[q][1], b_sc], pwrites=[b_dst])
                else:
                    P.op(DVE, lambda: V.tensor_scalar(out=dst_fn(kc), in0=pin, scalar1=scol[:, kc:kc + 1],
                                                      scalar2=bcol[:, kc:kc + 1], op0=ALU.mult, op1=ALU.add),
                         reads=[bk[q][1], b_sc], pwrites=[b_dst])
            else:
                if bcol is None:
                    P.op(ACT, lambda: A.activation(out=dst_fn(kc), in_=pin, func=AF.Copy, scale=scol[:, kc:kc + 1]),
                         reads=[bk[q][1], b_sc], pwrites=[b_dst])
                else:
                    P.op(ACT, lambda: A.activation(out=dst_fn(kc), in_=pin, func=AF.Identity, scale=scol[:, kc:kc + 1],
                                                   bias=bcol[:, kc:kc + 1]), reads=[bk[q][1], b_sc], pwrites=[b_dst])

    def stage_a(xs, r0, ntiles, Tt, b_Tt, hT, b_hT, s):
        for i in range(ntiles):
            sl = i % 2
            xt, b_xt = Tt[sl], b_Tt[sl]
            xn, b_xn = Tt[2 + sl], b_Tt[2 + sl]
            import os
            DA = int(os.environ.get("DBG_A", "9"))
            if i >= int(os.environ.get("DBG_NT", "99")):
                continue
            P.dma(SP, d_ld[sl], xt, xs[r0 + i * 128:r0 + (i + 1) * 128, :], writes=[b_xt])
            P.op(DVE, lambda: V.scalar_tensor_tensor(out=junk[:], in0=xt, scalar=1.0, in1=xt, op0=ALU.mult, op1=ALU.mult,
                                                     accum_out=stt[sl][:, 0:1]), reads=[b_xt], writes=[b_stt[sl]])
            if DA < 2:
                continue
            rstd_from_sumsq(sl, D)
            if DA < 3:
                continue
            P.op(ACT, lambda: A.activation(out=xn, in_=xt, func=AF.Copy, scale=stt[sl][:, 2:3]),
                 reads=[b_xt, b_stt[sl]], writes=[b_xn])
            if DA < 4:
                continue
            transpose_tile(xn, b_xn, lambda kc, i=i: hT[:, kc, i * 128:(i + 1) * 128], b_hT[i], Acol(s), Bcol(s), b_cs[s])

    def stage_d(xs, rs0, xd, rd0, ntiles, Tt, b_Tt, ct, b_ct, nsl=2):
        def loads(i):
            sl = i % nsl
            P.dma(SP, d_ld[sl], Tt[sl], Yd[i * 128:(i + 1) * 128, :], writes=[b_Tt[sl]])
            P.dma(SP, d_ld[nsl + sl], Tt[nsl + sl], xs[rs0 + i * 128:rs0 + (i + 1) * 128, :], writes=[b_Tt[nsl + sl]])

        for i in range(min(nsl - 1, ntiles)):
            loads(i)
        for i in range(ntiles):
            if i + nsl - 1 < ntiles:
                loads(i + nsl - 1)
            sl = i % nsl
            yt, b_yt = Tt[sl], b_Tt[sl]
            xt, b_xt = Tt[nsl + sl], b_Tt[nsl + sl]
            P.op(DVE, lambda: V.scalar_tensor_tensor(out=junk[:], in0=yt, scalar=1.0, in1=yt, op0=ALU.mult, op1=ALU.mult,
                                                     accum_out=stt[sl][:, 0:1]), reads=[b_yt], writes=[b_stt[sl]])
            rstd_from_sumsq(sl, D)
            P.op(DVE, lambda: V.scalar_tensor_tensor(out=yt, in0=yt, scalar=stt[sl][:, 2:3], in1=ct, op0=ALU.mult,
                                                     op1=ALU.mult), reads=[b_stt[sl], b_ct], writes=[b_yt])
            P.op(DVE, lambda: V.tensor_tensor(out=xt, in0=yt, in1=xt, op=ALU.add), reads=[b_yt], writes=[b_xt])
            tok = P.dma(SP, d_st[sl], xd[rd0 + i * 128:rd0 + (i + 1) * 128, :], xt, reads=[b_xt])
            P.pending_stores.append(tok)

    evac_ctr = [0]

    def store_y_block(bank, b_bk, i, cb):
        k = evac_ctr[0] % 2
        evac_ctr[0] += 1
        if k == 0:
            P.op(ACT, lambda: A.copy(out=ystage[k][:], in_=bank[:, :]), reads=[b_bk], writes=[b_ystage[k]])
        else:
            P.op(DVE, lambda: V.tensor_copy(out=ystage[k][:], in_=bank[:, :]), reads=[b_bk], writes=[b_ystage[k]])
        tok = P.dma(SP, d_ys[k], Yd[i * 128:(i + 1) * 128, cb * 512:(cb + 1) * 512], ystage[k][:], reads=[b_ystage[k]])
        P.pending_stores.append(tok)

    def load_ct(ct, b_ct, s):
        P.dma(SP, d_ct, ct, Cs[s * 128:(s + 1) * 128, :], reads=[b_Cs[s]], writes=[b_ct])

    ada_scope = ExitStack()
    e3 = ada_scope.enter_context
    cT = e3(nc.sbuf_tensor("cT_sb", [128, 16], F32))
    cact = e3(nc.sbuf_tensor("cact", [128, 16], F32))
    crep = e3(nc.sbuf_tensor("crep", [128, 16, 128], BF16))
    badab = [e3(nc.sbuf_tensor(f"badab{i}", [128, 512], F32)) for i in range(2)]
    gpostb = [e3(nc.sbuf_tensor(f"gpostb{i}", [128, 512], F32)) for i in range(2)]
    cresb = [e3(nc.sbuf_tensor(f"cresb{i}", [128, 512], F32)) for i in range(2)]
    mblk = e3(nc.sbuf_tensor("mblk", [128, 512], F32))
    tmpb = e3(nc.sbuf_tensor("tmpb", [128, 512], F32))
    dcol = e3(nc.sbuf_tensor("dcol", [128, 4], F32))
    b_cT, b_cact, b_crep, b_mblk, b_tmpb, b_dcol = [Buf() for _ in range(6)]
    b_badab, b_gpostb, b_cresb = [Buf(), Buf()], [Buf(), Buf()], [Buf(), Buf()]
    d_ab, d_gp, d_cs2 = [P.dsem("ab0"), P.dsem("ab1")], [P.dsem("gp0"), P.dsem("gp1")], [P.dsem("cs0"), P.dsem("cs1")]
    ada_ctr = [0]

    def ada_block(v, nb):
        s_, role = v // 3, v % 3
        k = ada_ctr[0] % 2
        ada_ctr[0] += 1
        c0 = v * D + nb * 512
        P.dma(SP, d_ab[k], badab[k][:], bada_d[0:1, c0:c0 + 512].partition_broadcast(128), writes=[b_badab[k]])
        if role == 2:
            P.dma(SP, d_gp[k], gpostb[k][:], gpost_d[s_:s_ + 1, nb * 512:(nb + 1) * 512].partition_broadcast(128),
                  writes=[b_gpostb[k]])
        slot, b_slot = ws_get(f"ada{v}_{nb}")
        sv = slot[:, :].rearrange("p (k c) -> p k c", k=16)
        bk, b_bk = next_bank()
        P.pe_group([lambda kc=kc: T.matmul(out=bk[:, :], lhsT=crep[:, kc, :], rhs=sv[:, kc, :],
                                           start=(kc == 0), stop=(kc == 15)) for kc in range(16)],
                   reads=[b_crep, b_slot], writes=[b_bk])
        ws_done()
        P.op(DVE, lambda: V.tensor_tensor(out=mblk[:], in0=bk[:, :], in1=badab[k][:], op=ALU.add),
             reads=[b_bk, b_badab[k]], writes=[b_mblk])
        if role in (0, 1):
            P.op(DVE, lambda: V.tensor_tensor(out=tmpb[:].rearrange("p (a b) -> p a b", a=4),
                                              in0=mblk[:].rearrange("p (a b) -> p a b", a=4),
                                              in1=ident[:].unsqueeze(1).to_broadcast([128, 4, 128]), op=ALU.mult),
                 reads=[b_mblk, b_ident], writes=[b_tmpb])
            if role == 0:
                P.op(DVE, lambda: V.tensor_reduce(out=Bcol(s_)[:, nb * 4:(nb + 1) * 4],
                                                  in_=tmpb[:].rearrange("p (a b) -> p a b", a=4), axis=AX.X, op=ALU.add),
                     reads=[b_tmpb], pwrites=[b_cs[s_]])
            else:
                P.op(DVE, lambda: V.tensor_reduce(out=dcol[:], in_=tmpb[:].rearrange("p (a b) -> p a b", a=4),
                                                  axis=AX.X, op=ALU.add), reads=[b_tmpb], writes=[b_dcol])
                P.op(DVE, lambda: V.scalar_tensor_tensor(out=Acol(s_)[:, nb * 4:(nb + 1) * 4], in0=dcol[:], scalar=1.0,
                                                         in1=gprecol(s_)[:, nb * 4:(nb + 1) * 4], op0=ALU.add, op1=ALU.mult),
                     reads=[b_dcol, b_cols], pwrites=[b_cs[s_]])
        else:
            gsc = 1.0 if s_ == 1 else 0.5
            P.op(DVE, lambda: V.scalar_tensor_tensor(out=cresb[k][:], in0=mblk[:], scalar=gsc, in1=gpostb[k][:],
                                                     op0=ALU.mult, op1=ALU.mult),
                 reads=[b_mblk, b_gpostb[k]], writes=[b_cresb[k]])
            tok = P.dma(SP, d_cs2[k], Cs[s_ * 128:(s_ + 1) * 128, nb * 512:(nb + 1) * 512], cresb[k][:],
                        reads=[b_cresb[k]], pwrites=[b_Cs[s_]])
            P.pending_stores.append(tok)

    ada_queue = []
    if skip_pro:
        for s_ in range(3):
            P.op(DVE, lambda: V.memset(cols[:, 32 * s_:32 * s_ + 32], 1.0), writes=[b_cs[s_]])
    else:
        P.dma(SP, P.dsem("m"), cT[:], cT_d[:, :], writes=[b_cT])
        P.op(ACT, lambda: A.activation(out=cact[:], in_=cT[:], func=AF.Silu), reads=[b_cT], writes=[b_cact])
        P.op(DVE, lambda: V.tensor_copy(out=crep[:], in_=cact[:].unsqueeze(2).to_broadcast([128, 16, 128])),
             reads=[b_cact], writes=[b_crep])
        nearly = 3 if lvl >= 1 else 9
        for v in range(nearly):
            for nb in range(4):
                ada_block(v, nb)
        ada_queue = [(v, nb) for v in range(nearly, 9) for nb in range(4)]
    if dbg and lvl == 0:
        P.pending_stores.append(P.dma(SP, P.dsem("dbg"), dbg_cols[:, :], cols[:], reads=[b_cols] + b_cs))
        P.barrier()

    def ffn_phase(f, xs, xd, groups, s):
        with ExitStack() as sc:
            e2 = sc.enter_context
            hTt = e2(nc.sbuf_tensor(f"hT{f}", [128, 16, 1024], BF16))
            RA = e2(nc.sbuf_tensor(f"RA{f}", [128, 22016], F32))
            sg = [e2(nc.sbuf_tensor(f"sg{f}_{i}", [128, 512], F32)) for i in range(2)]
            b_sg = [Buf(), Buf()]
            aT = RA[:, :].bitcast(BF16).rearrange("p (j t) -> p j t", j=NJ)
            Tt = [RA[:, i * D:(i + 1) * D] for i in range(8)]
            ct = RA[:, 8 * D:9 * D]
            b_Tt = [Buf() for _ in range(8)]
            b_ct = Buf()
            b_hT = [Buf() for _ in range(8)]
            b_aT = [Buf() for _ in range(NJ)]
            sgc = 0
            for g, (rs0, rd0, ntok) in enumerate(groups):
                nt = ntok // 128
                ntb = ntok // 512
                stage_a(xs, rs0, nt, Tt, b_Tt, hTt, b_hT, s)
                P.wait_bufs((DVE,), b_Tt + [b_ct])
                if stop == 1:
                    P.pending_stores.append(P.dma(SP, P.dsem("dbg"), dbg_hT[:, :], hTt[:].rearrange("p k t -> p (k t)"), reads=b_hT))
                    P.barrier()
                    P.stopped = True
                    return
                for blk in range(22):
                    slot, b_slot = ws_get(f"gu{f}_{g}_{blk}")
                    sv = slot[:, :].rearrange("p (two k c) -> p two k c", two=2, k=16)
                    for jj in range(2):
                        j = 2 * blk + jj
                        if j >= NJ:
                            continue
                        for tb in range(ntb):
                            bg, b_bg = next_bank()
                            bu, b_bu = next_bank()
                            hb = [b_hT[4 * tb + q] for q in range(4)]
                            P.pe_group([lambda kc=kc: T.matmul(out=bg[:, :], lhsT=sv[:, 0, kc, jj * 128:(jj + 1) * 128],
                                                               rhs=hTt[:, kc, tb * 512:(tb + 1) * 512],
                                                               start=(kc == 0), stop=(kc == 15)) for kc in range(16)],
                                       reads=[b_slot] + hb, writes=[b_bg])
                            P.pe_group([lambda kc=kc: T.matmul(out=bu[:, :], lhsT=sv[:, 1, kc, jj * 128:(jj + 1) * 128],
                                                               rhs=hTt[:, kc, tb * 512:(tb + 1) * 512],
                                                               start=(kc == 0), stop=(kc == 15)) for kc in range(16)],
                                       reads=[b_slot] + hb, writes=[b_bu])
                            k = sgc % 2
                            sgc += 1
                            P.op(ACT, lambda: A.activation(out=sg[k][:], in_=bg[:, :], func=AF.Silu),
                                 reads=[b_bg], writes=[b_sg[k]])
                            P.op(DVE, lambda: V.tensor_tensor(out=aT[:, j, tb * 512:(tb + 1) * 512], in0=sg[k][:],
                                                              in1=bu[:, :], op=ALU.mult),
                                 reads=[b_sg[k], b_bu], pwrites=[b_aT[j]])
                    ws_done()
                    if f == 0 and ada_queue:
                        ada_block(*ada_queue.pop(0))
                if stop == 2:
                    P.barrier()
                    P.pending_stores.append(P.dma(SP, P.dsem("dbg"), dbg_aT[:, :], RA[:, :].bitcast(BF16), reads=b_aT))
                    P.barrier()
                    P.stopped = True
                    return
                for cb in range(4):
                    for jb in range(3):
                        slot, b_slot = ws_get(f"dn{f}_{g}_{cb}_{jb}")
                        sv = slot[:, :].rearrange("p (j c) -> p j c", j=16)
                        j0 = 16 * jb
                        nj = min(16, NJ - j0)
                        for i in range(nt):
                            P.pe_group([lambda j=j: T.matmul(out=banks[i][:, :], lhsT=aT[:, j, i * 128:(i + 1) * 128],
                                                             rhs=sv[:, j - j0, :], start=(j == 0), stop=(j == NJ - 1))
                                        for j in range(j0, j0 + nj)],
                                       reads=[b_slot] + [b_aT[j] for j in range(j0, j0 + nj)], writes=[b_bank[i]])
                            if jb == 2:
                                store_y_block(banks[i], b_bank[i], i, cb)
                        ws_done()
                P.sp_sync()
                if stop == 3:
                    P.stopped = True
                    return
                load_ct(ct, b_ct, s)
                stage_d(xs, rs0, xd, rd0, nt, Tt, b_Tt, ct, b_ct, nsl=4)
                if stop == 4:
                    P.barrier(b_Tt)
                    P.stopped = True
                    return
            P.barrier(b_Tt)
            P.pending_stores = []

    P.stopped = False
    SKIP_F1 = bool(os.environ.get("SKIP_F1"))
    if lvl >= 1 and not SKIP_F1:
        ffn_phase(0, x_d, X1, [(0, 0, 1024), (1024, 1024, 1024), (2048, 2048, 512)], 0)
    assert not ada_queue or SKIP_F1 or P.stopped or lvl < 1, ada_queue
    P.barrier()
    ada_scope.close()

    def mixer_phase():
      with ExitStack() as sc:
        e2 = sc.enter_context
        hTm = e2(nc.sbuf_tensor("hTm", [128, 16, 512], BF16))
        qT = e2(nc.sbuf_tensor("qT", [128, 8, 512], BF16))
        kT = [e2(nc.sbuf_tensor(f"kT{i}", [128, 8, 512], BF16)) for i in range(2)]
        vaug = [e2(nc.sbuf_tensor(f"vaug{i}", [128, 4, NH, VW], BF16)) for i in range(2)]
        RM = e2(nc.sbuf_tensor("RM", [128, 5 * D], F32))
        out_a = e2(nc.sbuf_tensor("out_a", [128, 1024], F32))
        out_b = e2(nc.sbuf_tensor("out_b", [128, 1024], F32))
        vln = e2(nc.sbuf_tensor("vln", [128, 1024], BF16))
        sc_sb = [e2(nc.sbuf_tensor(f"sc_sb{i}", [128, 640], F32)) for i in range(4)]
        pT = [e2(nc.sbuf_tensor(f"pT{i}", [128, 640], BF16)) for i in range(4)]
        biasb = [e2(nc.sbuf_tensor(f"biasb{i}", [128, 640], F32)) for i in range(4)]
        lnb = e2(nc.sbuf_tensor("lnb", [128, 2048], F32))
        wsTf = e2(nc.sbuf_tensor("wsTf", [128, 1024], F32))
        wsT = e2(nc.sbuf_tensor("wsT_bf", [128, 8, 128], BF16))
        b_hTm = [Buf() for _ in range(4)]
        b_qT, b_kT, b_vaug = Buf(), [Buf(), Buf()], [Buf(), Buf()]
        b_out_a, b_out_b, b_vln = Buf(), Buf(), Buf()
        b_sc, b_pT, b_biasb = [Buf() for _ in range(4)], [Buf() for _ in range(4)], [Buf() for _ in range(4)]
        b_lnb, b_wsTf, b_wsT = Buf(), Buf(), Buf()
        Tt = [RM[:, i * D:(i + 1) * D] for i in range(4)]
        ct = RM[:, 4 * D:5 * D]
        b_Tt = [Buf() for _ in range(4)]
        b_ct = Buf()
        u_g = RM[:, 0:2 * D].rearrange("p (t c) -> p t c", t=4)
        vg = RM[:, 2 * D:4 * D].rearrange("p (t c) -> p t c", t=4)
        merged = RM[:, 4 * D:5 * D]
        b_ug, b_vg = [Buf() for _ in range(4)], [Buf() for _ in range(4)]
        b_merged = Buf()
        d_bias = [P.dsem(f"bias{i}") for i in range(4)]

        P.dma(SP, P.dsem("m"), lnb[:, 0:1024], lnv_d[0:1, :].partition_broadcast(128), pwrites=[b_lnb])
        P.dma(SP, P.dsem("m"), lnb[:, 1024:2048], lnv_d[1:2, :].partition_broadcast(128), pwrites=[b_lnb])
        P.dma(SP, P.dsem("m"), wsTf[:], wsT_d[:, :], writes=[b_wsTf])
        P.op(DVE, lambda: V.tensor_copy(out=wsT[:].rearrange("p g t -> p (g t)"), in_=wsTf[:]), reads=[b_wsTf], writes=[b_wsT])
        P.op(DVE, lambda: V.memset(wsT[64:128, :, 0:64], 0.0), writes=[b_wsT])

        ev = [0]

        def copy_evac(out, in_, reads, writes):
            ev[0] += 1
            if ev[0] % 2 == 0:
                P.op(ACT, lambda: A.copy(out=out, in_=in_), reads=reads, pwrites=writes)
            else:
                P.op(DVE, lambda: V.tensor_copy(out=out, in_=in_), reads=reads, pwrites=writes)

        for gi in range(5):
            ks = gi % 2
            stage_a(X1, 512 * gi, 4, Tt, b_Tt, hTm, b_hTm, 1)
            P.wait_bufs((ACT, DVE), b_Tt + [b_ct])
            P.op(DVE, lambda: V.memset(vaug[ks][:, :, :, 64:65], 1.0), writes=[b_vaug[ks]])
            if gi == 0:
                P.op(DVE, lambda: V.tensor_scalar(out=vaug[ks][:, :, :, 64:65], in0=vaug[ks][:, :, :, 64:65],
                                                  scalar1=hv[:, 0:1], scalar2=None, op0=ALU.mult),
                     reads=[b_hv], writes=[b_vaug[ks]])
            blks = [2, 3, 4, 5] if gi == 0 else list(range(10))
            for blk in blks:
                slot, b_slot = ws_get(f"in{gi}_{blk}")
                sv = slot[:, :].rearrange("p (k c) -> p k c", k=16)
                if blk < 4:
                    dst, b_dst = (qT, b_qT) if blk < 2 else (kT[ks], b_kT[ks])
                    for c in range(4):
                        bk, b_bk = next_bank()
                        P.pe_group([lambda kc=kc: T.matmul(out=bk[:, :], lhsT=sv[:, kc, c * 128:(c + 1) * 128],
                                                           rhs=hTm[:, kc, :], start=(kc == 0), stop=(kc == 15))
                                    for kc in range(16)], reads=[b_slot] + b_hTm, writes=[b_bk])
                        copy_evac(dst[:, (blk % 2) * 4 + c, :], bk[:, :], [b_bk], [b_dst])
                else:
                    for i in range(4):
                        bk, b_bk = next_bank()
                        P.pe_group([lambda kc=kc: T.matmul(out=bk[:, :], lhsT=hTm[:, kc, i * 128:(i + 1) * 128],
                                                           rhs=sv[:, kc, :], start=(kc == 0), stop=(kc == 15))
                                    for kc in range(16)], reads=[b_slot, b_hTm[i]], writes=[b_bk])
                        if blk < 6:
                            hb = blk - 4
                            o = vaug[ks][:, i, hb * 8:(hb + 1) * 8, 0:64]
                            src = bk[:, :].rearrange("p (h d) -> p h d", h=8)
                            if gi == 0:
                                P.op(DVE, lambda: V.tensor_scalar(out=o, in0=src, scalar1=hv[:, 0:1], scalar2=None,
                                                                  op0=ALU.mult), reads=[b_bk, b_hv], pwrites=[b_vaug[ks]])
                            else:
                                copy_evac(o, src, [b_bk], [b_vaug[ks]])
                        elif blk < 8:
                            P.op(ACT, lambda: A.activation(out=u_g[:, i, (blk - 6) * 512:(blk - 5) * 512], in_=bk[:, :],
                                                           func=AF.Gelu), reads=[b_bk], pwrites=[b_ug[i]])
                        else:
                            P.op(ACT, lambda: A.activation(out=vg[:, i, (blk - 8) * 512:(blk - 7) * 512], in_=bk[:, :],
                                                           func=AF.Gelu), reads=[b_bk], pwrites=[b_vg[i]])
                ws_done()
            DM = int(os.environ.get("DBG_M", "99"))
            if gi == 0:
                if DM == 1:
                    P.barrier()
                    return
                continue
            if DM == 2:
                P.barrier()
                return
            for t in range(4):
                ubanks = {}

                def scores(h):
                    k = h % 4
                    hp, hc = h % 2, h // 2
                    P.dma(SP, d_bias[k], biasb[k][:], biasT_d[h, :, :], writes=[b_biasb[k]])
                    bA, b_bA = next_bank()
                    bB, b_bB = next_bank()
                    ubanks[h] = (bB, b_bB)
                    for c in range(5):
                        kt = 4 * gi + t - 4 + c
                        ksl = (kt // 4) % 2
                        kc0 = (kt % 4) * 128
                        o = bA[:, c * 128:(c + 1) * 128] if c < 4 else bB[:, 0:128]
                        if c == 0:
                            wr = dict(writes=[b_bA])
                        elif c == 4:
                            wr = dict(writes=[b_bB])
                        else:
                            wr = dict(pwrites=[b_bA])
                        P.pe_group([lambda: T.matmul(out=o, lhsT=kT[ksl][hp * 64:(hp + 1) * 64, hc, kc0:kc0 + 128],
                                                     rhs=qT[hp * 64:(hp + 1) * 64, hc, t * 128:(t + 1) * 128],
                                                     start=True, stop=True)],
                                   reads=[b_kT[ksl], b_qT], **wr)
                    P.op(DVE, lambda: V.scalar_tensor_tensor(out=sc_sb[k][:, 0:512], in0=bA[:, :], scalar=0.125,
                                                             in1=biasb[k][:, 0:512], op0=ALU.mult, op1=ALU.add),
                         reads=[b_bA, b_biasb[k]], writes=[b_sc[k]])
                    P.op(DVE, lambda: V.scalar_tensor_tensor(out=sc_sb[k][:, 512:640], in0=bB[:, 0:128], scalar=0.125,
                                                             in1=biasb[k][:, 512:640], op0=ALU.mult, op1=ALU.add),
                         reads=[b_bB, b_biasb[k]], pwrites=[b_sc[k]])
                    P.op(ACT, lambda: A.activation(out=pT[k][:], in_=sc_sb[k][:], func=AF.Exp),
                         reads=[b_sc[k]], writes=[b_pT[k]])

                def pv(h):
                    k = h % 4
                    bO, b_bO = ubanks.pop(h)
                    fs = []
                    rd = [b_pT[k]]
                    for c in range(5):
                        kt = 4 * gi + t - 4 + c
                        ksl = (kt // 4) % 2
                        rd.append(b_vaug[ksl])
                        fs.append(lambda c=c, kt=kt, ksl=ksl: T.matmul(out=bO[:, 128:128 + VW], lhsT=pT[k][:, c * 128:(c + 1) * 128],
                                                                      rhs=vaug[ksl][:, kt % 4, h, :],
                                                                      start=(c == 0), stop=(c == 4)))
                    P.pe_group(fs, reads=rd, pwrites=[b_bO])
                    P.op(DVE, lambda: V.reciprocal(out=stt[2][:, 0:1], in_=bO[:, 192:193]), reads=[b_bO], writes=[b_stt[2]])
                    P.op(DVE, lambda: V.tensor_scalar(out=out_a[:, h * 64:(h + 1) * 64], in0=bO[:, 128:192],
                                                      scalar1=stt[2][:, 0:1], scalar2=None, op0=ALU.mult),
                         reads=[b_bO, b_stt[2]], pwrites=[b_out_a])

                for h in range(NH + 2):
                    if h < NH:
                        scores(h)
                    if h >= 2:
                        pv(h - 2)
                if DM == 3:
                    P.barrier()
                    return
                vt = vg[:, t, :]
                P.op(DVE, lambda: V.tensor_reduce(out=stt[3][:, 0:1], in_=vt, axis=AX.X, op=ALU.add),
                     reads=[b_vg[t]], writes=[b_stt[3]])
                P.op(DVE, lambda: V.tensor_scalar(out=stt[3][:, 1:2], in0=stt[3][:, 0:1], scalar1=-1.0 / 1024, scalar2=None,
                                                  op0=ALU.mult), writes=[b_stt[3]])
                P.op(DVE, lambda: V.tensor_scalar(out=vt, in0=vt, scalar1=stt[3][:, 1:2], scalar2=None, op0=ALU.add),
                     reads=[b_stt[3]], writes=[b_vg[t]])
                P.op(DVE, lambda: V.scalar_tensor_tensor(out=junk[:, 0:1024], in0=vt, scalar=1.0, in1=vt, op0=ALU.mult,
                                                         op1=ALU.mult, accum_out=stt[0][:, 0:1]),
                     reads=[b_vg[t]], writes=[b_stt[0]])
                rstd_from_sumsq(0, 1024)
                P.op(DVE, lambda: V.scalar_tensor_tensor(out=vt, in0=vt, scalar=stt[0][:, 2:3], in1=lnb[:, 0:1024],
                                                         op0=ALU.mult, op1=ALU.mult),
                     reads=[b_stt[0], b_lnb], writes=[b_vg[t]])
                P.op(DVE, lambda: V.tensor_tensor(out=vln[:], in0=vt, in1=lnb[:, 1024:2048], op=ALU.add),
                     reads=[b_vg[t], b_lnb], writes=[b_vln])
                bz = [next_bank(), next_bank()]
                for g in range(8):
                    zb, b_zb = bz[g // 4]
                    P.pe_group([lambda: T.matmul(out=zb[:, (g % 4) * 128:(g % 4 + 1) * 128], lhsT=wsT[:, g, :],
                                                 rhs=vln[:, g * 128:(g + 1) * 128], start=True, stop=True)],
                               reads=[b_wsT, b_vln], pwrites=[b_zb])
                for g in range(8):
                    zb, b_zb = bz[g // 4]
                    P.op(DVE, lambda: V.scalar_tensor_tensor(out=out_b[:, g * 128:(g + 1) * 128],
                                                             in0=zb[:, (g % 4) * 128:(g % 4 + 1) * 128],
                                                             scalar=bsT[:, g:g + 1], in1=u_g[:, t, g * 128:(g + 1) * 128],
                                                             op0=ALU.add, op1=ALU.mult),
                         reads=[b_zb, b_bsT, b_ug[t]], pwrites=[b_out_b])
                if DM == 4:
                    P.barrier()
                    return
                P.op(DVE, lambda: V.scalar_tensor_tensor(out=junk[:, 0:1024], in0=out_a[:], scalar=1.0, in1=out_a[:],
                                                         op0=ALU.mult, op1=ALU.mult, accum_out=stt[0][:, 0:1]),
                     reads=[b_out_a], writes=[b_stt[0]])
                rstd_from_sumsq(0, 1024)
                P.op(DVE, lambda: V.scalar_tensor_tensor(out=junk[:, 1024:2048], in0=out_b[:], scalar=1.0, in1=out_b[:],
                                                         op0=ALU.mult, op1=ALU.mult, accum_out=stt[1][:, 0:1]),
                     reads=[b_out_b], writes=[b_stt[1]])
                rstd_from_sumsq(1, 1024)
                P.op(ACT, lambda: A.activation(out=merged[:, 0:1024], in_=out_a[:], func=AF.Copy, scale=stt[0][:, 2:3]),
                     reads=[b_out_a, b_stt[0]], pwrites=[b_merged])
                P.op(ACT, lambda: A.activation(out=merged[:, 1024:2048], in_=out_b[:], func=AF.Copy, scale=stt[1][:, 2:3]),
                     reads=[b_out_b, b_stt[1]], pwrites=[b_merged])
                transpose_tile(merged, b_merged, lambda kc, t=t: hTm[:, kc, t * 128:(t + 1) * 128], b_hTm[t], goutcol, None, b_cols)
                if DM == 5:
                    P.barrier()
                    return
            for cb in range(4):
                slot, b_slot = ws_get(f"wo{gi}_{cb}")
                sv = slot[:, :].rearrange("p (k c) -> p k c", k=16)
                for i in range(4):
                    bk, b_bk = next_bank()
                    P.pe_group([lambda kc=kc: T.matmul(out=bk[:, :], lhsT=hTm[:, kc, i * 128:(i + 1) * 128],
                                                       rhs=sv[:, kc, :], start=(kc == 0), stop=(kc == 15))
                                for kc in range(16)], reads=[b_slot, b_hTm[i]], writes=[b_bk])
                    store_y_block(bk, b_bk, i, cb)
                ws_done()
            P.sp_sync()
            load_ct(ct, b_ct, 1)
            stage_d(X1, 512 * gi, X2, 512 * (gi - 1), 4, Tt, b_Tt, ct, b_ct)
            if DM == 6:
                P.barrier(b_Tt)
                return
        P.barrier(b_Tt)
        P.pending_stores = []

    if lvl >= 2 and not P.stopped:
        mixer_phase()
    if lvl >= 3 and not P.stopped:
        ffn_phase(1, X2, out_d, [(0, 0, 1024), (1024, 1024, 1024)], 2)
    P.barrier(b_wring)
    P.st.close()
    return nc


_CACHE = {}


def _host_layout(inputs, core):
    b, half = core // 2, core % 2
    f32 = np.float32
    x = np.asarray(inputs["x"], dtype=f32)
    xs = np.zeros((NEXT, D), f32)
    xs[HALO:] = x[b, half * NMAIN:(half + 1) * NMAIN]
    if half == 1:
        xs[:HALO] = x[b, NMAIN - HALO:NMAIN]
    return {
        "x": xs,
        "cT": np.ascontiguousarray(np.asarray(inputs["c"], f32)[b].reshape(16, 128).T),
        "hv": np.full((128, 1), float(half), f32),
    }


def _shared_layout(inputs):
    f32 = np.float32
    g = lambda k: np.asarray(inputs[k], dtype=f32)[0]
    col = lambda v: np.ascontiguousarray(v.reshape(16, 128).T)
    gpre = np.concatenate([col(g("ffn1_pre_g")), col(g("mix_pre_g")), col(g("ffn2_pre_g"))], axis=1)
    gpost = np.stack([g("ffn1_post_g"), g("mix_post_g"), g("ffn2_post_g")])
    lnv = np.stack([g("ln_v_g"), g("ln_v_b")])
    gout = col(np.concatenate([g("g_out_a"), g("g_out_b")]))
    bsT = np.ascontiguousarray(g("b_s").T)
    wsT = np.ascontiguousarray(g("w_s").transpose(2, 0, 1)).reshape(128, 1024)
    rel_ext = np.concatenate([g("rel_bias"), np.full((NH, 1), -30000.0, f32)], axis=1)
    kj = np.arange(640)[:, None]
    qi = np.arange(128)[None, :]
    idx = np.clip(qi - kj + 512, -128, 128) + 128
    masked = ((kj < 64) & (qi >= 64)) | ((kj >= 576) & (qi < 64))
    idx = np.where(masked, 257, idx)
    bt = rel_ext[:, idx]
    biasT = np.ascontiguousarray(bt.reshape(NH, 5, 128, 128).transpose(0, 2, 1, 3)).reshape(NH, 128, 640)
    return {
        "w_ada": g("w_ada"), "b_ada": g("b_ada")[None, :],
        "w_gu1": g("ffn1_w_gu"), "w_dn1": g("ffn1_w_down"), "w_gu2": g("ffn2_w_gu"), "w_dn2": g("ffn2_w_down"),
        "w_in": g("w_in"), "w_out": g("w_out"),
        "gpre": np.ascontiguousarray(gpre), "gpost": gpost, "lnv": lnv, "gout": gout, "bsT": bsT, "wsT": wsT,
        "biasT": biasT, "ident": np.eye(128, dtype=f32),
    }


def kernel(**inputs):
    if "nc" not in _CACHE:
        _CACHE["nc"] = build_program()
    nc = _CACHE["nc"]
    shared = _shared_layout(inputs)
    in_maps = []
    for core in range(8):
        m = dict(shared)
        m.update(_host_layout(inputs, core))
        in_maps.append(m)
    res = run_bass_kernel_spmd(nc, in_maps, core_ids=list(range(8)))
    out = np.zeros((4, 4096, D), np.float32)
    for core in range(8):
        b, half = core // 2, core % 2
        out[b, half * NMAIN:(half + 1) * NMAIN] = res.results[core]["out"]
    return out
```

```python
import numpy as np
import concourse.bass as bass
import concourse.mybir as mybir
from concourse.bass_utils import run_bass_kernel_spmd
from contextlib import ExitStack

F32 = mybir.dt.float32
BF16 = mybir.dt.bfloat16
AF = mybir.ActivationFunctionType
ALU = mybir.AluOpType
AX = mybir.AxisListType

D = 2048
DFF = 5504
NJ = 43
HALO = 512
NMAIN = 2048
NEXT = NMAIN + HALO
EPS = 1e-6
NH = 16
VW = 65


import os
SKIP_SELF = os.environ.get("SKIP_SELF", "act,pe").split(",")


class Eng:
    def __init__(self, e, sem, name):
        self.e, self.sem, self.name = e, sem, name
        self.n = 0
        self.seen = {}
        self.last = None

    def wait(self, toks):
        best = {}
        for t in toks:
            if t is None:
                continue
            s, v = t
            k = id(s)
            if s is self.sem and self.name in SKIP_SELF:
                continue
            if k not in best or best[k][1] < v:
                best[k] = (s, v)
        for k, (s, v) in best.items():
            if self.seen.get(k, 0) < v:
                self.e.wait_ge(s, v)
                self.seen[k] = v

    def mark(self, ins):
        ins.then_inc(self.sem, 1)
        self.n += 1
        self.last = (self.sem, self.n)
        return self.last


class Buf:
    def __init__(self, name=""):
        self.name = name
        self.w = {}
        self.r = {}
        self.pr = {}

    @staticmethod
    def _add(d, tok):
        if tok is None:
            return
        k = id(tok[0])
        if k not in d or d[k][1] < tok[1]:
            d[k] = tok

    def rdeps(self):
        return list(self.w.values())

    def wdeps(self):
        return list(self.w.values()) + list(self.r.values()) + list(self.pr.values())

    def pwdeps(self):
        if self.r:
            npr = {}
            for d in (self.w, self.r, self.pr):
                for t in d.values():
                    self._add(npr, t)
            self.pr, self.w, self.r = npr, {}, {}
        return list(self.pr.values())

    def add_r(self, tok):
        self._add(self.r, tok)

    def set_w(self, tok):
        self.w = {}
        self._add(self.w, tok)
        self.r = {}
        self.pr = {}
        self._add(self.pr, tok)

    def add_w(self, tok):
        self._add(self.w, tok)


class _Stop(Exception):
    pass


class Prog:
    def __init__(self):
        self.nc = bass.Bass("TRN2", target_bir_lowering=False)
        self.st = ExitStack()
        self.nsem = 0
        self.pending_stores = []

    def sem(self, name):
        self.nsem += 1
        return self.st.enter_context(self.nc.semaphore(f"{name}_{self.nsem}"))

    def setup_engines(self):
        nc = self.nc
        self.PE = Eng(nc.tensor, self.sem("pe"), "pe")
        self.ACT = Eng(nc.scalar, self.sem("act"), "act")
        self.DVE = Eng(nc.vector, self.sem("dve"), "dve")
        self.POOL = Eng(nc.gpsimd, self.sem("pool"), "pool")
        self.SP = Eng(nc.sync, self.sem("sp"), "sp")

    def _deps(self, reads, writes, pwrites):
        deps = []
        for b in reads:
            deps += b.rdeps()
        for b in writes:
            deps += b.wdeps()
        for b in pwrites:
            deps += b.pwdeps()
        return deps

    def _upd(self, tok, reads, writes, pwrites):
        for b in reads:
            b.add_r(tok)
        for b in writes:
            b.set_w(tok)
        for b in pwrites:
            b.add_w(tok)

    def op(self, E, f, reads=(), writes=(), pwrites=()):
        E.wait(self._deps(reads, writes, pwrites))
        tok = E.mark(f())
        self._upd(tok, reads, writes, pwrites)
        return tok

    def pe_group(self, fs, reads=(), writes=(), pwrites=()):
        self.PE.wait(self._deps(reads, writes, pwrites))
        ins = None
        for f in fs:
            ins = f()
        tok = self.PE.mark(ins)
        self._upd(tok, reads, writes, pwrites)
        return tok

    def dma(self, Q, dsem, out, in_, reads=(), writes=(), pwrites=()):
        Q.wait(self._deps(reads, writes, pwrites))
        Q.e.dma_start(out=out, in_=in_).then_inc(dsem[0], 16)
        dsem[1] += 1
        tok = (dsem[0], 16 * dsem[1])
        self._upd(tok, reads, writes, pwrites)
        return tok

    def dsem(self, name):
        return [self.sem(name), 0]

    def sp_sync(self, extra=()):
        toks = [E.last for E in (self.PE, self.ACT, self.DVE)] + list(extra) + self.pending_stores
        self.SP.wait(toks)

    def wait_bufs(self, engines, bufs):
        toks = []
        for b in bufs:
            toks += b.wdeps()
        toks += self.pending_stores
        for E in engines:
            E.wait(toks)

    def barrier(self, bufs=()):
        toks = [E.last for E in (self.PE, self.ACT, self.DVE)]
        for b in bufs:
            toks += b.wdeps()
        toks += self.pending_stores
        for E in (self.PE, self.ACT, self.DVE, self.SP):
            E.wait(toks)


def build_program(upto="all", dbg=False, stop=0, skip_pro=False):
    P = Prog()
    lvl = {"pro": 0, "ffn1": 1, "mix": 2, "all": 3}[upto]
    nc = P.nc
    st = P.st
    en = st.enter_context

    def inp(name, shape):
        return nc.dram_tensor(name, shape, F32, kind="ExternalInput").ap()

    x_d = inp("x", [NEXT, D])
    cT_d = inp("cT", [128, 16])
    hv_d = inp("hv", [128, 1])
    wada_d = inp("w_ada", [D, 9 * D]) if not skip_pro else None
    bada_d = inp("b_ada", [1, 9 * D])
    wgu_d = [inp("w_gu1", [D, 2 * DFF]) if (lvl >= 1 and not os.environ.get("SKIP_F1")) else None, inp("w_gu2", [D, 2 * DFF]) if lvl >= 3 else None]
    wdn_d = [inp("w_dn1", [DFF, D]) if (lvl >= 1 and not os.environ.get("SKIP_F1")) else None, inp("w_dn2", [DFF, D]) if lvl >= 3 else None]
    win_d = inp("w_in", [D, 5120]) if lvl >= 2 else None
    wout_d = inp("w_out", [D, D]) if lvl >= 2 else None
    gpre_d = inp("gpre", [128, 3 * 16])
    gpost_d = inp("gpost", [3, D])
    lnv_d = inp("lnv", [2, 1024])
    gout_d = inp("gout", [128, 16])
    bsT_d = inp("bsT", [128, 8])
    wsT_d = inp("wsT", [128, 8 * 128])
    biasT_d = inp("biasT", [NH, 128, 640])
    ident_d = inp("ident", [128, 128])
    out_d = nc.dram_tensor("out", [NMAIN, D], F32, kind="ExternalOutput").ap()
    kd = dict(kind="ExternalOutput") if dbg else {}
    X1 = nc.dram_tensor("X1", [NEXT, D], F32, **kd).ap()
    X2 = nc.dram_tensor("X2", [NMAIN, D], F32, **kd).ap()
    Yd = nc.dram_tensor("Yd", [1024, D], F32, **kd).ap()
    Cs = nc.dram_tensor("Cs", [3 * 128, D], F32, **kd).ap()
    if dbg:
        dbg_cols = nc.dram_tensor("dbg_cols", [128, 160], F32, kind="ExternalOutput").ap()
        dbg_hT = nc.dram_tensor("dbg_hT", [128, 16 * 1024], BF16, kind="ExternalOutput").ap()
        dbg_aT = nc.dram_tensor("dbg_aT", [128, NJ * 1024], BF16, kind="ExternalOutput").ap()

    P.setup_engines()
    PE, ACT, DVE, POOL, SP = P.PE, P.ACT, P.DVE, P.POOL, P.SP
    V, A, T = nc.vector, nc.scalar, nc.tensor

    ident = en(nc.sbuf_tensor("ident_sb", [128, 128], F32))
    cols = en(nc.sbuf_tensor("cols", [128, 16 * 10], F32))
    hv = en(nc.sbuf_tensor("hv_sb", [128, 1], F32))
    epst = en(nc.sbuf_tensor("epst", [128, 1], F32))
    bsT = en(nc.sbuf_tensor("bsT_sb", [128, 8], F32))
    wring = [en(nc.sbuf_tensor(f"wring{i}", [128, 8192], BF16)) for i in range(3)]
    stt = [en(nc.sbuf_tensor(f"stt{i}", [128, 8], F32)) for i in range(4)]
    junk = en(nc.sbuf_tensor("junk", [128, D], BF16))
    ystage = [en(nc.sbuf_tensor(f"ystage{i}", [128, 512], F32)) for i in range(2)]
    banks = [en(nc.psum_tensor(f"bank{i}", [128, 512], F32)) for i in range(8)]

    b_ident, b_cols, b_hv, b_eps, b_bsT = Buf(), Buf(), Buf(), Buf(), Buf()
    b_cs = [Buf(), Buf(), Buf()]
    b_Cs = [Buf(), Buf(), Buf()]
    b_wring = [Buf(f"wr{i}") for i in range(3)]
    b_stt = [Buf() for _ in range(4)]
    b_ystage = [Buf(), Buf()]
    b_bank = [Buf(f"bank{i}") for i in range(8)]
    bank_ctr = [0]

    def next_bank():
        i = bank_ctr[0] % 8
        bank_ctr[0] += 1
        return banks[i], b_bank[i]

    def Acol(s):
        return cols[:, (2 * s) * 16:(2 * s + 1) * 16]

    def Bcol(s):
        return cols[:, (2 * s + 1) * 16:(2 * s + 2) * 16]

    def gprecol(s):
        return cols[:, (6 + s) * 16:(7 + s) * 16]

    goutcol = cols[:, 9 * 16:10 * 16]

    d_ct = P.dsem("ct")
    d_w = [P.dsem(f"w{i}") for i in range(3)]
    d_ld = [P.dsem(f"ld{i}") for i in range(8)]
    d_st = [P.dsem(f"st{i}") for i in range(4)]
    d_cs = P.dsem("cs")
    d_ys = [P.dsem("ys0"), P.dsem("ys1")]

    plan = []

    def plan_ada(vs):
        for v in vs:
            for nb in range(4):
                c0 = v * D + nb * 512
                plan.append((f"ada{v}_{nb}",
                             lambda s, c0=c0: (s[:, :].rearrange("p (k c) -> p k c", k=16),
                                               wada_d[:, c0:c0 + 512].rearrange("(k p) c -> p k c", p=128))))

    def plan_ada_one(v, nb):
        c0 = v * D + nb * 512
        plan.append((f"ada{v}_{nb}",
                     lambda s, c0=c0: (s[:, :].rearrange("p (k c) -> p k c", k=16),
                                       wada_d[:, c0:c0 + 512].rearrange("(k p) c -> p k c", p=128))))

    def plan_ffn(f):
        wg, wd = wgu_d[f], wdn_d[f]
        ngroups = 3 if f == 0 else 2
        for g in range(ngroups):
            for blk in range(22):
                j0 = 2 * blk
                ncol = 256 if blk < 21 else 128
                plan.append((f"gu{f}_{g}_{blk}",
                             lambda s, j0=j0, ncol=ncol, wg=wg: (
                                 s[:, :].rearrange("p (two k c) -> p two k c", two=2, k=16)[:, :, :, 0:ncol],
                                 wg[:, :].rearrange("(k p) (two c) -> p two k c", p=128, two=2)[:, :, :, j0 * 128:j0 * 128 + ncol])))
                if f == 0 and plan_q:
                    v = plan_q.pop(0)
                    n_left = sum(1 for q in plan_q if q == v)
                    plan_ada_one(v, 3 - n_left)
            for cb in range(4):
                for jb in range(3):
                    j0 = 16 * jb
                    nj = min(16, NJ - j0)
                    plan.append((f"dn{f}_{g}_{cb}_{jb}",
                                 lambda s, j0=j0, nj=nj, cb=cb, wd=wd: (
                                     s[:, :].rearrange("p (j c) -> p j c", j=16)[:, 0:nj, :],
                                     wd[j0 * 128:(j0 + nj) * 128, cb * 512:(cb + 1) * 512].rearrange("(j p) c -> p j c", p=128))))

    def plan_mixer():
        for gi in range(5):
            blks = [2, 3, 4, 5] if gi == 0 else list(range(10))
            for blk in blks:
                plan.append((f"in{gi}_{blk}",
                             lambda s, blk=blk: (s[:, :].rearrange("p (k c) -> p k c", k=16),
                                                 win_d[:, blk * 512:(blk + 1) * 512].rearrange("(k p) c -> p k c", p=128))))
            if gi > 0:
                for cb in range(4):
                    plan.append((f"wo{gi}_{cb}",
                                 lambda s, cb=cb: (s[:, :].rearrange("p (k c) -> p k c", k=16),
                                                   wout_d[:, cb * 512:(cb + 1) * 512].rearrange("(k p) c -> p k c", p=128))))

    plan_q = []
    if not skip_pro:
        plan_ada(range(3) if lvl >= 1 else range(9))
        if lvl >= 1:
            plan_q = [v for v in range(3, 9) for _ in range(4)]
    if lvl >= 1 and stop != 1 and not os.environ.get("SKIP_F1"):
        plan_ffn(0)
    if lvl >= 2:
        plan_mixer()
    if lvl >= 3:
        plan_ffn(1)
    ws_state = {"issued": 0, "got": 0}

    def ws_issue_upto(n):
        while ws_state["issued"] < min(n, len(plan)):
            i = ws_state["issued"]
            s = i % 3
            o, src = plan[i][1](wring[s])
            P.dma(POOL, d_w[s], o, src, writes=[b_wring[s]])
            ws_state["issued"] += 1

    def ws_get(tag):
        i = ws_state["got"]
        assert plan[i][0] == tag, (plan[i][0], tag)
        ws_issue_upto(i + 1)
        ws_state["got"] += 1
        return wring[i % 3], b_wring[i % 3]

    def ws_done():
        ws_issue_upto(ws_state["got"] + 2)

    P.dma(SP, P.dsem("m"), ident[:], ident_d[:, :], writes=[b_ident])
    P.dma(SP, P.dsem("m"), hv[:], hv_d[:, :], writes=[b_hv])
    P.dma(SP, P.dsem("m"), bsT[:], bsT_d[:, :], writes=[b_bsT])
    P.dma(SP, P.dsem("m"), cols[:, 96:144], gpre_d[:, :], pwrites=[b_cols])
    P.dma(SP, P.dsem("m"), cols[:, 144:160], gout_d[:, :], pwrites=[b_cols])
    P.op(DVE, lambda: V.memset(epst[:], EPS), writes=[b_eps])
    ws_issue_upto(2)

    def rstd_from_sumsq(si, n):
        P.op(ACT, lambda: A.activation(out=stt[si][:, 1:2], in_=stt[si][:, 0:1], func=AF.Sqrt,
                                       scale=1.0 / n, bias=epst[:, 0:1]), reads=[b_eps], writes=[b_stt[si]])
        P.op(DVE, lambda: V.reciprocal(out=stt[si][:, 2:3], in_=stt[si][:, 1:2]), writes=[b_stt[si]])

    def rstd_of_tile_act(si, src, b_src, n):
        P.op(ACT, lambda: A.activation(out=junk[:, 0:n], in_=src, func=AF.Square, accum_out=stt[si][:, 0:1]),
             reads=[b_src], writes=[b_stt[si]])
        P.op(DVE, lambda: V.tensor_scalar(out=stt[si][:, 3:4], in0=stt[si][:, 0:1], scalar1=1.0 / n, scalar2=EPS,
                                          op0=ALU.mult, op1=ALU.add), writes=[b_stt[si]])
        P.op(ACT, lambda: A.activation(out=stt[si][:, 1:2], in_=stt[si][:, 3:4], func=AF.Sqrt), writes=[b_stt[si]])
        P.op(DVE, lambda: V.reciprocal(out=stt[si][:, 2:3], in_=stt[si][:, 1:2]), writes=[b_stt[si]])

    def transpose_tile(src, b_src, dst_fn, b_dst, scol, bcol, b_sc):
        bk = [next_bank() for _ in range(4)]
        for q in range(4):
            P.pe_group([lambda kc=kc, q=q: T.transpose(out=bk[q][0][:, (kc % 4) * 128:(kc % 4 + 1) * 128],
                                                       in_=src[:, kc * 128:(kc + 1) * 128], identity=ident[:])
                        for kc in range(4 * q, 4 * q + 4)], reads=[b_src, b_ident], writes=[bk[q][1]])
        DT = int(os.environ.get("DBG_T", "2"))
        for kc in range(16):
            q = kc // 4
            pin = bk[q][0][:, (kc % 4) * 128:(kc % 4 + 1) * 128]
            if DT == 0 or (DT == 1 and kc % 2 == 1) or (DT == 3 and kc % 2 == 0):
                continue
            if True:
                if bcol is None:
                    P.op(DVE, lambda: V.tensor_scalar(out=dst_fn(kc), in0=pin, scalar1=scol[:, kc:kc + 1], scalar2=None,
                                                      op0=ALU.mult), reads=[bk[q][1], b_sc], pwrites=[b_dst])
                else:
                    P.op(DVE, lambda: V.tensor_scalar(out=dst_fn(kc), in0=pin, scalar1=scol[:, kc:kc + 1],
                                                      scalar2=bcol[:, kc:kc + 1], op0=ALU.mult, op1=ALU.add),
                         reads=[bk[q][1], b_sc], pwrites=[b_dst])
            else:
                if bcol is None:
                    P.op(ACT, lambda: A.activation(out=dst_fn(kc), in_=pin, func=AF.Copy, scale=scol[:, kc:kc + 1]),
                         reads=[bk[q][1], b_sc], pwrites=[b_dst])
                else:
                    P.op(ACT, lambda: A.activation(out=dst_fn(kc), in_=pin, func=AF.Identity, scale=scol[:, kc:kc + 1],
                                                   bias=bcol[:, kc:kc + 1]), reads=[bk[q][1], b_sc], pwrites=[b_dst])

    def stage_a(xs, r0, ntiles, Tt, b_Tt, hT, b_hT, s):
        for i in range(ntiles):
            sl = i % 2
            xt, b_xt = Tt[sl], b_Tt[sl]
            xn, b_xn = Tt[2 + sl], b_Tt[2 + sl]
            import os
            DA = int(os.environ.get("DBG_A", "9"))
            if i >= int(os.environ.get("DBG_NT", "99")):
                continue
            P.dma(SP, d_ld[sl], xt, xs[r0 + i * 128:r0 + (i + 1) * 128, :], writes=[b_xt])
            rstd_of_tile_act(sl, xt, b_xt, D)
            if DA < 3:
                continue
            P.op(ACT, lambda: A.activation(out=xn, in_=xt, func=AF.Copy, scale=stt[sl][:, 2:3]),
                 reads=[b_xt, b_stt[sl]], writes=[b_xn])
            if DA < 4:
                continue
            transpose_tile(xn, b_xn, lambda kc, i=i: hT[:, kc, i * 128:(i + 1) * 128], b_hT[i], Acol(s), Bcol(s), b_cs[s])

    def stage_d(xs, rs0, xd, rd0, ntiles, Tt, b_Tt, ct, b_ct, nsl=2):
        def loads(i):
            sl = i % nsl
            P.dma(SP, d_ld[sl], Tt[sl], Yd[i * 128:(i + 1) * 128, :], writes=[b_Tt[sl]])
            P.dma(SP, d_ld[nsl + sl], Tt[nsl + sl], xs[rs0 + i * 128:rs0 + (i + 1) * 128, :], writes=[b_Tt[nsl + sl]])

        for i in range(min(nsl - 1, ntiles)):
            loads(i)
        for i in range(ntiles):
            if i + nsl - 1 < ntiles:
                loads(i + nsl - 1)
            sl = i % nsl
            yt, b_yt = Tt[sl], b_Tt[sl]
            xt, b_xt = Tt[nsl + sl], b_Tt[nsl + sl]
            rstd_of_tile_act(sl, yt, b_yt, D)
            P.op(DVE, lambda: V.scalar_tensor_tensor(out=yt, in0=yt, scalar=stt[sl][:, 2:3], in1=ct, op0=ALU.mult,
                                                     op1=ALU.mult), reads=[b_stt[sl], b_ct], writes=[b_yt])
            P.op(DVE, lambda: V.tensor_tensor(out=xt, in0=yt, in1=xt, op=ALU.add), reads=[b_yt], writes=[b_xt])
            tok = P.dma(SP, d_st[sl], xd[rd0 + i * 128:rd0 + (i + 1) * 128, :], xt, reads=[b_xt])
            P.pending_stores.append(tok)

    evac_ctr = [0]

    def store_y_block(bank, b_bk, i, cb):
        k = evac_ctr[0] % 2
        evac_ctr[0] += 1
        if k == 0:
            P.op(ACT, lambda: A.copy(out=ystage[k][:], in_=bank[:, :]), reads=[b_bk], writes=[b_ystage[k]])
        else:
            P.op(DVE, lambda: V.tensor_copy(out=ystage[k][:], in_=bank[:, :]), reads=[b_bk], writes=[b_ystage[k]])
        tok = P.dma(SP, d_ys[k], Yd[i * 128:(i + 1) * 128, cb * 512:(cb + 1) * 512], ystage[k][:], reads=[b_ystage[k]])
        P.pending_stores.append(tok)

    def load_ct(ct, b_ct, s):
        P.dma(SP, d_ct, ct, Cs[s * 128:(s + 1) * 128, :], reads=[b_Cs[s]], writes=[b_ct])

    ada_scope = ExitStack()
    e3 = ada_scope.enter_context
    cT = e3(nc.sbuf_tensor("cT_sb", [128, 16], F32))
    cact = e3(nc.sbuf_tensor("cact", [128, 16], F32))
    crep = e3(nc.sbuf_tensor("crep", [128, 16, 128], BF16))
    badab = [e3(nc.sbuf_tensor(f"badab{i}", [128, 512], F32)) for i in range(2)]
    gpostb = [e3(nc.sbuf_tensor(f"gpostb{i}", [128, 512], F32)) for i in range(2)]
    cresb = [e3(nc.sbuf_tensor(f"cresb{i}", [128, 512], F32)) for i in range(2)]
    mblk = e3(nc.sbuf_tensor("mblk", [128, 512], F32))
    tmpb = e3(nc.sbuf_tensor("tmpb", [128, 512], F32))
    dcol = e3(nc.sbuf_tensor("dcol", [128, 4], F32))
    b_cT, b_cact, b_crep, b_mblk, b_tmpb, b_dcol = [Buf() for _ in range(6)]
    b_badab, b_gpostb, b_cresb = [Buf(), Buf()], [Buf(), Buf()], [Buf(), Buf()]
    d_ab, d_gp, d_cs2 = [P.dsem("ab0"), P.dsem("ab1")], [P.dsem("gp0"), P.dsem("gp1")], [P.dsem("cs0"), P.dsem("cs1")]
    ada_ctr = [0]

    def ada_block(v, nb):
        s_, role = v // 3, v % 3
        k = ada_ctr[0] % 2
        ada_ctr[0] += 1
        c0 = v * D + nb * 512
        P.dma(SP, d_ab[k], badab[k][:], bada_d[0:1, c0:c0 + 512].partition_broadcast(128), writes=[b_badab[k]])
        if role == 2:
            P.dma(SP, d_gp[k], gpostb[k][:], gpost_d[s_:s_ + 1, nb * 512:(nb + 1) * 512].partition_broadcast(128),
                  writes=[b_gpostb[k]])
        slot, b_slot = ws_get(f"ada{v}_{nb}")
        sv = slot[:, :].rearrange("p (k c) -> p k c", k=16)
        bk, b_bk = next_bank()
        P.pe_group([lambda kc=kc: T.matmul(out=bk[:, :], lhsT=crep[:, kc, :], rhs=sv[:, kc, :],
                                           start=(kc == 0), stop=(kc == 15)) for kc in range(16)],
                   reads=[b_crep, b_slot], writes=[b_bk])
        ws_done()
        P.op(DVE, lambda: V.tensor_tensor(out=mblk[:], in0=bk[:, :], in1=badab[k][:], op=ALU.add),
             reads=[b_bk, b_badab[k]], writes=[b_mblk])
        if role in (0, 1):
            P.op(DVE, lambda: V.tensor_tensor(out=tmpb[:].rearrange("p (a b) -> p a b", a=4),
                                              in0=mblk[:].rearrange("p (a b) -> p a b", a=4),
                                              in1=ident[:].unsqueeze(1).to_broadcast([128, 4, 128]), op=ALU.mult),
                 reads=[b_mblk, b_ident], writes=[b_tmpb])
            if role == 0:
                P.op(DVE, lambda: V.tensor_reduce(out=Bcol(s_)[:, nb * 4:(nb + 1) * 4],
                                                  in_=tmpb[:].rearrange("p (a b) -> p a b", a=4), axis=AX.X, op=ALU.add),
                     reads=[b_tmpb], pwrites=[b_cs[s_]])
            else:
                P.op(DVE, lambda: V.tensor_reduce(out=dcol[:], in_=tmpb[:].rearrange("p (a b) -> p a b", a=4),
                                                  axis=AX.X, op=ALU.add), reads=[b_tmpb], writes=[b_dcol])
                P.op(DVE, lambda: V.scalar_tensor_tensor(out=Acol(s_)[:, nb * 4:(nb + 1) * 4], in0=dcol[:], scalar=1.0,
                                                         in1=gprecol(s_)[:, nb * 4:(nb + 1) * 4], op0=ALU.add, op1=ALU.mult),
                     reads=[b_dcol, b_cols], pwrites=[b_cs[s_]])
        else:
            gsc = 1.0 if s_ == 1 else 0.5
            P.op(DVE, lambda: V.scalar_tensor_tensor(out=cresb[k][:], in0=mblk[:], scalar=gsc, in1=gpostb[k][:],
                                                     op0=ALU.mult, op1=ALU.mult),
                 reads=[b_mblk, b_gpostb[k]], writes=[b_cresb[k]])
            tok = P.dma(SP, d_cs2[k], Cs[s_ * 128:(s_ + 1) * 128, nb * 512:(nb + 1) * 512], cresb[k][:],
                        reads=[b_cresb[k]], pwrites=[b_Cs[s_]])
            P.pending_stores.append(tok)

    ada_queue = []
    if skip_pro:
        for s_ in range(3):
            P.op(DVE, lambda: V.memset(cols[:, 32 * s_:32 * s_ + 32], 1.0), writes=[b_cs[s_]])
    else:
        P.dma(SP, P.dsem("m"), cT[:], cT_d[:, :], writes=[b_cT])
        P.op(ACT, lambda: A.activation(out=cact[:], in_=cT[:], func=AF.Silu), reads=[b_cT], writes=[b_cact])
        P.op(DVE, lambda: V.tensor_copy(out=crep[:], in_=cact[:].unsqueeze(2).to_broadcast([128, 16, 128])),
             reads=[b_cact], writes=[b_crep])
        nearly = 3 if lvl >= 1 else 9
        for v in range(nearly):
            for nb in range(4):
                ada_block(v, nb)
        ada_queue = [(v, nb) for v in range(nearly, 9) for nb in range(4)]
    if dbg and lvl == 0:
        P.pending_stores.append(P.dma(SP, P.dsem("dbg"), dbg_cols[:, :], cols[:], reads=[b_cols] + b_cs))
        P.barrier()

    def ffn_phase(f, xs, xd, groups, s):
        with ExitStack() as sc:
            e2 = sc.enter_context
            hTt = e2(nc.sbuf_tensor(f"hT{f}", [128, 16, 1024], BF16))
            RA = e2(nc.sbuf_tensor(f"RA{f}", [128, 22016], F32))
            sg = [e2(nc.sbuf_tensor(f"sg{f}_{i}", [128, 512], F32)) for i in range(2)]
            b_sg = [Buf(), Buf()]
            aT = RA[:, :].bitcast(BF16).rearrange("p (j t) -> p j t", j=NJ)
            Tt = [RA[:, i * D:(i + 1) * D] for i in range(8)]
            ct = RA[:, 8 * D:9 * D]
            b_Tt = [Buf() for _ in range(8)]
            b_ct = Buf()
            b_hT = [Buf() for _ in range(8)]
            b_aT = [Buf() for _ in range(NJ)]
            sgc = 0
            for g, (rs0, rd0, ntok) in enumerate(groups):
                nt = ntok // 128
                ntb = ntok // 512
                stage_a(xs, rs0, nt, Tt, b_Tt, hTt, b_hT, s)
                P.wait_bufs((DVE,), b_Tt + [b_ct])
                if stop == 1:
                    P.pending_stores.append(P.dma(SP, P.dsem("dbg"), dbg_hT[:, :], hTt[:].rearrange("p k t -> p (k t)"), reads=b_hT))
                    P.barrier()
                    P.stopped = True
                    return
                for blk in range(22):
                    slot, b_slot = ws_get(f"gu{f}_{g}_{blk}")
                    sv = slot[:, :].rearrange("p (two k c) -> p two k c", two=2, k=16)
                    for jj in range(2):
                        j = 2 * blk + jj
                        if j >= NJ:
                            continue
                        for tb in range(ntb):
                            bg, b_bg = next_bank()
                            bu, b_bu = next_bank()
                            hb = [b_hT[4 * tb + q] for q in range(4)]
                            P.pe_group([lambda kc=kc: T.matmul(out=bg[:, :], lhsT=sv[:, 0, kc, jj * 128:(jj + 1) * 128],
                                                               rhs=hTt[:, kc, tb * 512:(tb + 1) * 512],
                                                               start=(kc == 0), stop=(kc == 15)) for kc in range(16)],
                                       reads=[b_slot] + hb, writes=[b_bg])
                            P.pe_group([lambda kc=kc: T.matmul(out=bu[:, :], lhsT=sv[:, 1, kc, jj * 128:(jj + 1) * 128],
                                                               rhs=hTt[:, kc, tb * 512:(tb + 1) * 512],
                                                               start=(kc == 0), stop=(kc == 15)) for kc in range(16)],
                                       reads=[b_slot] + hb, writes=[b_bu])
                            k = sgc % 2
                            sgc += 1
                            P.op(ACT, lambda: A.activation(out=sg[k][:], in_=bg[:, :], func=AF.Silu),
                                 reads=[b_bg], writes=[b_sg[k]])
                            P.op(DVE, lambda: V.tensor_tensor(out=aT[:, j, tb * 512:(tb + 1) * 512], in0=sg[k][:],
                                                              in1=bu[:, :], op=ALU.mult),
                                 reads=[b_sg[k], b_bu], pwrites=[b_aT[j]])
                    ws_done()
                    if f == 0 and ada_queue:
                        ada_block(*ada_queue.pop(0))
                if stop == 2:
                    P.barrier()
                    P.pending_stores.append(P.dma(SP, P.dsem("dbg"), dbg_aT[:, :], RA[:, :].bitcast(BF16), reads=b_aT))
                    P.barrier()
                    P.stopped = True
                    return
                for cb in range(4):
                    for jb in range(3):
                        slot, b_slot = ws_get(f"dn{f}_{g}_{cb}_{jb}")
                        sv = slot[:, :].rearrange("p (j c) -> p j c", j=16)
                        j0 = 16 * jb
                        nj = min(16, NJ - j0)
                        for i in range(nt):
                            P.pe_group([lambda j=j: T.matmul(out=banks[i][:, :], lhsT=aT[:, j, i * 128:(i + 1) * 128],
                                                             rhs=sv[:, j - j0, :], start=(j == 0), stop=(j == NJ - 1))
                                        for j in range(j0, j0 + nj)],
                                       reads=[b_slot] + [b_aT[j] for j in range(j0, j0 + nj)], writes=[b_bank[i]])
                            if jb == 2:
                                store_y_block(banks[i], b_bank[i], i, cb)
                        ws_done()
                P.sp_sync()
                if stop == 3:
                    P.stopped = True
                    return
                load_ct(ct, b_ct, s)
                stage_d(xs, rs0, xd, rd0, nt, Tt, b_Tt, ct, b_ct, nsl=4)
                if stop == 4:
                    P.barrier(b_Tt)
                    P.stopped = True
                    return
            P.barrier(b_Tt)
            P.pending_stores = []

    P.stopped = False
    SKIP_F1 = bool(os.environ.get("SKIP_F1"))
    if lvl >= 1 and not SKIP_F1:
        ffn_phase(0, x_d, X1, [(0, 0, 1024), (1024, 1024, 1024), (2048, 2048, 512)], 0)
    assert not ada_queue or SKIP_F1 or P.stopped or lvl < 1, ada_queue
    P.barrier()
    ada_scope.close()

    def mixer_phase():
      with ExitStack() as sc:
        e2 = sc.enter_context
        hTm = e2(nc.sbuf_tensor("hTm", [128, 16, 512], BF16))
        qT = e2(nc.sbuf_tensor("qT", [128, 8, 512], BF16))
        kT = [e2(nc.sbuf_tensor(f"kT{i}", [128, 8, 512], BF16)) for i in range(2)]
        vaug = [e2(nc.sbuf_tensor(f"vaug{i}", [128, 4, NH, VW], BF16)) for i in range(2)]
        RM = e2(nc.sbuf_tensor("RM", [128, 5 * D], F32))
        out_a = e2(nc.sbuf_tensor("out_a", [128, 1024], F32))
        out_b = e2(nc.sbuf_tensor("out_b", [128, 1024], F32))
        vln = e2(nc.sbuf_tensor("vln", [128, 1024], BF16))
        sc_sb = [e2(nc.sbuf_tensor(f"sc_sb{i}", [128, 640], F32)) for i in range(4)]
        pT = [e2(nc.sbuf_tensor(f"pT{i}", [128, 640], BF16)) for i in range(4)]
        biasb = [e2(nc.sbuf_tensor(f"biasb{i}", [128, 640], F32)) for i in range(4)]
        lnb = e2(nc.sbuf_tensor("lnb", [128, 2048], F32))
        wsTf = e2(nc.sbuf_tensor("wsTf", [128, 1024], F32))
        wsT = e2(nc.sbuf_tensor("wsT_bf", [128, 8, 128], BF16))
        b_hTm = [Buf() for _ in range(4)]
        b_qT, b_kT, b_vaug = Buf(), [Buf(), Buf()], [Buf(), Buf()]
        b_out_a, b_out_b, b_vln = Buf(), Buf(), Buf()
        b_sc, b_pT, b_biasb = [Buf() for _ in range(4)], [Buf() for _ in range(4)], [Buf() for _ in range(4)]
        b_lnb, b_wsTf, b_wsT = Buf(), Buf(), Buf()
        Tt = [RM[:, i * D:(i + 1) * D] for i in range(4)]
        ct = RM[:, 4 * D:5 * D]
        b_Tt = [Buf() for _ in range(4)]
        b_ct = Buf()
        u_g = RM[:, 0:2 * D].rearrange("p (t c) -> p t c", t=4)
        vg = RM[:, 2 * D:4 * D].rearrange("p (t c) -> p t c", t=4)
        merged = RM[:, 4 * D:5 * D]
        b_ug, b_vg = [Buf() for _ in range(4)], [Buf() for _ in range(4)]
        b_merged = Buf()
        d_bias = [P.dsem(f"bias{i}") for i in range(4)]

        P.dma(SP, P.dsem("m"), lnb[:, 0:1024], lnv_d[0:1, :].partition_broadcast(128), pwrites=[b_lnb])
        P.dma(SP, P.dsem("m"), lnb[:, 1024:2048], lnv_d[1:2, :].partition_broadcast(128), pwrites=[b_lnb])
        P.dma(SP, P.dsem("m"), wsTf[:], wsT_d[:, :], writes=[b_wsTf])
        P.op(DVE, lambda: V.tensor_copy(out=wsT[:].rearrange("p g t -> p (g t)"), in_=wsTf[:]), reads=[b_wsTf], writes=[b_wsT])
        P.op(DVE, lambda: V.memset(wsT[64:128, :, 0:64], 0.0), writes=[b_wsT])

        ev = [0]

        def copy_evac(out, in_, reads, writes):
            ev[0] += 1
            if ev[0] % 2 == 0:
                P.op(ACT, lambda: A.copy(out=out, in_=in_), reads=reads, pwrites=writes)
            else:
                P.op(DVE, lambda: V.tensor_copy(out=out, in_=in_), reads=reads, pwrites=writes)

        for gi in range(5):
            ks = gi % 2
            stage_a(X1, 512 * gi, 4, Tt, b_Tt, hTm, b_hTm, 1)
            P.wait_bufs((ACT, DVE), b_Tt + [b_ct])
            P.op(DVE, lambda: V.memset(vaug[ks][:, :, :, 64:65], 1.0), writes=[b_vaug[ks]])
            if gi == 0:
                P.op(DVE, lambda: V.tensor_scalar(out=vaug[ks][:, :, :, 64:65], in0=vaug[ks][:, :, :, 64:65],
                                                  scalar1=hv[:, 0:1], scalar2=None, op0=ALU.mult),
                     reads=[b_hv], writes=[b_vaug[ks]])
            blks = [2, 3, 4, 5] if gi == 0 else list(range(10))
            for blk in blks:
                slot, b_slot = ws_get(f"in{gi}_{blk}")
                sv = slot[:, :].rearrange("p (k c) -> p k c", k=16)
                if blk < 4:
                    dst, b_dst = (qT, b_qT) if blk < 2 else (kT[ks], b_kT[ks])
                    for c in range(4):
                        bk, b_bk = next_bank()
                        P.pe_group([lambda kc=kc: T.matmul(out=bk[:, :], lhsT=sv[:, kc, c * 128:(c + 1) * 128],
                                                           rhs=hTm[:, kc, :], start=(kc == 0), stop=(kc == 15))
                                    for kc in range(16)], reads=[b_slot] + b_hTm, writes=[b_bk])
                        copy_evac(dst[:, (blk % 2) * 4 + c, :], bk[:, :], [b_bk], [b_dst])
                else:
                    for i in range(4):
                        bk, b_bk = next_bank()
                        P.pe_group([lambda kc=kc: T.matmul(out=bk[:, :], lhsT=hTm[:, kc, i * 128:(i + 1) * 128],
                                                           rhs=sv[:, kc, :], start=(kc == 0), stop=(kc == 15))
                                    for kc in range(16)], reads=[b_slot, b_hTm[i]], writes=[b_bk])
                        if blk < 6:
                            hb = blk - 4
                            o = vaug[ks][:, i, hb * 8:(hb + 1) * 8, 0:64]
                            src = bk[:, :].rearrange("p (h d) -> p h d", h=8)
                            if gi == 0:
                                P.op(DVE, lambda: V.tensor_scalar(out=o, in0=src, scalar1=hv[:, 0:1], scalar2=None,
                                                                  op0=ALU.mult), reads=[b_bk, b_hv], pwrites=[b_vaug[ks]])
                            else:
                                copy_evac(o, src, [b_bk], [b_vaug[ks]])
                        elif blk < 8:
                            P.op(ACT, lambda: A.activation(out=u_g[:, i, (blk - 6) * 512:(blk - 5) * 512], in_=bk[:, :],
                                                           func=AF.Gelu), reads=[b_bk], pwrites=[b_ug[i]])
                        else:
                            P.op(ACT, lambda: A.activation(out=vg[:, i, (blk - 8) * 512:(blk - 7) * 512], in_=bk[:, :],
                                                           func=AF.Gelu), reads=[b_bk], pwrites=[b_vg[i]])
                ws_done()
            DM = int(os.environ.get("DBG_M", "99"))
            if gi == 0:
                if DM == 1:
                    P.barrier()
                    return
                continue
            if DM == 2:
                P.barrier()
                return
            for t in range(4):
                ubanks = {}

                def scores(h):
                    k = h % 4
                    hp, hc = h % 2, h // 2
                    P.dma(SP, d_bias[k], biasb[k][:], biasT_d[h, :, :], writes=[b_biasb[k]])
                    bA, b_bA = next_bank()
                    bB, b_bB = next_bank()
                    ubanks[h] = (bB, b_bB)
                    for c in range(5):
                        kt = 4 * gi + t - 4 + c
                        ksl = (kt // 4) % 2
                        kc0 = (kt % 4) * 128
                        o = bA[:, c * 128:(c + 1) * 128] if c < 4 else bB[:, 0:128]
                        if c == 0:
                            wr = dict(writes=[b_bA])
                        elif c == 4:
                            wr = dict(writes=[b_bB])
                        else:
                            wr = dict(pwrites=[b_bA])
                        P.pe_group([lambda: T.matmul(out=o, lhsT=kT[ksl][hp * 64:(hp + 1) * 64, hc, kc0:kc0 + 128],
                                                     rhs=qT[hp * 64:(hp + 1) * 64, hc, t * 128:(t + 1) * 128],
                                                     start=True, stop=True)],
                                   reads=[b_kT[ksl], b_qT], **wr)
                    P.op(DVE, lambda: V.scalar_tensor_tensor(out=sc_sb[k][:, 0:512], in0=bA[:, :], scalar=0.125,
                                                             in1=biasb[k][:, 0:512], op0=ALU.mult, op1=ALU.add),
                         reads=[b_bA, b_biasb[k]], writes=[b_sc[k]])
                    P.op(DVE, lambda: V.scalar_tensor_tensor(out=sc_sb[k][:, 512:640], in0=bB[:, 0:128], scalar=0.125,
                                                             in1=biasb[k][:, 512:640], op0=ALU.mult, op1=ALU.add),
                         reads=[b_bB, b_biasb[k]], pwrites=[b_sc[k]])
                    P.op(ACT, lambda: A.activation(out=pT[k][:], in_=sc_sb[k][:], func=AF.Exp),
                         reads=[b_sc[k]], writes=[b_pT[k]])

                def pv(h):
                    k = h % 4
                    bO, b_bO = ubanks.pop(h)
                    fs = []
                    rd = [b_pT[k]]
                    for c in range(5):
                        kt = 4 * gi + t - 4 + c
                        ksl = (kt // 4) % 2
                        rd.append(b_vaug[ksl])
                        fs.append(lambda c=c, kt=kt, ksl=ksl: T.matmul(out=bO[:, 128:128 + VW], lhsT=pT[k][:, c * 128:(c + 1) * 128],
                                                                      rhs=vaug[ksl][:, kt % 4, h, :],
                                                                      start=(c == 0), stop=(c == 4)))
                    P.pe_group(fs, reads=rd, pwrites=[b_bO])
                    P.op(DVE, lambda: V.reciprocal(out=stt[2][:, 0:1], in_=bO[:, 192:193]), reads=[b_bO], writes=[b_stt[2]])
                    P.op(DVE, lambda: V.tensor_scalar(out=out_a[:, h * 64:(h + 1) * 64], in0=bO[:, 128:192],
                                                      scalar1=stt[2][:, 0:1], scalar2=None, op0=ALU.mult),
                         reads=[b_bO, b_stt[2]], pwrites=[b_out_a])

                for h in range(NH + 2):
                    if h < NH:
                        scores(h)
                    if h >= 2:
                        pv(h - 2)
                if DM == 3:
                    P.barrier()
                    return
                vt = vg[:, t, :]
                P.op(DVE, lambda: V.tensor_reduce(out=stt[3][:, 0:1], in_=vt, axis=AX.X, op=ALU.add),
                     reads=[b_vg[t]], writes=[b_stt[3]])
                P.op(DVE, lambda: V.tensor_scalar(out=stt[3][:, 1:2], in0=stt[3][:, 0:1], scalar1=-1.0 / 1024, scalar2=None,
                                                  op0=ALU.mult), writes=[b_stt[3]])
                P.op(DVE, lambda: V.tensor_scalar(out=vt, in0=vt, scalar1=stt[3][:, 1:2], scalar2=None, op0=ALU.add),
                     reads=[b_stt[3]], writes=[b_vg[t]])
                P.op(DVE, lambda: V.scalar_tensor_tensor(out=junk[:, 0:1024], in0=vt, scalar=1.0, in1=vt, op0=ALU.mult,
                                                         op1=ALU.mult, accum_out=stt[0][:, 0:1]),
                     reads=[b_vg[t]], writes=[b_stt[0]])
                rstd_from_sumsq(0, 1024)
                P.op(DVE, lambda: V.scalar_tensor_tensor(out=vt, in0=vt, scalar=stt[0][:, 2:3], in1=lnb[:, 0:1024],
                                                         op0=ALU.mult, op1=ALU.mult),
                     reads=[b_stt[0], b_lnb], writes=[b_vg[t]])
                P.op(DVE, lambda: V.tensor_tensor(out=vln[:], in0=vt, in1=lnb[:, 1024:2048], op=ALU.add),
                     reads=[b_vg[t], b_lnb], writes=[b_vln])
                bz = [next_bank(), next_bank()]
                for g in range(8):
                    zb, b_zb = bz[g // 4]
                    P.pe_group([lambda: T.matmul(out=zb[:, (g % 4) * 128:(g % 4 + 1) * 128], lhsT=wsT[:, g, :],
                                                 rhs=vln[:, g * 128:(g + 1) * 128], start=True, stop=True)],
                               reads=[b_wsT, b_vln], pwrites=[b_zb])
                for g in range(8):
                    zb, b_zb = bz[g // 4]
                    P.op(DVE, lambda: V.scalar_tensor_tensor(out=out_b[:, g * 128:(g + 1) * 128],
                                                             in0=zb[:, (g % 4) * 128:(g % 4 + 1) * 128],
                                                             scalar=bsT[:, g:g + 1], in1=u_g[:, t, g * 128:(g + 1) * 128],
                                                             op0=ALU.add, op1=ALU.mult),
                         reads=[b_zb, b_bsT, b_ug[t]], pwrites=[b_out_b])
                if DM == 4:
                    P.barrier()
                    return
                P.op(DVE, lambda: V.scalar_tensor_tensor(out=junk[:, 0:1024], in0=out_a[:], scalar=1.0, in1=out_a[:],
                                                         op0=ALU.mult, op1=ALU.mult, accum_out=stt[0][:, 0:1]),
                     reads=[b_out_a], writes=[b_stt[0]])
                rstd_from_sumsq(0, 1024)
                P.op(DVE, lambda: V.scalar_tensor_tensor(out=junk[:, 1024:2048], in0=out_b[:], scalar=1.0, in1=out_b[:],
                                                         op0=ALU.mult, op1=ALU.mult, accum_out=stt[1][:, 0:1]),
                     reads=[b_out_b], writes=[b_stt[1]])
                rstd_from_sumsq(1, 1024)
                P.op(ACT, lambda: A.activation(out=merged[:, 0:1024], in_=out_a[:], func=AF.Copy, scale=stt[0][:, 2:3]),
                     reads=[b_out_a, b_stt[0]], pwrites=[b_merged])
                P.op(ACT, lambda: A.activation(out=merged[:, 1024:2048], in_=out_b[:], func=AF.Copy, scale=stt[1][:, 2:3]),
                     reads=[b_out_b, b_stt[1]], pwrites=[b_merged])
                transpose_tile(merged, b_merged, lambda kc, t=t: hTm[:, kc, t * 128:(t + 1) * 128], b_hTm[t], goutcol, None, b_cols)
                if DM == 5:
                    P.barrier()
                    return
            for cb in range(4):
                slot, b_slot = ws_get(f"wo{gi}_{cb}")
                sv = slot[:, :].rearrange("p (k c) -> p k c", k=16)
                for i in range(4):
                    bk, b_bk = next_bank()
                    P.pe_group([lambda kc=kc: T.matmul(out=bk[:, :], lhsT=hTm[:, kc, i * 128:(i + 1) * 128],
                                                       rhs=sv[:, kc, :], start=(kc == 0), stop=(kc == 15))
                                for kc in range(16)], reads=[b_slot, b_hTm[i]], writes=[b_bk])
                    store_y_block(bk, b_bk, i, cb)
                ws_done()
            P.sp_sync()
            load_ct(ct, b_ct, 1)
            stage_d(X1, 512 * gi, X2, 512 * (gi - 1), 4, Tt, b_Tt, ct, b_ct)
            if DM == 6:
                P.barrier(b_Tt)
                return
        P.barrier(b_Tt)
        P.pending_stores = []

    if lvl >= 2 and not P.stopped:
        mixer_phase()
    if lvl >= 3 and not P.stopped:
        ffn_phase(1, X2, out_d, [(0, 0, 1024), (1024, 1024, 1024)], 2)
    P.barrier(b_wring)
    P.st.close()
    return nc


_CACHE = {}


def _host_layout(inputs, core):
    b, half = core // 2, core % 2
    f32 = np.float32
    x = np.asarray(inputs["x"], dtype=f32)
    xs = np.zeros((NEXT, D), f32)
    xs[HALO:] = x[b, half * NMAIN:(half + 1) * NMAIN]
    if half == 1:
        xs[:HALO] = x[b, NMAIN - HALO:NMAIN]
    return {
        "x": xs,
        "cT": np.ascontiguousarray(np.asarray(inputs["c"], f32)[b].reshape(16, 128).T),
        "hv": np.full((128, 1), float(half), f32),
    }


def _shared_layout(inputs):
    f32 = np.float32
    g = lambda k: np.asarray(inputs[k], dtype=f32)[0]
    col = lambda v: np.ascontiguousarray(v.reshape(16, 128).T)
    gpre = np.concatenate([col(g("ffn1_pre_g")), col(g("mix_pre_g")), col(g("ffn2_pre_g"))], axis=1)
    gpost = np.stack([g("ffn1_post_g"), g("mix_post_g"), g("ffn2_post_g")])
    lnv = np.stack([g("ln_v_g"), g("ln_v_b")])
    gout = col(np.concatenate([g("g_out_a"), g("g_out_b")]))
    bsT = np.ascontiguousarray(g("b_s").T)
    wsT = np.ascontiguousarray(g("w_s").transpose(2, 0, 1)).reshape(128, 1024)
    rel_ext = np.concatenate([g("rel_bias"), np.full((NH, 1), -30000.0, f32)], axis=1)
    kj = np.arange(640)[:, None]
    qi = np.arange(128)[None, :]
    idx = np.clip(qi - kj + 512, -128, 128) + 128
    masked = ((kj < 64) & (qi >= 64)) | ((kj >= 576) & (qi < 64))
    idx = np.where(masked, 257, idx)
    bt = rel_ext[:, idx]
    biasT = np.ascontiguousarray(bt.reshape(NH, 5, 128, 128).transpose(0, 2, 1, 3)).reshape(NH, 128, 640)
    return {
        "w_ada": g("w_ada"), "b_ada": g("b_ada")[None, :],
        "w_gu1": g("ffn1_w_gu"), "w_dn1": g("ffn1_w_down"), "w_gu2": g("ffn2_w_gu"), "w_dn2": g("ffn2_w_down"),
        "w_in": g("w_in"), "w_out": g("w_out"),
        "gpre": np.ascontiguousarray(gpre), "gpost": gpost, "lnv": lnv, "gout": gout, "bsT": bsT, "wsT": wsT,
        "biasT": biasT, "ident": np.eye(128, dtype=f32),
    }


def kernel(**inputs):
    if "nc" not in _CACHE:
        _CACHE["nc"] = build_program()
    nc = _CACHE["nc"]
    shared = _shared_layout(inputs)
    in_maps = []
    for core in range(8):
        m = dict(shared)
        m.update(_host_layout(inputs, core))
        in_maps.append(m)
    res = run_bass_kernel_spmd(nc, in_maps, core_ids=list(range(8)))
    out = np.zeros((4, 4096, D), np.float32)
    for core in range(8):
        b, half = core // 2, core % 2
        out[b, half * NMAIN:(half + 1) * NMAIN] = res.results[core]["out"]
    return out
```
